# Optimizing a Trainium2 kernel written in Bass

```python
import jax
import jax.numpy as jnp
from jax import lax

D_MODEL = 1024
BATCH = 16
SEQ = 256
DEPTH = 1
DEC_BATCH = 4
DEC_SEQ = 2048
PAST_LEN = 256

GRID_W = 64
N_RET_HEADS = 4
RET_DK = 256
RET_DV = 512
D_RET_QK = N_RET_HEADS * RET_DK
D_RET_V = N_RET_HEADS * RET_DV
RET_CHUNK = 128
N_LRU_BLOCKS = 10
LRU_BLOCK = 128
D_LRU = N_LRU_BLOCKS * LRU_BLOCK
LRU_C = 8.0
CONV_W = 4
CONV_LEFT = 2
EPS = 1e-6
D_IN = 2 * D_RET_QK + 2 * D_RET_V + 2 * D_LRU + 2 * D_MODEL

kernel_name = 'hybrid_retention_rglru_diffusion_step'

F32 = jnp.float32


def rmsnorm(x, w):
    x32 = x.astype(F32)
    return x32 * lax.rsqrt(jnp.mean(x32 * x32, axis=-1, keepdims=True) + EPS) * w.astype(F32)


def short_conv(x, w, b):
    L = x.shape[-2]
    pad = [(0, 0)] * (x.ndim - 2) + [(CONV_LEFT, CONV_W - 1 - CONV_LEFT), (0, 0)]
    xp = jnp.pad(x, pad)
    out = b.astype(F32)
    for t in range(CONV_W):
        out = out + xp[..., t:t + L, :] * w[t].astype(F32)
    return out


def retention_dir(q, k, v, log_g, s0, strict):
    B, L, H, DK = q.shape
    DV = v.shape[-1]
    C = min(RET_CHUNK, L)
    nc = L // C
    qc = q.reshape(B, nc, C, H, DK)
    kc = k.reshape(B, nc, C, H, DK)
    vc = v.reshape(B, nc, C, H, DV)
    pos = jnp.arange(C, dtype=F32)
    diff = pos[:, None] - pos[None, :]
    mask = (diff > 0) if strict else (diff >= 0)
    decay = jnp.where(mask[None], jnp.exp(log_g[:, None, None] * jnp.where(mask, diff, 0.0)[None]), 0.0)
    scores = jnp.einsum('bnihd,bnjhd->bnhij', qc, kc) * decay
    intra = jnp.einsum('bnhij,bnjhe->bnihe', scores, vc)
    k_dec = jnp.exp((C - 1 - pos)[:, None] * log_g[None])
    kv = jnp.einsum('bnjhd,jh,bnjhe->nbhde', kc, k_dec, vc)
    c_dec = jnp.exp(C * log_g)[:, None, None]

    def step(S, kv_n):
        return c_dec * S + kv_n, S

    s_fin, s_prev = lax.scan(step, s0.astype(F32), kv)
    q_dec = jnp.exp((pos + 1)[:, None] * log_g[None])
    inter = jnp.einsum('bnihd,ih,nbhde->bnihe', qc, q_dec, s_prev)
    return (intra + inter).reshape(B, L, H, DV), s_fin


def rglru(x, wa, ba, wx, bx, a_param, h0, reverse):
    B, L, _ = x.shape
    xb = x.reshape(B, L, N_LRU_BLOCKS, LRU_BLOCK)
    r = jax.nn.sigmoid(jnp.einsum('blhi,hij->blhj', xb, wa.astype(F32)).reshape(B, L, D_LRU) + ba.astype(F32))
    gi = jax.nn.sigmoid(jnp.einsum('blhi,hij->blhj', xb, wx.astype(F32)).reshape(B, L, D_LRU) + bx.astype(F32))
    log_a = -LRU_C * r * jax.nn.softplus(-a_param.astype(F32))
    a = jnp.exp(log_a)
    b = jnp.sqrt(-jnp.expm1(2.0 * log_a)) * (gi * x)

    def combine(e1, e2):
        a1, b1 = e1
        a2, b2 = e2
        return a1 * a2, a2 * b1 + b2

    a_cum, h = lax.associative_scan(combine, (a, b), axis=1, reverse=reverse)
    h = h + a_cum * h0.astype(F32)[:, None]
    fin = h[:, 0] if reverse else h[:, -1]
    return h, fin


def branch_mixer(h, s0_ret, h0_lru, on_grid, w_in, decay_logit, gn_w, w_ret_down, conv_w, conv_b,
                 wa, ba, wx, bx, a_param, w_lru_down, w_out):
    B, L, _ = h.shape
    z = h @ w_in.astype(F32)
    sizes = [D_RET_QK, D_RET_QK, D_RET_V, D_RET_V, D_LRU, D_LRU, D_MODEL, D_MODEL]
    idx = []
    acc = 0
    for s in sizes[:-1]:
        acc += s
        idx.append(acc)
    q, k, v, g_ret, x_lru, g_lru, m_ret, m_lru = jnp.split(z, idx, axis=-1)
    q = q.reshape(B, L, N_RET_HEADS, RET_DK)
    k = k.reshape(B, L, N_RET_HEADS, RET_DK) * (RET_DK ** -0.5)
    v = v.reshape(B, L, N_RET_HEADS, RET_DV)
    log_g = jax.nn.log_sigmoid(decay_logit.astype(F32))
    o_f, s_f = retention_dir(q, k, v, log_g[0], s0_ret[:, 0], False)
    o_b, s_b = retention_dir(q[:, ::-1], k[:, ::-1], v[:, ::-1], log_g[1], s0_ret[:, 1], True)
    o = o_f + o_b[:, ::-1]
    o = o * lax.rsqrt(jnp.mean(o * o, axis=-1, keepdims=True) + EPS)
    o = o.reshape(B, L, D_RET_V) * gn_w.astype(F32) * jax.nn.silu(g_ret)
    ret_out = o @ w_ret_down.astype(F32)
    if on_grid:
        rows = L // GRID_W
        xc = short_conv(x_lru.reshape(B, rows, GRID_W, D_LRU), conv_w, conv_b).reshape(B, L, D_LRU)
    else:
        xc = short_conv(x_lru, conv_w, conv_b)
    hf, fin_f = rglru(xc, wa[0], ba[0], wx[0], bx[0], a_param[0], h0_lru[:, 0], False)
    hb, fin_b = rglru(xc, wa[1], ba[1], wx[1], bx[1], a_param[1], h0_lru[:, 1], True)
    lru_out = ((hf + hb) * jax.nn.silu(g_lru)) @ w_lru_down.astype(F32)
    merged = jax.nn.sigmoid(m_ret) * ret_out + jax.nn.sigmoid(m_lru) * lru_out
    out = merged @ w_out.astype(F32)
    return out, jnp.stack([s_f, s_b], axis=1), jnp.stack([fin_f, fin_b], axis=1)


def trunk_layer(x, cond, s0_ret, h0_lru, on_grid, norm_w, w_ada, b_ada, w_in, decay_logit, gn_w,
                w_ret_down, conv_w, conv_b, wa, ba, wx, bx, a_param, w_lru_down, w_out):
    mod = jax.nn.silu(cond.astype(F32)) @ w_ada.astype(F32) + b_ada.astype(F32)
    shift, scale, gate = jnp.split(mod, 3, axis=-1)
    h = rmsnorm(x, norm_w) * (1.0 + scale[:, None]) + shift[:, None]
    out, s_ret, s_lru = branch_mixer(h, s0_ret, h0_lru, on_grid, w_in, decay_logit, gn_w, w_ret_down,
                                     conv_w, conv_b, wa, ba, wx, bx, a_param, w_lru_down, w_out)
    return x + gate[:, None] * out, s_ret, s_lru


def setup_inputs(seed: int = 0) -> dict:
    key = jax.random.key(seed)
    ks = jax.random.split(key, 24)
    n = jax.random.normal
    heads = jnp.arange(N_RET_HEADS, dtype=F32)
    base_logit = jnp.log(jnp.exp2(5.0 + heads) - 1.0)
    u = jax.random.uniform(ks[19], (DEPTH, 2, D_LRU), F32, 0.9, 0.999)
    s = u ** (1.0 / LRU_C)
    return {
        'x_prompt': n(ks[0], (BATCH, SEQ, D_MODEL), F32),
        'x_sample': n(ks[1], (DEC_BATCH, DEC_SEQ, D_MODEL), F32),
        'state_ret': 0.5 * n(ks[2], (DEC_BATCH, DEPTH, 2, N_RET_HEADS, RET_DK, RET_DV), F32),
        'state_lru': 0.5 * n(ks[3], (DEC_BATCH, DEPTH, 2, D_LRU), F32),
        'c': n(ks[4], (DEC_BATCH, D_MODEL), F32),
        'c_ctx': n(ks[5], (D_MODEL,), F32),
        'norm_w': 1.0 + 0.05 * n(ks[6], (DEPTH, D_MODEL), F32),
        'w_ada': n(ks[7], (DEPTH, D_MODEL, 3 * D_MODEL), F32) * (0.5 * D_MODEL ** -0.5),
        'b_ada': 0.02 * n(ks[8], (DEPTH, 3 * D_MODEL), F32),
        'w_in': n(ks[9], (DEPTH, D_MODEL, D_IN), F32) * D_MODEL ** -0.5,
        'ret_decay_logit': base_logit + 0.1 * n(ks[10], (DEPTH, 2, N_RET_HEADS), F32),
        'ret_gn_w': 1.0 + 0.05 * n(ks[11], (DEPTH, D_RET_V), F32),
        'w_ret_down': n(ks[12], (DEPTH, D_RET_V, D_MODEL), F32) * D_RET_V ** -0.5,
        'conv_w': n(ks[13], (DEPTH, CONV_W, D_LRU), F32) * CONV_W ** -0.5,
        'conv_b': 0.02 * n(ks[14], (DEPTH, D_LRU), F32),
        'lru_wa': n(ks[15], (DEPTH, 2, N_LRU_BLOCKS, LRU_BLOCK, LRU_BLOCK), F32) * LRU_BLOCK ** -0.5,
        'lru_ba': 0.02 * n(ks[16], (DEPTH, 2, D_LRU), F32),
        'lru_wx': n(ks[17], (DEPTH, 2, N_LRU_BLOCKS, LRU_BLOCK, LRU_BLOCK), F32) * LRU_BLOCK ** -0.5,
        'lru_bx': 0.02 * n(ks[18], (DEPTH, 2, D_LRU), F32),
        'lru_a_param': jnp.log(s) - jnp.log1p(-s),
        'w_lru_down': n(ks[20], (DEPTH, D_LRU, D_MODEL), F32) * D_LRU ** -0.5,
        'w_out': n(ks[21], (DEPTH, D_MODEL, D_MODEL), F32) * D_MODEL ** -0.5,
        'final_norm_w': 1.0 + 0.05 * n(ks[22], (D_MODEL,), F32),
    }


def reference(x_prompt, x_sample, state_ret, state_lru, c, c_ctx, norm_w, w_ada, b_ada, w_in,
              ret_decay_logit, ret_gn_w, w_ret_down, conv_w, conv_b, lru_wa, lru_ba, lru_wx, lru_bx,
              lru_a_param, w_lru_down, w_out, final_norm_w):
    x = x_prompt.astype(F32)
    y = x_sample.astype(F32)
    b_ctx = x.shape[0]
    cond_ctx = jnp.broadcast_to(c_ctx.astype(F32), (b_ctx, D_MODEL))
    zero_ret = jnp.zeros((b_ctx, 2, N_RET_HEADS, RET_DK, RET_DV), F32)
    zero_lru = jnp.zeros((b_ctx, 2, D_LRU), F32)
    new_ret = []
    new_lru = []
    for l in range(DEPTH):
        params = (norm_w[l], w_ada[l], b_ada[l], w_in[l], ret_decay_logit[l], ret_gn_w[l], w_ret_down[l],
                  conv_w[l], conv_b[l], lru_wa[l], lru_ba[l], lru_wx[l], lru_bx[l], lru_a_param[l],
                  w_lru_down[l], w_out[l])
        x, s_ret, s_lru = trunk_layer(x, cond_ctx, zero_ret, zero_lru, False, *params)
        new_ret.append(s_ret)
        new_lru.append(s_lru)
        y, _, _ = trunk_layer(y, c, state_ret[:, l], state_lru[:, l], True, *params)
    y_prompt = rmsnorm(x, final_norm_w).astype(x_prompt.dtype)
    y_sample = rmsnorm(y, final_norm_w).astype(x_sample.dtype)
    new_state_ret = jnp.stack(new_ret, axis=1).astype(state_ret.dtype)
    new_state_lru = jnp.stack(new_lru, axis=1).astype(state_lru.dtype)
    return (y_prompt, y_sample, new_state_ret, new_state_lru)
```

```python
import numpy as np
import concourse.bass as bass
import concourse.mybir as mybir
from concourse.bass_utils import run_bass_kernel_spmd

F32 = mybir.dt.float32
BF16 = mybir.dt.bfloat16
AF = mybir.ActivationFunctionType
ALU = mybir.AluOpType

D = 1024
DIN = 10752
NT = 2048
NCH = 16
EPS = 1e-6
OQ, OK_, OV, OG, OXL, OGL, OMR, OML = 0, 1024, 2048, 4096, 6144, 7424, 8704, 9728
NCOLS = 178
C_NW, C_COND, C_BSH, C_BSC, C_GN, C_CW, C_CB, C_BA, C_BX, C_AP, C_H0 = 0, 8, 16, 24, 32, 48, 88, 98, 118, 138, 158
NSM = 8 + 34

DEBUG = False


class Buf:
    def __init__(self, ap, arena, lo, hi, esize):
        self.ap = ap
        self.arena = arena
        self.lo = lo
        self.hi = hi
        self.esize = esize

    def reg(self, elo=None, ehi=None):
        if elo is None:
            return (self.arena, self.lo, self.hi)
        return (self.arena, self.lo + elo * self.esize, self.lo + ehi * self.esize)


class Op:
    __slots__ = ("eng", "emit", "deps", "is_dma", "signal", "sigcount", "dsem", "dval", "dprev", "gid")


class Sched:
    ENGS = ("pe", "act", "dve", "pool", "sp")

    def __init__(self, nc):
        self.nc = nc
        self.ops = []
        self.eng_ops = {e: [] for e in self.ENGS}
        self.recs = {}
        self.enabled = True

    def _access(self, opid, eng, is_dma, regs, is_write, deps):
        for (a, lo, hi) in regs:
            if a.startswith("ps"):
                r = self.recs.get(a)
                if r is not None:
                    rop, rw, reng = r
                    if rop != opid:
                        if is_write or rw or reng != eng:
                            deps.add(rop)
                    else:
                        is_write = is_write or rw
                self.recs[a] = (opid, is_write, eng)
                continue
            recs = self.recs.get(a)
            if recs is None:
                recs = []
                self.recs[a] = recs
            new = []
            for r in recs:
                rlo, rhi, rop, rw, reng, rdma = r
                if rhi <= lo or rlo >= hi:
                    new.append(r)
                    continue
                if (is_write or rw) and rop != opid:
                    deps.add(rop)
                covered = lo <= rlo and rhi <= hi
                if covered and is_write:
                    continue
                if covered and (not is_write) and (not rw) and reng == eng and not rdma and not is_dma:
                    continue
                new.append(r)
            new.append((lo, hi, opid, is_write, eng, is_dma))
            self.recs[a] = new

    def add(self, eng, emit, reads=(), writes=(), dma=False):
        if not self.enabled:
            return -1
        op = Op()
        op.eng = eng
        op.emit = emit
        op.is_dma = dma
        op.signal = False
        opid = len(self.ops)
        deps = set()
        self._access(opid, eng, dma, writes, True, deps)
        self._access(opid, eng, dma, reads, False, deps)
        op.deps = deps
        self.ops.append(op)
        self.eng_ops[eng].append(opid)
        return opid

    def pe(self, emit, reads, writes):
        return self.add("pe", emit, reads, writes)

    def act(self, emit, reads, writes):
        return self.add("act", emit, reads, writes)

    def dve(self, emit, reads, writes):
        return self.add("dve", emit, reads, writes)

    def pool(self, emit, reads, writes):
        return self.add("pool", emit, reads, writes)

    def dma(self, q, out, in_, reads, writes, **kw):
        return self.add(q, lambda e: e.dma_start(out=out, in_=in_, **kw), reads, writes, dma=True)

    def mm(self, mms, reads, writes):
        n = len(mms)

        def emit(e):
            ins = None
            for i, (o, l, r) in enumerate(mms):
                ins = e.matmul(o, l, r, start=(i == 0), stop=(i == n - 1))
            return ins
        return self.add("pe", emit, reads, writes)

    def emit_all(self, sems):
        nc = self.nc
        ops = self.ops
        for op in ops:
            for d in op.deps:
                if not ops[d].is_dma:
                    ops[d].signal = True
        for e in self.ENGS:
            comp = [i for i in self.eng_ops[e] if not ops[i].is_dma]
            if comp:
                ops[comp[-1]].signal = True
        esem = {e: sems.pop() for e in ("pe", "act", "dve", "pool")}
        final = {}
        for e in ("pe", "act", "dve", "pool"):
            c = 0
            for i in self.eng_ops[e]:
                op = ops[i]
                if op.is_dma:
                    continue
                if op.signal:
                    c += 1
                op.sigcount = c if op.signal else None
            final[esem[e]] = c
        KD = 20
        dpool = {q: [sems.pop() for _ in range(KD)] for q in ("sp", "pool")}
        for q in ("sp", "pool"):
            cnt = [0] * KD
            k = 0
            for i in self.eng_ops[q]:
                op = ops[i]
                if not op.is_dma:
                    continue
                s = k % KD
                op.dsem = dpool[q][s]
                op.dprev = cnt[s] * 16
                cnt[s] += 1
                op.dval = cnt[s] * 16
                k += 1
            for s in range(KD):
                final[dpool[q][s]] = cnt[s] * 16
        engobj = {"pe": "tensor", "act": "scalar", "dve": "vector", "pool": "gpsimd", "sp": "sync"}
        self.nwaits = 0

        self.trace = {}

        def run_engine(ename, e):
            waited = {}
            tr = []
            self.trace[ename] = tr

            def wait(sem, val):
                if val <= 0:
                    return
                if waited.get(sem, 0) >= val:
                    return
                waited[sem] = val
                e.wait_ge(sem, val)
                tr.append(("w", id(sem), val))
                self.nwaits += 1
            for i in self.eng_ops[ename]:
                op = ops[i]
                need = {}
                for d in op.deps:
                    dop = ops[d]
                    if dop.is_dma:
                        sm, vl = dop.dsem, dop.dval
                    else:
                        if dop.eng == "pe" and ename == "pe":
                            continue
                        sm, vl = esem[dop.eng], dop.sigcount
                    if need.get(id(sm), (None, 0))[1] < vl:
                        need[id(sm)] = (sm, vl)
                for sm, vl in need.values():
                    wait(sm, vl)
                if op.is_dma:
                    wait(op.dsem, op.dprev)
                    ins = op.emit(e)
                    ins.then_inc(op.dsem, 16)
                    tr.append(("i", id(op.dsem), 16, i))
                else:
                    ins = op.emit(e)
                    if op.signal:
                        ins.then_inc(esem[ename], 1)
                        tr.append(("i", id(esem[ename]), 1, i))
                    else:
                        tr.append(("n", i))
            if ename == "sp":
                for sem, val in final.items():
                    wait(sem, val)

        with nc.Block() as block:
            @block.tensor
            def _(e):
                run_engine("pe", e)

            @block.scalar
            def _(e):
                run_engine("act", e)

            @block.vector
            def _(e):
                run_engine("dve", e)

            @block.gpsimd
            def _(e):
                run_engine("pool", e)

            @block.sync
            def _(e):
                run_engine("sp", e)


def bcast_ap(ap, dims):
    return bass.AP(ap.tensor, ap.offset, [list(ap.ap[0])] + [list(d) for d in dims])


class _Stop(Exception):
    pass


def build_program(debug=False, limit=None):
    nc = bass.Bass("TRN2", target_bir_lowering=False)
    dt = nc.dram_tensor
    x_d = dt("x", [NT, D], F32, kind="ExternalInput").ap()
    cols_d = dt("cols", [128, NCOLS], F32, kind="ExternalInput").ap()
    rows_d = dt("rows", [2, D], F32, kind="ExternalInput").ap()
    sm_d = dt("smalls", [1, NSM], F32, kind="ExternalInput").ap()
    consts_d = dt("consts", [128, 257], F32, kind="ExternalInput").ap()
    iret_d = dt("init_ret", [2, 4, 256, 512], F32, kind="ExternalInput").ap()
    wada_d = dt("w_ada", [D, 3 * D], F32, kind="ExternalInput").ap()
    win_d = dt("w_in", [D, DIN], F32, kind="ExternalInput").ap()
    wrd_d = dt("w_ret_down", [2048, D], F32, kind="ExternalInput").ap()
    wld_d = dt("w_lru_down", [1280, D], F32, kind="ExternalInput").ap()
    wout_d = dt("w_out", [D, D], F32, kind="ExternalInput").ap()
    wa_d = dt("lru_wa", [2, 10, 128, 128], F32, kind="ExternalInput").ap()
    wx_d = dt("lru_wx", [2, 10, 128, 128], F32, kind="ExternalInput").ap()
    y_d = dt("y", [NT, D], F32, kind="ExternalOutput").ap()
    sret_d = dt("st_ret", [4, 2, 4, 256, 512], F32, kind="ExternalOutput").ap()
    slru_d = dt("st_lru", [128, 80], F32, kind="ExternalOutput").ap()
    tb_d = dt("tb_scr", [4 * 16 * 128, 1024], BF16, kind="Internal").ap()
    gate_d = dt("gate_scr", [1, D], F32, kind="Internal").ap()
    dbg_d = None
    if debug:
        dbg_d = dt("dbg", [128, 8 * 2048], F32, kind="ExternalOutput").ap()

    S = Sched(nc)
    NW = 52800
    dumps = {}

    def dump(name, buf, n, dtype):
        if not debug:
            return
        dten = dt("dbg_" + name, [128, n], dtype, kind="ExternalOutput").ap()
        dumps[name] = (n, dtype)
        src = bass.AP(buf.ap.tensor, buf.ap.offset, [list(buf.ap.ap[0]), [1, n]])
        S.dma("sp", dten, src, [buf.reg()], [])
    from contextlib import ExitStack
    es = ExitStack()
    arena = es.enter_context(nc.sbuf_tensor("arena", [128, NW], F32))
    psb = [es.enter_context(nc.psum_tensor("ps%d" % i, [128, 512], F32)) for i in range(8)]
    sems = [es.enter_context(nc.semaphore("s%d" % i)) for i in range(48)]

    class Alloc:
        def __init__(self, lo, hi):
            self.p = lo
            self.hi = hi

        def f32(self, n, shape=None):
            w0 = self.p
            self.p += n
            assert self.p <= self.hi, ("sbuf overflow", self.p, self.hi)
            ap = arena[:, w0:w0 + n]
            if shape:
                ap = ap.rearrange(shape[0], **shape[1])
            return Buf(ap, "sb", w0 * 4, (w0 + n) * 4, 4)

        def bf(self, n, shape=None):
            nw = (n + 1) // 2
            w0 = self.p
            self.p += nw
            assert self.p <= self.hi, ("sbuf overflow", self.p, self.hi)
            ap = arena[:, w0:w0 + nw].bitcast(BF16)
            if shape:
                ap = ap.rearrange(shape[0], **shape[1])
            return Buf(ap, "sb", w0 * 4, (w0 + nw) * 4, 2)

    def psum(i, bf=False):
        ap = psb[i][:]
        if bf:
            return Buf(ap.bitcast(BF16), "ps%d" % i, 0, 2048, 2)
        return Buf(ap, "ps%d" % i, 0, 2048, 4)

    PS = [psum(i) for i in range(8)]
    PSB = [psum(i, True) for i in range(8)]

    A = Alloc(0, NW)
    hT = A.bf(8 * NT, ("p (k t) -> p k t", dict(k=8)))
    NSLOT = 5
    ring = [A.bf(2048) for _ in range(NSLOT)]
    ident = A.f32(128)
    identb = A.bf(128)
    iota_r = A.f32(128)
    iota_c = A.f32(1)
    cols = A.f32(NCOLS)
    smalls = A.f32(NSM)
    lg = A.f32(8)
    nlg = A.f32(8)
    lg128 = A.f32(8)
    lg127 = A.f32(8)
    cdec = A.f32(8)
    ck = A.f32(8 * 16)
    kd = A.f32(8)
    Dm = A.f32(4 * 128, ("p (h i) -> p h i", dict(h=4)))
    rowd = A.bf(8 * 128, ("p (g i) -> p g i", dict(g=8)))
    Acol = A.f32(8)
    Bcol = A.f32(8)
    scl = A.f32(20)
    scl2 = A.f32(20)
    fw = A.f32(40)
    lruout = A.f32(80)
    one_c = A.f32(1)
    eps_c = A.f32(1)
    tiny_c = A.f32(1)
    scb = A.bf(8 * 128, ("p (k m) -> p k m", dict(k=8)))
    tiny = A.f32(64)
    PERS_END = A.p

    flagsF = lambda n: smalls.ap[:, 8 + n:9 + n]
    flagsB = lambda n: smalls.ap[:, 24 + n:25 + n]
    convflag = smalls.ap[:, 40:41]
    lrukeep = smalls.ap[:, 41:42]

    wslot = [0]

    def next_slot():
        s = ring[wslot[0] % NSLOT]
        wslot[0] += 1
        return s

    def load_w(src_ap, shape_str, **kw):
        slot = next_slot()
        n = 1
        for v in src_ap.shape[1:]:
            n *= v
        view = slot.ap[:, 0:n].rearrange(shape_str, **kw)
        S.dma("pool", view, src_ap, [], [slot.reg(0, n)])
        return slot, view

    win_v = win_d.rearrange("(k p) c -> p k c", p=128)

    S.dma("sp", cols.ap, cols_d, [], [cols.reg()])
    S.dma("sp", smalls.ap, bass.AP(sm_d.tensor, 0, [[0, 128], [1, NSM]]), [], [smalls.reg()])
    S.dma("sp", ident.ap, consts_d[:, 0:128], [], [ident.reg()])
    S.dma("sp", iota_r.ap, consts_d[:, 128:256], [], [iota_r.reg()])
    S.dma("sp", iota_c.ap, consts_d[:, 256:257], [], [iota_c.reg()], allow_slow_non_contiguous=True)
    S.dve(lambda e: e.memset(one_c.ap, 1.0), [], [one_c.reg()])
    S.dve(lambda e: e.memset(eps_c.ap, EPS), [], [eps_c.reg()])
    S.dve(lambda e: e.memset(tiny_c.ap, 1e-18), [], [tiny_c.reg()])
    S.dve(lambda e: e.tensor_copy(out=identb.ap, in_=ident.ap), [ident.reg()], [identb.reg()])
    t0 = Buf(tiny.ap[:, 0:8], "sb", tiny.lo, tiny.lo + 32, 4)
    S.act(lambda e: e.activation(out=t0.ap, in_=smalls.ap[:, 0:8], func=AF.Exp, scale=-1.0), [smalls.reg()], [t0.reg()])
    S.act(lambda e: e.activation(out=t0.ap, in_=t0.ap, func=AF.Ln, bias=one_c.ap), [t0.reg(), one_c.reg()], [t0.reg()])
    S.dve(lambda e: e.tensor_scalar(out=lg.ap, in0=t0.ap, scalar1=-1.0, scalar2=None, op0=ALU.mult), [t0.reg()], [lg.reg()])
    S.dve(lambda e: e.tensor_copy(out=nlg.ap, in_=t0.ap), [t0.reg()], [nlg.reg()])
    S.dve(lambda e: e.tensor_scalar(out=lg128.ap, in0=lg.ap, scalar1=128.0, scalar2=None, op0=ALU.mult), [lg.reg()], [lg128.reg()])
    S.dve(lambda e: e.tensor_scalar(out=lg127.ap, in0=lg.ap, scalar1=127.0, scalar2=None, op0=ALU.mult), [lg.reg()], [lg127.reg()])
    S.act(lambda e: e.activation(out=cdec.ap, in_=lg128.ap, func=AF.Exp), [lg128.reg()], [cdec.reg()])
    for d in range(2):
        for h in range(4):
            g = d * 4 + h
            S.dve(lambda e, g=g, d=d: e.tensor_scalar(out=ck.ap[:, g * 16:(g + 1) * 16], in0=smalls.ap[:, 8 + 16 * d:24 + 16 * d],
                                                        scalar1=cdec.ap[:, g:g + 1], scalar2=None, op0=ALU.mult),
                  [smalls.reg(), cdec.reg()], [ck.reg(g * 16, (g + 1) * 16)])
    for h in range(4):
        S.act(lambda e, h=h: e.activation(out=kd.ap[:, h:h + 1], in_=iota_c.ap, func=AF.Exp, scale=nlg.ap[:, h:h + 1], bias=lg127.ap[:, h:h + 1]),
              [iota_c.reg(), nlg.reg(), lg127.reg()], [kd.reg(h, h + 1)])
        S.act(lambda e, h=h: e.activation(out=kd.ap[:, 4 + h:5 + h], in_=iota_c.ap, func=AF.Exp, scale=lg.ap[:, 4 + h:5 + h]),
              [iota_c.reg(), lg.reg()], [kd.reg(4 + h, 5 + h)])
    S.dve(lambda e: e.tensor_scalar(out=kd.ap, in0=kd.ap, scalar1=1.0 / 16.0, scalar2=None, op0=ALU.mult), [kd.reg()], [kd.reg()])
    P0 = Alloc(PERS_END, NW)
    rtmp = P0.f32(128)
    for h in range(4):
        S.act(lambda e, h=h: e.activation(out=rtmp.ap, in_=iota_r.ap, func=AF.Exp, scale=lg.ap[:, h:h + 1], bias=lg.ap[:, h:h + 1]),
              [iota_r.reg(), lg.reg()], [rtmp.reg()])
        S.dve(lambda e, h=h: e.tensor_copy(out=rowd.ap[:, h, :], in_=rtmp.ap), [rtmp.reg()], [rowd.reg(h * 128, (h + 1) * 128)])
        S.act(lambda e, h=h: e.activation(out=rtmp.ap, in_=iota_r.ap, func=AF.Exp, scale=nlg.ap[:, 4 + h:5 + h], bias=lg128.ap[:, 4 + h:5 + h]),
              [iota_r.reg(), nlg.reg(), lg128.reg()], [rtmp.reg()])
        S.dve(lambda e, h=h: e.tensor_copy(out=rowd.ap[:, 4 + h, :], in_=rtmp.ap), [rtmp.reg()], [rowd.reg((4 + h) * 128, (5 + h) * 128)])
    delta = P0.f32(128)
    dpos = P0.f32(128)
    dneg = P0.f32(128)
    mF = P0.f32(128)
    mB = P0.f32(128)
    e1 = P0.f32(128)
    e2 = P0.f32(128)
    S.dve(lambda e: e.tensor_scalar(out=delta.ap, in0=iota_r.ap, scalar1=iota_c.ap, scalar2=None, op0=ALU.subtract), [iota_r.reg(), iota_c.reg()], [delta.reg()])
    S.dve(lambda e: e.tensor_scalar(out=dpos.ap, in0=delta.ap, scalar1=0.0, scalar2=None, op0=ALU.max), [delta.reg()], [dpos.reg()])
    S.dve(lambda e: e.tensor_scalar(out=dneg.ap, in0=delta.ap, scalar1=-1.0, scalar2=0.0, op0=ALU.mult, op1=ALU.max), [delta.reg()], [dneg.reg()])
    S.dve(lambda e: e.tensor_scalar(out=mF.ap, in0=delta.ap, scalar1=0.0, scalar2=None, op0=ALU.is_ge), [delta.reg()], [mF.reg()])
    S.dve(lambda e: e.tensor_scalar(out=mB.ap, in0=delta.ap, scalar1=0.0, scalar2=None, op0=ALU.is_lt), [delta.reg()], [mB.reg()])
    for h in range(4):
        S.act(lambda e, h=h: e.activation(out=e1.ap, in_=dpos.ap, func=AF.Exp, scale=lg.ap[:, h:h + 1]), [dpos.reg(), lg.reg()], [e1.reg()])
        S.act(lambda e, h=h: e.activation(out=e2.ap, in_=dneg.ap, func=AF.Exp, scale=lg.ap[:, 4 + h:5 + h]), [dneg.reg(), lg.reg()], [e2.reg()])
        S.dve(lambda e: e.tensor_tensor(out=e1.ap, in0=e1.ap, in1=mF.ap, op=ALU.mult), [e1.reg(), mF.reg()], [e1.reg()])
        S.dve(lambda e: e.tensor_tensor(out=e2.ap, in0=e2.ap, in1=mB.ap, op=ALU.mult), [e2.reg(), mB.reg()], [e2.reg()])
        S.dve(lambda e, h=h: e.tensor_tensor(out=Dm.ap[:, h, :], in0=e1.ap, in1=e2.ap, op=ALU.add), [e1.reg(), e2.reg()], [Dm.reg(h * 128, (h + 1) * 128)])
    t1 = Buf(tiny.ap[:, 16:36], "sb", tiny.lo + 64, tiny.lo + 144, 4)
    S.act(lambda e: e.activation(out=t1.ap, in_=cols.ap[:, C_AP:C_AP + 20], func=AF.Exp, scale=-1.0), [cols.reg()], [t1.reg()])
    S.act(lambda e: e.activation(out=t1.ap, in_=t1.ap, func=AF.Ln, bias=one_c.ap), [t1.reg(), one_c.reg()], [t1.reg()])
    S.dve(lambda e: e.tensor_scalar(out=scl.ap, in0=t1.ap, scalar1=-8.0, scalar2=None, op0=ALU.mult), [t1.reg()], [scl.reg()])
    S.dve(lambda e: e.tensor_scalar(out=scl2.ap, in0=t1.ap, scalar1=-16.0, scalar2=None, op0=ALU.mult), [t1.reg()], [scl2.reg()])
    S.dve(lambda e: e.tensor_scalar(out=fw.ap, in0=cols.ap[:, C_CW:C_CW + 40], scalar1=convflag, scalar2=None, op0=ALU.mult), [cols.reg(), smalls.reg()], [fw.reg()])
    sc32 = Buf(tiny.ap[:, 40:48], "sb", tiny.lo + 160, tiny.lo + 192, 4)
    S.act(lambda e: e.activation(out=sc32.ap, in_=cols.ap[:, C_COND:C_COND + 8], func=AF.Silu), [cols.reg()], [sc32.reg()])
    S.dve(lambda e: e.tensor_copy(out=scb.ap, in_=bcast_ap(sc32.ap, [[1, 8], [0, 128]])), [sc32.reg()], [scb.reg()])

    wada_v = wada_d.rearrange("(k p) c -> p k c", p=128)
    modps = PS[7]
    for sl in range(8):
        slot, wv = load_w(wada_v[:, :, sl * 256:(sl + 1) * 256], "p (k c) -> p k c", k=8)
        for j in range(2):
            ft = sl * 2 + j
            S.mm([(modps.ap[:, ft:ft + 1], wv[:, k, j * 128:(j + 1) * 128], scb.ap[:, k, 0:1]) for k in range(8)],
                 [slot.reg(), scb.reg()], [modps.reg(ft, ft + 1)])
    S.dve(lambda e: e.tensor_tensor(out=Bcol.ap, in0=modps.ap[:, 0:8], in1=cols.ap[:, C_BSH:C_BSH + 8], op=ALU.add), [modps.reg(0, 16), cols.reg()], [Bcol.reg()])
    S.dve(lambda e: e.tensor_tensor(out=Acol.ap, in0=modps.ap[:, 8:16], in1=cols.ap[:, C_BSC:C_BSC + 8], op=ALU.add), [modps.reg(0, 16), cols.reg()], [Acol.reg()])
    S.dve(lambda e: e.scalar_tensor_tensor(out=Acol.ap, in0=Acol.ap, scalar=1.0, in1=cols.ap[:, C_NW:C_NW + 8], op0=ALU.add, op1=ALU.mult), [Acol.reg(), cols.reg()], [Acol.reg()])

    def checkpoint(k):
        if limit == k:
            S.enabled = False

    checkpoint(0)
    xs = [P0.f32(1024) for _ in range(2)]
    xn = [P0.f32(1024) for _ in range(2)]
    junk = P0.bf(1024)
    ssq = [P0.f32(1) for _ in range(2)]
    rstd = [P0.f32(1) for _ in range(2)]
    for n in range(NCH):
        b = n % 2
        S.dma("sp", xs[b].ap, x_d[n * 128:(n + 1) * 128, :], [], [xs[b].reg()])
        S.act(lambda e, b=b: e.activation(out=junk.ap, in_=xs[b].ap, func=AF.Square, accum_out=ssq[b].ap), [xs[b].reg()], [junk.reg(), ssq[b].reg()])
        S.act(lambda e, b=b: e.activation(out=rstd[b].ap, in_=ssq[b].ap, func=AF.Ln, scale=1.0 / D, bias=eps_c.ap), [ssq[b].reg(), eps_c.reg()], [rstd[b].reg()])
        S.act(lambda e, b=b: e.activation(out=rstd[b].ap, in_=rstd[b].ap, func=AF.Exp, scale=-0.5), [rstd[b].reg()], [rstd[b].reg()])
        S.act(lambda e, b=b: e.activation(out=xn[b].ap, in_=xs[b].ap, func=AF.Copy, scale=rstd[b].ap), [xs[b].reg(), rstd[b].reg()], [xn[b].reg()])
        if n == 0:
            checkpoint(10)
        if n == 1:
            checkpoint(13)
        for half in range(2):
            pb = PS[half]
            for j in range(4):
                k = half * 4 + j
                if n == 0 and half == 0 and j == 1:
                    checkpoint(11)
                S.pe(lambda e, k=k, j=j, pb=pb, b=b: e.transpose(pb.ap[:, j * 128:(j + 1) * 128], xn[b].ap[:, k * 128:(k + 1) * 128], ident.ap),
                     [xn[b].reg(k * 128, (k + 1) * 128), ident.reg()], [pb.reg(j * 128, (j + 1) * 128)])
            for j in range(4):
                k = half * 4 + j
                S.dve(lambda e, k=k, j=j, pb=pb, n=n: e.tensor_scalar(out=hT.ap[:, k, n * 128:(n + 1) * 128], in0=pb.ap[:, j * 128:(j + 1) * 128],
                                                                          scalar1=Acol.ap[:, k:k + 1], scalar2=Bcol.ap[:, k:k + 1], op0=ALU.mult, op1=ALU.add),
                      [pb.reg(j * 128, (j + 1) * 128), Acol.reg(), Bcol.reg()], [hT.reg(k * NT + n * 128, k * NT + (n + 1) * 128)])

        if n == 0:
            checkpoint(12)
    dump('hT', hT, 8 * NT, BF16)
    checkpoint(1)
    ipb = [0]

    def inproj_fm(col0, ncols, consumer, banks=(0, 1)):
        slot, wv = load_w(win_v[:, :, col0:col0 + ncols], "p (k c) -> p k c", k=8)

        def compute():
            for ct in range(ncols // 128):
                for tb in range(4):
                    pb = PS[banks[ipb[0] % len(banks)]]
                    ipb[0] += 1
                    S.mm([(pb.ap, wv[:, k, ct * 128:(ct + 1) * 128], hT.ap[:, k, tb * 512:(tb + 1) * 512]) for k in range(8)],
                         [slot.reg(), hT.reg()], [pb.reg()])
                    consumer(ct, tb, pb)
        return compute

    P1 = Alloc(PERS_END, NW)
    oT = P1.bf(16 * NT, ("p (k t) -> p k t", dict(k=16)))
    qT = P1.bf(2 * NT, ("p (k t) -> p k t", dict(k=2)))
    kT = P1.bf(2 * NT, ("p (k t) -> p k t", dict(k=2)))
    vt = P1.bf(NCH * 512, ("p (n e) -> p n e", dict(n=NCH)))
    U = [[P1.f32(1024, ("p (k e) -> p k e", dict(k=2))) for _ in range(2)] for _ in range(2)]
    SFb = [P1.bf(1024, ("p (k e) -> p k e", dict(k=2))) for _ in range(2)]
    TBw = [P1.bf(1024, ("p (k e) -> p k e", dict(k=2))) for _ in range(2)]
    TBr = [P1.bf(1024, ("p (k e) -> p k e", dict(k=2))) for _ in range(2)]
    kpr = [P1.bf(256) for _ in range(2)]
    Pm = [P1.bf(128) for _ in range(2)]
    qF = [P1.bf(256, ("p (k i) -> p k i", dict(k=2))) for _ in range(2)]
    qB = [P1.bf(256, ("p (k i) -> p k i", dict(k=2))) for _ in range(2)]
    og = [P1.bf(512) for _ in range(2)]
    junk1 = P1.bf(512)
    ss1 = [P1.f32(1) for _ in range(2)]
    rs1 = [P1.f32(1) for _ in range(2)]
    sgt = [P1.bf(512) for _ in range(2)]
    grow = P1.f32(1024)
    growb = P1.f32(1024)
    P1_END = P1.p

    tb_v = tb_d.rearrange("(h n p) (k e) -> h n p k e", h=4, n=16, k=2)

    def tb_reg(h, n):
        base = ((h * 16 + n) * 128) * 1024 * 2
        return ("scr", base, base + 128 * 1024 * 2)

    actdve = [0]

    def evac_copy(out_ap, out_reg, pb, scale=None):
        if actdve[0] % 2 == 0:
            if scale is None:
                S.act(lambda e: e.activation(out=out_ap, in_=pb.ap, func=AF.Copy), [pb.reg()], [out_reg])
            else:
                S.act(lambda e: e.activation(out=out_ap, in_=pb.ap, func=AF.Copy, scale=scale), [pb.reg()], [out_reg])
        else:
            if scale is None:
                S.dve(lambda e: e.tensor_copy(out=out_ap, in_=pb.ap), [pb.reg()], [out_reg])
            else:
                S.dve(lambda e: e.tensor_scalar(out=out_ap, in0=pb.ap, scalar1=scale, scalar2=None, op0=ALU.mult), [pb.reg()], [out_reg])
        actdve[0] += 1

    def gate_row_job():
        S.dma("sp", growb.ap[0:1, :], rows_d[0:1, :], [], [growb.reg()])
        for s_ in range(4):
            slot, wv = load_w(wada_v[:, :, 2048 + s_ * 256:2048 + (s_ + 1) * 256], "p (k c) -> p k c", k=8)
            pbg = PS[4 + s_ % 2]
            S.mm([(pbg.ap[:, 0:256], scb.ap[:, k, :], wv[:, k, :]) for k in range(8)], [slot.reg(), scb.reg()], [pbg.reg(0, 256)])
            S.dve(lambda e, s_=s_, pbg=pbg: e.tensor_tensor(out=grow.ap[0:1, s_ * 256:(s_ + 1) * 256], in0=pbg.ap[0:1, 0:256], in1=growb.ap[0:1, s_ * 256:(s_ + 1) * 256], op=ALU.add),
                  [pbg.reg(0, 256), growb.reg()], [grow.reg(s_ * 256, (s_ + 1) * 256)])
        S.dma("sp", gate_d, grow.ap[0:1, :], [grow.reg()], [("scr2", 0, 4096)])

    gate_ctr = [0]

    def make_gate_jobs(hg, banks):
        slabs = [load_w(win_v[:, :, OG + hg * 512 + s_ * 256:OG + hg * 512 + (s_ + 1) * 256], "p (k c) -> p k c", k=8) for s_ in range(2)]
        jobs = []
        for s_ in range(2):
            slot, wv = slabs[s_]
            for ct in range(2):
                for tb in range(4):
                    def job(slot=slot, wv=wv, s_=s_, ct=ct, tb=tb):
                        i_ = gate_ctr[0]
                        gate_ctr[0] += 1
                        b2 = i_ % 2
                        pb = PS[banks[i_ % len(banks)]]
                        S.mm([(pb.ap, wv[:, k, ct * 128:(ct + 1) * 128], hT.ap[:, k, tb * 512:(tb + 1) * 512]) for k in range(8)],
                             [slot.reg(), hT.reg()], [pb.reg()])
                        S.act(lambda e: e.activation(out=sgt[b2].ap, in_=pb.ap, func=AF.Silu), [pb.reg()], [sgt[b2].reg()])
                        k_ = hg * 4 + s_ * 2 + ct
                        rg = oT.reg(k_ * NT + tb * 512, k_ * NT + (tb + 1) * 512)
                        S.pool(lambda e: e.tensor_tensor(out=oT.ap[:, k_, tb * 512:(tb + 1) * 512], in0=oT.ap[:, k_, tb * 512:(tb + 1) * 512], in1=sgt[b2].ap, op=ALU.mult),
                               [rg, sgt[b2].reg()], [rg])
                    jobs.append(job)
        return jobs

    for h in range(4):
        gF, gB = h, 4 + h
        def q_cons(ct, tb, pb):
            evac_copy(qT.ap[:, ct, tb * 512:(tb + 1) * 512], qT.reg(ct * NT + tb * 512, ct * NT + (tb + 1) * 512), pb)

        def k_cons(ct, tb, pb):
            evac_copy(kT.ap[:, ct, tb * 512:(tb + 1) * 512], kT.reg(ct * NT + tb * 512, ct * NT + (tb + 1) * 512), pb)
        cq = inproj_fm(OQ + h * 256, 256, q_cons)
        ckk = inproj_fm(OK_ + h * 256, 256, k_cons)
        vsl = [load_w(win_v[:, :, OV + h * 512 + s * 256: OV + h * 512 + (s + 1) * 256], "p (k c) -> p k c", k=8) for s in range(2)]
        cq()
        ckk()
        for n in range(NCH):
            pb = PS[n % 2]
            for s in range(2):
                slot, wv = vsl[s]
                S.mm([(pb.ap[:, s * 256:(s + 1) * 256], hT.ap[:, k, n * 128:(n + 1) * 128], wv[:, k, :]) for k in range(8)],
                     [slot.reg(), hT.reg()], [pb.reg(s * 256, (s + 1) * 256)])
            evac_copy(vt.ap[:, n, :], vt.reg(n * 512, (n + 1) * 512), pb)
        if h == 0:
            dump('qT', qT, 2 * NT, BF16)
            dump('kT', kT, 2 * NT, BF16)
            dump('vt', vt, NCH * 512, BF16)
            checkpoint(2)
        for d in range(2):
            S.dma("sp", U[d][0].ap, iret_d[d, h].rearrange("(k p) e -> p k e", p=128), [], [U[d][0].reg()])
        def T_(n, single=False):
            kps = PSB[0] if single else PSB[n % 2]
            for dtl in range(2):
                S.pe(lambda e, dtl=dtl, n=n, kps=kps: e.transpose(kps.ap[:, dtl * 128:(dtl + 1) * 128], kT.ap[:, dtl, n * 128:(n + 1) * 128], identb.ap),
                     [kT.reg(dtl * NT + n * 128, dtl * NT + (n + 1) * 128), identb.reg()], [kps.reg(dtl * 128, (dtl + 1) * 128)])

        def KPR_(n, g, single=False):
            kps = PSB[0] if single else PSB[n % 2]
            b = n % 2
            S.act(lambda e: e.activation(out=kpr[b].ap, in_=kps.ap[:, 0:256], func=AF.Copy, scale=kd.ap[:, g:g + 1]),
                  [kps.reg(0, 256), kd.reg()], [kpr[b].reg()])

        def KV_(n):
            b = n % 2
            for dtl in range(2):
                pb = PS[6 + dtl]
                S.mm([(pb.ap, kpr[b].ap[:, dtl * 128:(dtl + 1) * 128], vt.ap[:, n, :])], [kpr[b].reg(), vt.reg(n * 512, (n + 1) * 512)], [pb.reg()])

        def UPD_(n, d, g, cur):
            nxt = 1 - cur
            for dtl in range(2):
                pb = PS[6 + dtl]
                S.dve(lambda e, dtl=dtl, pb=pb: e.scalar_tensor_tensor(
                    out=U[d][nxt].ap[:, dtl, :], in0=U[d][cur].ap[:, dtl, :], scalar=ck.ap[:, g * 16 + n:g * 16 + n + 1], in1=pb.ap, op0=ALU.mult, op1=ALU.add),
                    [U[d][cur].reg(dtl * 512, (dtl + 1) * 512), ck.reg(), pb.reg()], [U[d][nxt].reg(dtl * 512, (dtl + 1) * 512)])
            return nxt

        def TBW_(n, cur):
            b = n % 2
            S.act(lambda e: e.activation(out=TBw[b].ap, in_=U[1][cur].ap, func=AF.Copy, scale=flagsB(n)),
                  [U[1][cur].reg(), smalls.reg()], [TBw[b].reg()])
            S.dma("sp", tb_v[h, n], TBw[b].ap, [TBw[b].reg()], [tb_reg(h, n)])

        cur = 0
        gjobs = make_gate_jobs(h - 1, (3,)) if h >= 1 else []
        if h == 0:
            gate_row_job()
        T_(NCH - 1)
        KPR_(NCH - 1, gB)
        TBW_(NCH - 1, cur)
        for n in range(NCH - 1, -1, -1):
            if n >= 1:
                T_(n - 1)
            KV_(n)
            if n >= 1:
                KPR_(n - 1, gB)
            cur = UPD_(n, 1, gB, cur)
            if n >= 1:
                TBW_(n - 1, cur)
            if gjobs:
                gjobs.pop(0)()
            if n in (0, 2, 4, 6):
                S.dma("sp", sret_d[n // 2, 1, h].rearrange("(k p) e -> p k e", p=128), U[1][cur].ap, [U[1][cur].reg()], [])
        if h == 0:
            checkpoint(3)

        def S_(n):
            sps = PS[2]
            S.mm([(sps.ap[:, 0:128], kT.ap[:, dtl, n * 128:(n + 1) * 128], qT.ap[:, dtl, n * 128:(n + 1) * 128]) for dtl in range(2)],
                 [kT.reg(), qT.reg()], [sps.reg(0, 128)])

        def P_(n):
            sps = PS[2]
            b = n % 2
            S.dve(lambda e, h=h: e.scalar_tensor_tensor(out=Pm[b].ap, in0=sps.ap[:, 0:128], scalar=1.0 / 16.0, in1=Dm.ap[:, h, :], op0=ALU.mult, op1=ALU.mult),
                  [sps.reg(0, 128), Dm.reg()], [Pm[b].reg()])

        def Q_(n):
            b = n % 2
            S.pool(lambda e, gF=gF: e.tensor_tensor(out=qF[b].ap, in0=qT.ap[:, :, n * 128:(n + 1) * 128],
                                             in1=bcast_ap(rowd.ap[:, gF, :], [[0, 2], [1, 128]]), op=ALU.mult),
                   [qT.reg(), rowd.reg()], [qF[b].reg()])
            S.pool(lambda e, gB=gB: e.tensor_tensor(out=qB[b].ap, in0=qT.ap[:, :, n * 128:(n + 1) * 128],
                                             in1=bcast_ap(rowd.ap[:, gB, :], [[0, 2], [1, 128]]), op=ALU.mult),
                   [qT.reg(), rowd.reg()], [qB[b].reg()])

        def SF_(n, cur):
            b = n % 2
            S.dve(lambda e: e.tensor_scalar(out=SFb[b].ap, in0=U[0][cur].ap, scalar1=flagsF(n), scalar2=None, op0=ALU.mult),
                  [U[0][cur].reg(), smalls.reg()], [SFb[b].reg()])

        def O_(n):
            b = n % 2
            ops_ = PS[4 + b]
            mms = [(ops_.ap, Pm[b].ap, vt.ap[:, n, :])]
            mms += [(ops_.ap, qB[b].ap[:, dtl, :], TBr[b].ap[:, dtl, :]) for dtl in range(2)]
            mms += [(ops_.ap, qF[b].ap[:, dtl, :], SFb[b].ap[:, dtl, :]) for dtl in range(2)]
            S.mm(mms, [Pm[b].reg(), vt.reg(n * 512, (n + 1) * 512), qF[b].reg(), qB[b].reg(), SFb[b].reg(), TBr[b].reg()], [ops_.reg()])

        def NORM_(n):
            b = n % 2
            ops_ = PS[4 + b]
            S.act(lambda e: e.activation(out=junk1.ap, in_=ops_.ap, func=AF.Square, accum_out=ss1[b].ap), [ops_.reg()], [junk1.reg(), ss1[b].reg()])
            S.act(lambda e: e.activation(out=rs1[b].ap, in_=ss1[b].ap, func=AF.Ln, scale=1.0 / 512.0, bias=eps_c.ap), [ss1[b].reg(), eps_c.reg()], [rs1[b].reg()])
            S.act(lambda e: e.activation(out=rs1[b].ap, in_=rs1[b].ap, func=AF.Exp, scale=-0.5), [rs1[b].reg()], [rs1[b].reg()])
            S.act(lambda e: e.activation(out=og[b].ap, in_=ops_.ap, func=AF.Copy, scale=rs1[b].ap), [ops_.reg(), rs1[b].reg()], [og[b].reg()])

        def OGT_(n):
            b = n % 2
            tps = PSB[1]
            for et in range(4):
                S.pe(lambda e, et=et: e.transpose(tps.ap[:, et * 128:(et + 1) * 128], og[b].ap[:, et * 128:(et + 1) * 128], identb.ap),
                     [og[b].reg(et * 128, (et + 1) * 128), identb.reg()], [tps.reg(et * 128, (et + 1) * 128)])

        def OTE_(n):
            tps = PSB[1]
            S.dve(lambda e, h=h: e.tensor_tensor(out=oT.ap[:, h * 4:(h + 1) * 4, n * 128:(n + 1) * 128],
                                            in0=tps.ap[:, 0:512].rearrange("p (a t) -> p a t", a=4),
                                            in1=bcast_ap(cols.ap[:, C_GN + h * 4:C_GN + h * 4 + 4], [[1, 4], [0, 128]]), op=ALU.mult),
                  [tps.reg(0, 512), cols.reg()], [oT.reg((h * 4) * NT, (h * 4 + 4) * NT)])

        cur = 0
        S.dma("sp", TBr[0].ap, tb_v[h, 0], [tb_reg(h, 0)], [TBr[0].reg()])
        T_(0, True)
        KPR_(0, gF, True)
        S_(0)
        P_(0)
        Q_(0)
        SF_(0, cur)
        for n in range(NCH):
            last = (n + 1 == NCH)
            if not last:
                S.dma("sp", TBr[(n + 1) % 2].ap, tb_v[h, n + 1], [tb_reg(h, n + 1)], [TBr[(n + 1) % 2].reg()])
                T_(n + 1, True)
                S_(n + 1)
            KV_(n)
            O_(n)
            if n >= 1:
                OGT_(n - 1)
            cur = UPD_(n, 0, gF, cur)
            if n in (1, 3, 5, 7):
                S.dma("sp", sret_d[n // 2, 0, h].rearrange("(k p) e -> p k e", p=128), U[0][cur].ap, [U[0][cur].reg()], [])
            if not last:
                SF_(n + 1, cur)
                KPR_(n + 1, gF, True)
                P_(n + 1)
                Q_(n + 1)
            if n >= 1:
                OTE_(n - 1)
            NORM_(n)
        OGT_(NCH - 1)
        OTE_(NCH - 1)
        if h == 0:
            dump('ss1a', ss1[0], 1, F32)
            dump('ss1b', ss1[1], 1, F32)
            dump('rs1a', rs1[0], 1, F32)
            dump('rs1b', rs1[1], 1, F32)
            dump('og0', og[0], 512, BF16)
            dump('og1', og[1], 512, BF16)
            dump('oT0', oT, 4 * NT, BF16)
            checkpoint(4)
        if h == 3:
            for j_ in make_gate_jobs(3, (2, 3)):
                j_()

    dump('oT', oT, 16 * NT, BF16)
    checkpoint(5)
    P1b = Alloc(PERS_END + 16 * NT // 2, NW)
    PT = P1b.bf(8 * NT, ("p (k t) -> p k t", dict(k=8)))
    smr = [P1b.bf(512) for _ in range(2)]
    wrd_v = wrd_d.rearrange("(k p) c -> p k c", p=128)
    def rd_loads(ct):
        return (load_w(win_v[:, :, OMR + ct * 128:OMR + (ct + 1) * 128], "p (k c) -> p k c", k=8),
                load_w(wrd_v[:, :, ct * 128:(ct + 1) * 128], "p (k c) -> p k c", k=16))
    rd_next = rd_loads(0)
    for ct in range(8):
        (slot_m, wm), (slot_r, wr) = rd_next
        if ct + 1 < 8:
            rd_next = rd_loads(ct + 1)
        for tb in range(4):
            b2 = tb % 2
            pm = PS[b2]
            S.mm([(pm.ap, wm[:, k, :], hT.ap[:, k, tb * 512:(tb + 1) * 512]) for k in range(8)], [slot_m.reg(), hT.reg()], [pm.reg()])
            S.act(lambda e, b2=b2, pm=pm: e.activation(out=smr[b2].ap, in_=pm.ap, func=AF.Sigmoid), [pm.reg()], [smr[b2].reg()])
            pr = PS[2 + b2]
            S.mm([(pr.ap, wr[:, k, :], oT.ap[:, k, tb * 512:(tb + 1) * 512]) for k in range(16)], [slot_r.reg(), oT.reg()], [pr.reg()])
            S.dve(lambda e, b2=b2, pr=pr, ct=ct, tb=tb: e.tensor_tensor(out=PT.ap[:, ct, tb * 512:(tb + 1) * 512], in0=pr.ap, in1=smr[b2].ap, op=ALU.mult),
                  [pr.reg(), smr[b2].reg()], [PT.reg(ct * NT + tb * 512, ct * NT + (tb + 1) * 512)])

    dump('PTret', PT, 8 * NT, BF16)
    checkpoint(6)
    P2 = Alloc(PERS_END, PERS_END + 16 * NT // 2)
    yT = P2.bf(10 * NT, ("p (k t) -> p k t", dict(k=10)))
    hh = [P2.f32(NT) for _ in range(2)]
    sgq = [P2.bf(512) for _ in range(4)]
    xq_p2 = [P2.f32(512) for _ in range(2)]
    P2c = Alloc(P1b.p, NW)
    xq_a = [P2c.f32(512) for _ in range(4)]
    xcb0 = P2c.bf(NT)
    tgd = [[P2c.f32(NT // 2) for _ in range(2)] for _ in range(2)]
    aad = [[P2c.f32(NT // 2) for _ in range(2)] for _ in range(2)]
    hcol = P2c.f32(60)
    onep = P2c.f32(1)
    lnhalf = P2c.f32(1)
    xq_c = P2c.f32(512)
    S.dve(lambda e: e.tensor_scalar(out=hcol.ap[:, 0:40], in0=cols.ap[:, C_BA:C_BA + 40], scalar1=0.5, scalar2=None, op0=ALU.mult), [cols.reg()], [hcol.reg(0, 40)])
    S.dve(lambda e: e.tensor_scalar(out=hcol.ap[:, 40:60], in0=scl.ap, scalar1=0.5, scalar2=None, op0=ALU.mult), [scl.reg()], [hcol.reg(40, 60)])
    S.dve(lambda e: e.memset(onep.ap, 0.25), [], [onep.reg()])
    S.dve(lambda e: e.memset(lnhalf.ap, -0.6931471805599453), [], [lnhalf.reg()])

    def q3(buf, tb, c0, c1):
        ap = buf.ap
        return bass.AP(ap.tensor, ap.offset + tb * 512 + c0, [list(ap.ap[0]), [64, 8], [1, c1 - c0]])

    def qseq(buf, tb, r0, nr, c):
        ap = buf.ap
        return bass.AP(ap.tensor, ap.offset + tb * 512 + r0 * 64 + c, [list(ap.ap[0]), [256, 2], [64, nr]])

    def rev(ap2d, n):
        return bass.AP(ap2d.tensor, ap2d.offset + n - 1, [list(ap2d.ap[0]), [-1, n]])

    LW = Alloc(ring[0].lo // 4, ring[NSLOT - 1].hi // 4)
    lwsets = [(LW.bf(1024), LW.bf(256), LW.bf(256), LW.bf(1024)) for _ in range(2)]
    xcbs = [xcb0, xcb0]
    trd = [[t_, t_] for t_ in (LW.f32(NT // 2), LW.f32(NT // 2))]
    xq_l = LW.f32(512)
    xcq = [xq_a, [xq_p2[0], xq_p2[1], xq_c, xq_l]]

    def lru_loads(cb):
        bx_, ba_, bw_, bg_ = lwsets[cb % 2]
        vx = bx_.ap.rearrange("p (k c) -> p k c", k=8)
        va = ba_.ap.rearrange("p (d j) -> p d j", d=2)
        vw = bw_.ap.rearrange("p (d j) -> p d j", d=2)
        vg = bg_.ap.rearrange("p (k c) -> p k c", k=8)
        S.dma("pool", vx, win_v[:, :, OXL + cb * 128:OXL + (cb + 1) * 128], [], [bx_.reg()])
        S.dma("pool", va, wa_d[:, cb].rearrange("d i j -> i d j"), [], [ba_.reg()])
        S.dma("pool", vw, wx_d[:, cb].rearrange("d i j -> i d j"), [], [bw_.reg()])
        S.dma("pool", vg, win_v[:, :, OGL + cb * 128:OGL + (cb + 1) * 128], [], [bg_.reg()])
        return (bx_, vx), (ba_, va), (bw_, vw), (bg_, vg)

    XB = [PS[0], PS[1], PS[6], PS[7]]

    def lru_front_pe(cb, wts):
        (slot_x, wxl) = wts[0]
        for tb in range(4):
            pb = XB[tb]
            S.mm([(pb.ap, wxl[:, k, :], hT.ap[:, k, tb * 512:(tb + 1) * 512]) for k in range(8)], [slot_x.reg(), hT.reg()], [pb.reg()])

    def lru_front_conv(cb, tbs=(0, 1, 2, 3)):
        xq = xcq[cb % 2]
        cw = lambda j: cols.ap[:, C_CW + cb * 4 + j:C_CW + cb * 4 + j + 1]
        fwc = lambda j: fw.ap[:, cb * 4 + j:cb * 4 + j + 1]
        cbias = cols.ap[:, C_CB + cb:C_CB + cb + 1]
        for tb in tbs:
            pb = XB[tb]
            xc = xq[tb]
            xr = xc.reg()
            w2, w1, w0, w3 = cw(2), cw(1), cw(0), cw(3)
            S.act(lambda e, pb=pb, xc=xc, w2=w2: e.activation(out=xc.ap, in_=pb.ap, func=AF.Identity, scale=w2, bias=cbias),
                  [pb.reg(), cols.reg()], [xr])
            for (wj, o0, o1, i0, i1) in ((w1, 1, 64, 0, 63), (w0, 2, 64, 0, 62), (w3, 0, 63, 1, 64)):
                S.dve(lambda e, pb=pb, xc=xc, wj=wj, o0=o0, o1=o1, i0=i0, i1=i1: e.scalar_tensor_tensor(
                    out=q3(xc, 0, o0, o1), in0=q3(pb, 0, i0, i1), scalar=wj, in1=q3(xc, 0, o0, o1), op0=ALU.mult, op1=ALU.add),
                    [pb.reg(), xr, cols.reg()], [xr])
            f1, f0, f3 = fwc(1), fwc(0), fwc(3)
            for (fj, oc, ic, orow, irow) in ((f1, 0, 63, 1, 0), (f0, 0, 62, 1, 0), (f0, 1, 63, 1, 0), (f3, 63, 0, 0, 1)):
                S.dve(lambda e, pb=pb, xc=xc, fj=fj, oc=oc, ic=ic, orow=orow, irow=irow: e.scalar_tensor_tensor(
                    out=qseq(xc, 0, orow, 3, oc), in0=qseq(pb, 0, irow, 3, ic), scalar=fj, in1=qseq(xc, 0, orow, 3, oc), op0=ALU.mult, op1=ALU.add),
                    [pb.reg(), xr, fw.reg()], [xr])

    def lru_casts(cb):
        xq = xcq[cb % 2]
        xcb = xcbs[cb % 2]
        for tb in range(4):
            S.act(lambda e, tb=tb: e.activation(out=xcb.ap[:, tb * 512:(tb + 1) * 512], in_=xq[tb].ap, func=AF.Copy),
                  [xq[tb].reg()], [xcb.reg(tb * 512, (tb + 1) * 512)])

    def lru_step(cb, st, wts, part):
        (slot_a, wav), (slot_w, wxv) = wts[1], wts[2]
        xq = xcq[cb % 2]
        xcb = xcbs[cb % 2]
        items = []
        for d in range(2):
            half = st if d == 0 else 1 - st
            qs = [2 * half, 2 * half + 1] if d == 0 else [2 * half + 1, 2 * half]
            items.append((d, half, qs))
        col = lambda base, d: hcol.ap[:, base + d * 10 + cb:base + d * 10 + cb + 1]
        lr = lambda b_, half, tb: b_.reg((tb - 2 * half) * 512, (tb - 2 * half + 1) * 512)
        la = lambda b_, half, tb: b_.ap[:, (tb - 2 * half) * 512:(tb - 2 * half + 1) * 512]
        gq = lambda b_, tb: b_.ap[:, tb * 512:(tb + 1) * 512]
        gr = lambda b_, tb: b_.reg(tb * 512, (tb + 1) * 512)
        pi = [0]
        doA = (part == 'A')
        for (d, half, qs) in (items if doA else []):
            for tb in qs:
                pa = PS[2 + pi[0] % 2]
                pg = PS[4 + pi[0] % 2]
                pi[0] += 1
                S.mm([(pa.ap, wav[:, d, :], gq(xcb, tb))], [slot_a.reg(), gr(xcb, tb)], [pa.reg()])
                S.act(lambda e, pa=pa, tb=tb, d=d, half=half: e.activation(out=la(trd[d][st], half, tb), in_=pa.ap, func=AF.Tanh, scale=0.5, bias=col(0, d)),
                      [pa.reg(), hcol.reg()], [lr(trd[d][st], half, tb)])
                S.mm([(pg.ap, wxv[:, d, :], gq(xcb, tb))], [slot_w.reg(), gr(xcb, tb)], [pg.reg()])
                S.act(lambda e, pg=pg, tb=tb, d=d, half=half: e.activation(out=la(tgd[d][st], half, tb), in_=pg.ap, func=AF.Tanh, scale=0.5, bias=col(20, d)),
                      [pg.reg(), hcol.reg()], [lr(tgd[d][st], half, tb)])
        for (d, half, qs) in (items if doA else []):
            for tb in qs:
                S.dve(lambda e, tb=tb, d=d, half=half: e.scalar_tensor_tensor(out=la(tgd[d][st], half, tb), in0=la(tgd[d][st], half, tb), scalar=1.0, in1=xq[tb].ap, op0=ALU.add, op1=ALU.mult),
                      [lr(tgd[d][st], half, tb), xq[tb].reg()], [lr(tgd[d][st], half, tb)])
        for (d, half, qs) in (items if doA else []):
            for tb in qs:
                S.act(lambda e, tb=tb, d=d, half=half: e.activation(out=la(aad[d][st], half, tb), in_=la(trd[d][st], half, tb), func=AF.Exp, scale=col(40, d), bias=col(40, d)),
                      [lr(trd[d][st], half, tb), hcol.reg()], [lr(aad[d][st], half, tb)])
        for (d, half, qs) in (items if doA else []):
            for tb in qs:
                S.dve(lambda e, tb=tb, d=d, half=half: e.scalar_tensor_tensor(out=la(trd[d][st], half, tb), in0=la(aad[d][st], half, tb), scalar=0.9999995, in1=la(aad[d][st], half, tb), op0=ALU.min, op1=ALU.mult),
                      [lr(aad[d][st], half, tb)], [lr(trd[d][st], half, tb)])
        if part == 'A':
            return
        for (d, half, qs) in items:
            for tb in qs:
                S.act(lambda e, tb=tb, d=d, half=half: e.activation(out=la(trd[d][st], half, tb), in_=la(trd[d][st], half, tb), func=AF.Sqrt, scale=-0.25, bias=onep.ap),
                      [lr(trd[d][st], half, tb), onep.reg()], [lr(trd[d][st], half, tb)])
        for (d, half, qs) in items:
            ap = aad[d][st].ap
            if d == 0:
                off, cnt = (256, 3) if half == 0 else (0, 4)
            else:
                off, cnt = (255, 4) if half == 0 else (255, 3)
            bv = bass.AP(ap.tensor, ap.offset + off, [list(ap.ap[0]), [256, cnt]])
            S.dve(lambda e, bv=bv: e.tensor_scalar(out=bv, in0=bv, scalar1=lrukeep, scalar2=None, op0=ALU.mult), [aad[d][st].reg(), smalls.reg()], [aad[d][st].reg()])
        for i_ in range(2):
            for (d, half, qs) in items:
                tb = qs[i_]
                h0 = cols.ap[:, C_H0 + d * 10 + cb:C_H0 + d * 10 + cb + 1]
                S.dve(lambda e, tb=tb, d=d, half=half: e.tensor_tensor(out=la(tgd[d][st], half, tb), in0=la(tgd[d][st], half, tb), in1=la(trd[d][st], half, tb), op=ALU.mult),
                      [lr(tgd[d][st], half, tb), lr(trd[d][st], half, tb)], [lr(tgd[d][st], half, tb)])
                first = (d == 0 and tb == 0) or (d == 1 and tb == 3)
                if first:
                    init, ireg = h0, cols.reg()
                elif d == 0:
                    init, ireg = hh[0].ap[:, tb * 512 - 1:tb * 512], hh[0].reg(tb * 512 - 1, tb * 512)
                else:
                    init, ireg = hh[1].ap[:, (tb + 1) * 512:(tb + 1) * 512 + 1], hh[1].reg((tb + 1) * 512, (tb + 1) * 512 + 1)
                if d == 0:
                    S.dve(lambda e, tb=tb, init=init, half=half: e.tensor_tensor_scan(out=gq(hh[0], tb), data0=la(aad[0][st], half, tb), data1=la(tgd[0][st], half, tb), initial=init, op0=ALU.mult, op1=ALU.add),
                          [lr(aad[0][st], half, tb), lr(tgd[0][st], half, tb), ireg], [gr(hh[0], tb)])
                else:
                    S.dve(lambda e, tb=tb, init=init, half=half: e.tensor_tensor_scan(out=rev(gq(hh[1], tb), 512), data0=rev(la(aad[1][st], half, tb), 512), data1=rev(la(tgd[1][st], half, tb), 512), initial=init, op0=ALU.mult, op1=ALU.add),
                          [lr(aad[1][st], half, tb), lr(tgd[1][st], half, tb), ireg], [gr(hh[1], tb)])

    def lru_back(cb, wts):
        (slot_g, wgl) = wts[3]
        fo = cb * 8
        S.pool(lambda e: e.tensor_copy(out=lruout.ap[:, fo:fo + 4], in_=bass.AP(hh[0].ap.tensor, hh[0].ap.offset + 255, [list(hh[0].ap.ap[0]), [256, 4]])),
               [hh[0].reg()], [lruout.reg(fo, fo + 4)])
        S.pool(lambda e: e.tensor_copy(out=lruout.ap[:, fo + 4:fo + 8], in_=bass.AP(hh[1].ap.tensor, hh[1].ap.offset, [list(hh[1].ap.ap[0]), [256, 4]])),
               [hh[1].reg()], [lruout.reg(fo + 4, fo + 8)])
        for tb in range(4):
            b2 = tb % 2
            pb = PS[2 + b2]
            S.mm([(pb.ap, wgl[:, k, :], hT.ap[:, k, tb * 512:(tb + 1) * 512]) for k in range(8)], [slot_g.reg(), hT.reg()], [pb.reg()])
            S.act(lambda e, pb=pb, tb=tb: e.activation(out=sgq[tb].ap, in_=pb.ap, func=AF.Silu), [pb.reg()], [sgq[tb].reg()])
            S.pool(lambda e, tb=tb: e.tensor_tensor(out=hh[0].ap[:, tb * 512:(tb + 1) * 512], in0=hh[0].ap[:, tb * 512:(tb + 1) * 512], in1=hh[1].ap[:, tb * 512:(tb + 1) * 512], op=ALU.add),
                   [hh[0].reg(tb * 512, (tb + 1) * 512), hh[1].reg(tb * 512, (tb + 1) * 512)], [hh[0].reg(tb * 512, (tb + 1) * 512)])
            S.pool(lambda e, tb=tb: e.tensor_tensor(out=yT.ap[:, cb, tb * 512:(tb + 1) * 512], in0=hh[0].ap[:, tb * 512:(tb + 1) * 512], in1=sgq[tb].ap, op=ALU.mult),
                  [hh[0].reg(tb * 512, (tb + 1) * 512), sgq[tb].reg()], [yT.reg(cb * NT + tb * 512, cb * NT + (tb + 1) * 512)])

    wts_cur = lru_loads(0)
    lru_front_pe(0, wts_cur)
    lru_front_conv(0)
    lru_casts(0)
    for cb in range(10):
        nxt = cb + 1 < 10
        lru_step(cb, 0, wts_cur, 'A')
        wts_nxt = None
        if nxt:
            wts_nxt = lru_loads(cb + 1)
            lru_front_pe(cb + 1, wts_nxt)
            lru_front_conv(cb + 1, (0,))
        lru_step(cb, 0, wts_cur, 'B')
        if nxt:
            lru_front_conv(cb + 1, (1,))
        lru_step(cb, 1, wts_cur, 'A')
        if nxt:
            lru_front_conv(cb + 1, (2, 3))
        lru_step(cb, 1, wts_cur, 'B')
        lru_back(cb, wts_cur)
        if cb + 1 < 10:
            lru_casts(cb + 1)
        if cb == 0:
            dump('hf0', hh[0], NT, F32)
            dump('hb0', hh[1], NT, F32)
        wts_cur = wts_nxt
    S.dma("sp", slru_d, lruout.ap, [lruout.reg()], [])

    dump('yT', yT, 10 * NT, BF16)
    checkpoint(7)
    wld_v = wld_d.rearrange("(k p) c -> p k c", p=128)
    P2d = Alloc(P1b.p, NW)
    sml = [P2d.f32(512) for _ in range(2)]
    tml = [P2d.f32(512) for _ in range(2)]
    def ld_loads(ct):
        return (load_w(win_v[:, :, OML + ct * 128:OML + (ct + 1) * 128], "p (k c) -> p k c", k=8),
                load_w(wld_v[:, :, ct * 128:(ct + 1) * 128], "p (k c) -> p k c", k=10))
    ld_next = ld_loads(0)
    WOA = Alloc(hh[0].lo // 4, PERS_END + 16 * NT // 2)
    wo = WOA.bf(8 * 1024, ("p (k c) -> p k c", dict(k=8)))
    wout_v = wout_d.rearrange("(k p) c -> p k c", p=128)
    S.dma("pool", wo.ap, wout_v, [], [wo.reg()])
    for ct in range(8):
        (slot_m, wm), (slot_r, wr) = ld_next
        if ct + 1 < 8:
            ld_next = ld_loads(ct + 1)
        for tb in range(4):
            b2 = tb % 2
            pm = PS[b2]
            S.mm([(pm.ap, wm[:, k, :], hT.ap[:, k, tb * 512:(tb + 1) * 512]) for k in range(8)], [slot_m.reg(), hT.reg()], [pm.reg()])
            S.act(lambda e, b2=b2, pm=pm: e.activation(out=sml[b2].ap, in_=pm.ap, func=AF.Sigmoid), [pm.reg()], [sml[b2].reg()])
            pr = PS[2 + b2]
            S.mm([(pr.ap, wr[:, k, :], yT.ap[:, k, tb * 512:(tb + 1) * 512]) for k in range(10)], [slot_r.reg(), yT.reg()], [pr.reg()])
            S.dve(lambda e, b2=b2, pr=pr: e.tensor_tensor(out=tml[b2].ap, in0=pr.ap, in1=sml[b2].ap, op=ALU.mult), [pr.reg(), sml[b2].reg()], [tml[b2].reg()])
            S.pool(lambda e, b2=b2, ct=ct, tb=tb: e.tensor_tensor(out=PT.ap[:, ct, tb * 512:(tb + 1) * 512], in0=PT.ap[:, ct, tb * 512:(tb + 1) * 512], in1=tml[b2].ap, op=ALU.add),
                   [PT.reg(ct * NT + tb * 512, ct * NT + (tb + 1) * 512), tml[b2].reg()], [PT.reg(ct * NT + tb * 512, ct * NT + (tb + 1) * 512)])

    dump('PT', PT, 8 * NT, BF16)
    checkpoint(8)
    P3 = Alloc(PERS_END, PERS_END + 16 * NT // 2)
    gate_bc = P3.f32(1024)
    fnw_bc = P3.f32(1024)
    x3 = [P3.f32(1024) for _ in range(4)]
    y3 = [P3.f32(1024) for _ in range(4)]
    junk3 = P2d.bf(1024)
    ss3 = [P2d.f32(1) for _ in range(2)]
    rs3 = [P2d.f32(1) for _ in range(2)]
    assert P3.p <= PERS_END + 10 * NT // 2, "output-phase tiles must stay inside the dead yT region"
    wout_v = wout_d.rearrange("(k p) c -> p k c", p=128)
    S.dma("sp", gate_bc.ap, bass.AP(gate_d.tensor, 0, [[0, 128], [1, 1024]]), [("scr2", 0, 4096)], [gate_bc.reg()])
    S.dma("sp", fnw_bc.ap, bass.AP(rows_d.tensor, 1024, [[0, 128], [1, 1024]]), [], [fnw_bc.reg()])
    def p3_load(n):
        S.dma("sp", x3[n % 4].ap, x_d[n * 128:(n + 1) * 128, :], [], [x3[n % 4].reg()])

    def p3_front(n):
        b = n % 2
        yb = y3[n % 4]
        if n + 3 < NCH:
            p3_load(n + 3)
        for half in range(2):
            pb = PS[2 * b + half]
            S.mm([(pb.ap, PT.ap[:, k, n * 128:(n + 1) * 128], wo.ap[:, k, half * 512:(half + 1) * 512]) for k in range(8)],
                 [PT.reg(), wo.reg()], [pb.reg()])
            S.dve(lambda e, half=half, pb=pb: e.tensor_tensor(out=yb.ap[:, half * 512:(half + 1) * 512], in0=pb.ap, in1=gate_bc.ap[:, half * 512:(half + 1) * 512], op=ALU.mult),
                  [pb.reg(), gate_bc.reg()], [yb.reg(half * 512, (half + 1) * 512)])
        S.pool(lambda e: e.tensor_tensor(out=yb.ap, in0=yb.ap, in1=x3[n % 4].ap, op=ALU.add), [yb.reg(), x3[n % 4].reg()], [yb.reg()])

    def p3_back(n):
        b = n % 2
        yb = y3[n % 4]
        S.act(lambda e: e.activation(out=junk3.ap, in_=yb.ap, func=AF.Square, accum_out=ss3[b].ap), [yb.reg()], [junk3.reg(), ss3[b].reg()])
        S.act(lambda e: e.activation(out=rs3[b].ap, in_=ss3[b].ap, func=AF.Ln, scale=1.0 / D, bias=eps_c.ap), [ss3[b].reg(), eps_c.reg()], [rs3[b].reg()])
        S.act(lambda e: e.activation(out=rs3[b].ap, in_=rs3[b].ap, func=AF.Exp, scale=-0.5), [rs3[b].reg()], [rs3[b].reg()])
        S.dve(lambda e: e.scalar_tensor_tensor(out=yb.ap, in0=yb.ap, scalar=rs3[b].ap, in1=fnw_bc.ap, op0=ALU.mult, op1=ALU.mult),
              [yb.reg(), rs3[b].reg(), fnw_bc.reg()], [yb.reg()])
        S.dma("sp", y_d[n * 128:(n + 1) * 128, :], yb.ap, [yb.reg()], [])

    for n_ in range(3):
        p3_load(n_)
    p3_front(0)
    for n in range(NCH):
        if n + 1 < NCH:
            p3_front(n + 1)
        p3_back(n)

    S.emit_all(sems)
    es.close()
    S.dumps = dumps
    return nc, S


_CACHE = {}


def _consts():
    c = np.zeros((128, 257), np.float32)
    c[:, 0:128] = np.eye(128, dtype=np.float32)
    c[:, 128:256] = np.arange(128, dtype=np.float32)[None, :]
    c[:, 256] = np.arange(128, dtype=np.float32)
    return c


def _colv(v, nt):
    return np.ascontiguousarray(np.asarray(v, np.float32).reshape(nt, 128).T)


def make_in_maps(x_prompt, x_sample, state_ret, state_lru, c, c_ctx, norm_w, w_ada, b_ada, w_in,
                 ret_decay_logit, ret_gn_w, w_ret_down, conv_w, conv_b, lru_wa, lru_ba, lru_wx, lru_bx,
                 lru_a_param, w_lru_down, w_out, final_norm_w):
    f = lambda a: np.ascontiguousarray(np.asarray(a, np.float32))
    consts = _consts()
    rows = np.stack([f(b_ada)[0, 2048:3072], f(final_norm_w)], 0)
    shared = dict(consts=consts, rows=rows, w_ada=f(w_ada)[0], w_in=f(w_in)[0], w_ret_down=f(w_ret_down)[0],
                  w_lru_down=f(w_lru_down)[0], w_out=f(w_out)[0], lru_wa=f(lru_wa)[0], lru_wx=f(lru_wx)[0])
    in_maps = []
    for core in range(8):
        cols = np.zeros((128, NCOLS), np.float32)
        cols[:, C_NW:C_NW + 8] = _colv(norm_w[0], 8)
        cols[:, C_BSH:C_BSH + 8] = _colv(b_ada[0, 0:1024], 8)
        cols[:, C_BSC:C_BSC + 8] = _colv(b_ada[0, 1024:2048], 8)
        cols[:, C_GN:C_GN + 16] = _colv(ret_gn_w[0], 16)
        cw = np.asarray(conv_w, np.float32)[0]
        for cb in range(10):
            for j in range(4):
                cols[:, C_CW + cb * 4 + j] = cw[j, cb * 128:(cb + 1) * 128]
        cols[:, C_CB:C_CB + 10] = _colv(conv_b[0], 10)
        for d in range(2):
            cols[:, C_BA + d * 10:C_BA + d * 10 + 10] = _colv(lru_ba[0, d], 10)
            cols[:, C_BX + d * 10:C_BX + d * 10 + 10] = _colv(lru_bx[0, d], 10)
            cols[:, C_AP + d * 10:C_AP + d * 10 + 10] = _colv(lru_a_param[0, d], 10)
        smalls = np.zeros((1, NSM), np.float32)
        smalls[0, 0:8] = np.asarray(ret_decay_logit, np.float32)[0].reshape(8)
        if core < 4:
            x = f(x_sample[core])
            cols[:, C_COND:C_COND + 8] = _colv(c[core], 8)
            init_ret = f(state_ret[core, 0])
            for d in range(2):
                cols[:, C_H0 + d * 10:C_H0 + d * 10 + 10] = _colv(state_lru[core, 0, d], 10)
            smalls[0, 8:24] = 1.0
            smalls[0, 24:40] = 1.0
            smalls[0, 40] = 0.0
            smalls[0, 41] = 1.0
        else:
            p0 = (core - 4) * 4
            x = np.zeros((NT, D), np.float32)
            x[:1024] = np.asarray(x_prompt[p0:p0 + 4], np.float32).reshape(1024, D)
            cols[:, C_COND:C_COND + 8] = _colv(c_ctx, 8)
            init_ret = np.zeros((2, 4, 256, 512), np.float32)
            kf = np.array([1.0 if (n % 2 == 1) else 0.0 for n in range(16)], np.float32)
            kf[0] = 1.0
            kb = np.array([1.0 if (n % 2 == 0) else 0.0 for n in range(16)], np.float32)
            kb[15] = 1.0
            smalls[0, 8:24] = kf
            smalls[0, 24:40] = kb
            smalls[0, 40] = 1.0
            smalls[0, 41] = 0.0
        m = dict(shared)
        m.update(x=x, cols=cols, smalls=smalls, init_ret=init_ret)
        in_maps.append(m)
    return in_maps


def kernel(**inputs):
    if "nc" not in _CACHE:
        _CACHE["nc"] = build_program(DEBUG)[0]
    nc = _CACHE["nc"]
    in_maps = make_in_maps(**inputs)
    res = run_bass_kernel_spmd(nc, in_maps, core_ids=list(range(8)))
    r = res.results
    y_sample = np.stack([r[i]["y"] for i in range(4)], 0).astype(np.float32)
    y_prompt = np.concatenate([r[i]["y"][:1024].reshape(4, 256, D) for i in range(4, 8)], 0).astype(np.float32)
    st_ret = np.concatenate([r[i]["st_ret"] for i in range(4, 8)], 0)[:, None].astype(np.float32)
    lr = []
    for i in range(4, 8):
        a = r[i]["st_lru"].reshape(128, 10, 2, 4)
        lr.append(a.transpose(3, 2, 1, 0).reshape(4, 2, 1280))
    st_lru = np.concatenate(lr, 0)[:, None].astype(np.float32)
    if DEBUG:
        _CACHE["dbg"] = r
    return (y_prompt, y_sample, st_ret, st_lru)
```

```python
import numpy as np
import concourse.bass as bass
import concourse.mybir as mybir
from concourse.bass_utils import run_bass_kernel_spmd

F32 = mybir.dt.float32
BF16 = mybir.dt.bfloat16
AF = mybir.ActivationFunctionType
ALU = mybir.AluOpType

D = 1024
DIN = 10752
NT = 2048
NCH = 16
EPS = 1e-6
OQ, OK_, OV, OG, OXL, OGL, OMR, OML = 0, 1024, 2048, 4096, 6144, 7424, 8704, 9728
NCOLS = 178
C_NW, C_COND, C_BSH, C_BSC, C_GN, C_CW, C_CB, C_BA, C_BX, C_AP, C_H0 = 0, 8, 16, 24, 32, 48, 88, 98, 118, 138, 158
NSM = 8 + 34

DEBUG = False


class Buf:
    def __init__(self, ap, arena, lo, hi, esize):
        self.ap = ap
        self.arena = arena
        self.lo = lo
        self.hi = hi
        self.esize = esize

    def reg(self, elo=None, ehi=None):
        if elo is None:
            return (self.arena, self.lo, self.hi)
        return (self.arena, self.lo + elo * self.esize, self.lo + ehi * self.esize)


class Op:
    __slots__ = ("eng", "emit", "deps", "is_dma", "signal", "sigcount", "dsem", "dval", "dprev", "gid")


class Sched:
    ENGS = ("pe", "act", "dve", "pool", "sp")

    def __init__(self, nc):
        self.nc = nc
        self.ops = []
        self.eng_ops = {e: [] for e in self.ENGS}
        self.recs = {}
        self.enabled = True

    def _access(self, opid, eng, is_dma, regs, is_write, deps):
        for (a, lo, hi) in regs:
            if a.startswith("ps"):
                r = self.recs.get(a)
                if r is not None:
                    rop, rw, reng = r
                    if rop != opid:
                        if is_write or rw or reng != eng:
                            deps.add(rop)
                    else:
                        is_write = is_write or rw
                self.recs[a] = (opid, is_write, eng)
                continue
            recs = self.recs.get(a)
            if recs is None:
                recs = []
                self.recs[a] = recs
            new = []
            for r in recs:
                rlo, rhi, rop, rw, reng, rdma = r
                if rhi <= lo or rlo >= hi:
                    new.append(r)
                    continue
                if (is_write or rw) and rop != opid:
                    deps.add(rop)
                covered = lo <= rlo and rhi <= hi
                if covered and is_write:
                    continue
                if covered and (not is_write) and (not rw) and reng == eng and not rdma and not is_dma:
                    continue
                new.append(r)
            new.append((lo, hi, opid, is_write, eng, is_dma))
            self.recs[a] = new

    def add(self, eng, emit, reads=(), writes=(), dma=False):
        if not self.enabled:
            return -1
        op = Op()
        op.eng = eng
        op.emit = emit
        op.is_dma = dma
        op.signal = False
        opid = len(self.ops)
        deps = set()
        self._access(opid, eng, dma, writes, True, deps)
        self._access(opid, eng, dma, reads, False, deps)
        op.deps = deps
        self.ops.append(op)
        self.eng_ops[eng].append(opid)
        return opid

    def pe(self, emit, reads, writes):
        return self.add("pe", emit, reads, writes)

    def act(self, emit, reads, writes):
        return self.add("act", emit, reads, writes)

    def dve(self, emit, reads, writes):
        return self.add("dve", emit, reads, writes)

    def pool(self, emit, reads, writes):
        return self.add("pool", emit, reads, writes)

    def dma(self, q, out, in_, reads, writes, **kw):
        return self.add(q, lambda e: e.dma_start(out=out, in_=in_, **kw), reads, writes, dma=True)

    def mm(self, mms, reads, writes):
        n = len(mms)

        def emit(e):
            ins = None
            for i, (o, l, r) in enumerate(mms):
                ins = e.matmul(o, l, r, start=(i == 0), stop=(i == n - 1))
            return ins
        return self.add("pe", emit, reads, writes)

    def emit_all(self, sems):
        nc = self.nc
        ops = self.ops
        for op in ops:
            for d in op.deps:
                if not ops[d].is_dma:
                    ops[d].signal = True
        for e in self.ENGS:
            comp = [i for i in self.eng_ops[e] if not ops[i].is_dma]
            if comp:
                ops[comp[-1]].signal = True
        esem = {e: sems.pop() for e in ("pe", "act", "dve", "pool")}
        final = {}
        for e in ("pe", "act", "dve", "pool"):
            c = 0
            for i in self.eng_ops[e]:
                op = ops[i]
                if op.is_dma:
                    continue
                if op.signal:
                    c += 1
                op.sigcount = c if op.signal else None
            final[esem[e]] = c
        KD = 20
        dpool = {q: [sems.pop() for _ in range(KD)] for q in ("sp", "pool")}
        for q in ("sp", "pool"):
            cnt = [0] * KD
            k = 0
            for i in self.eng_ops[q]:
                op = ops[i]
                if not op.is_dma:
                    continue
                s = k % KD
                op.dsem = dpool[q][s]
                op.dprev = cnt[s] * 16
                cnt[s] += 1
                op.dval = cnt[s] * 16
                k += 1
            for s in range(KD):
                final[dpool[q][s]] = cnt[s] * 16
        engobj = {"pe": "tensor", "act": "scalar", "dve": "vector", "pool": "gpsimd", "sp": "sync"}
        self.nwaits = 0

        self.trace = {}

        def run_engine(ename, e):
            waited = {}
            tr = []
            self.trace[ename] = tr

            def wait(sem, val):
                if val <= 0:
                    return
                if waited.get(sem, 0) >= val:
                    return
                waited[sem] = val
                e.wait_ge(sem, val)
                tr.append(("w", id(sem), val))
                self.nwaits += 1
            for i in self.eng_ops[ename]:
                op = ops[i]
                need = {}
                for d in op.deps:
                    dop = ops[d]
                    if dop.is_dma:
                        sm, vl = dop.dsem, dop.dval
                    else:
                        if dop.eng == "pe" and ename == "pe":
                            continue
                        sm, vl = esem[dop.eng], dop.sigcount
                    if need.get(id(sm), (None, 0))[1] < vl:
                        need[id(sm)] = (sm, vl)
                for sm, vl in need.values():
                    wait(sm, vl)
                if op.is_dma:
                    wait(op.dsem, op.dprev)
                    ins = op.emit(e)
                    ins.then_inc(op.dsem, 16)
                    tr.append(("i", id(op.dsem), 16, i))
                else:
                    ins = op.emit(e)
                    if op.signal:
                        ins.then_inc(esem[ename], 1)
                        tr.append(("i", id(esem[ename]), 1, i))
                    else:
                        tr.append(("n", i))
            if ename == "sp":
                for sem, val in final.items():
                    wait(sem, val)

        with nc.Block() as block:
            @block.tensor
            def _(e):
                run_engine("pe", e)

            @block.scalar
            def _(e):
                run_engine("act", e)

            @block.vector
            def _(e):
                run_engine("dve", e)

            @block.gpsimd
            def _(e):
                run_engine("pool", e)

            @block.sync
            def _(e):
                run_engine("sp", e)


def bcast_ap(ap, dims):
    return bass.AP(ap.tensor, ap.offset, [list(ap.ap[0])] + [list(d) for d in dims])


class _Stop(Exception):
    pass


def build_program(debug=False, limit=None):
    nc = bass.Bass("TRN2", target_bir_lowering=False)
    dt = nc.dram_tensor
    x_d = dt("x", [NT, D], F32, kind="ExternalInput").ap()
    cols_d = dt("cols", [128, NCOLS], F32, kind="ExternalInput").ap()
    rows_d = dt("rows", [2, D], F32, kind="ExternalInput").ap()
    sm_d = dt("smalls", [1, NSM], F32, kind="ExternalInput").ap()
    consts_d = dt("consts", [128, 257], F32, kind="ExternalInput").ap()
    iret_d = dt("init_ret", [2, 4, 256, 512], F32, kind="ExternalInput").ap()
    wada_d = dt("w_ada", [D, 3 * D], F32, kind="ExternalInput").ap()
    win_d = dt("w_in", [D, DIN], F32, kind="ExternalInput").ap()
    wrd_d = dt("w_ret_down", [2048, D], F32, kind="ExternalInput").ap()
    wld_d = dt("w_lru_down", [1280, D], F32, kind="ExternalInput").ap()
    wout_d = dt("w_out", [D, D], F32, kind="ExternalInput").ap()
    wa_d = dt("lru_wa", [2, 10, 128, 128], F32, kind="ExternalInput").ap()
    wx_d = dt("lru_wx", [2, 10, 128, 128], F32, kind="ExternalInput").ap()
    y_d = dt("y", [NT, D], F32, kind="ExternalOutput").ap()
    sret_d = dt("st_ret", [4, 2, 4, 256, 512], F32, kind="ExternalOutput").ap()
    slru_d = dt("st_lru", [128, 80], F32, kind="ExternalOutput").ap()
    tb_d = dt("tb_scr", [4 * 16 * 128, 1024], BF16, kind="Internal").ap()
    gate_d = dt("gate_scr", [1, D], F32, kind="Internal").ap()
    dbg_d = None
    if debug:
        dbg_d = dt("dbg", [128, 8 * 2048], F32, kind="ExternalOutput").ap()

    S = Sched(nc)
    NW = 52800
    dumps = {}

    def dump(name, buf, n, dtype):
        if not debug:
            return
        dten = dt("dbg_" + name, [128, n], dtype, kind="ExternalOutput").ap()
        dumps[name] = (n, dtype)
        src = bass.AP(buf.ap.tensor, buf.ap.offset, [list(buf.ap.ap[0]), [1, n]])
        S.dma("sp", dten, src, [buf.reg()], [])
    from contextlib import ExitStack
    es = ExitStack()
    arena = es.enter_context(nc.sbuf_tensor("arena", [128, NW], F32))
    psb = [es.enter_context(nc.psum_tensor("ps%d" % i, [128, 512], F32)) for i in range(8)]
    sems = [es.enter_context(nc.semaphore("s%d" % i)) for i in range(48)]

    class Alloc:
        def __init__(self, lo, hi):
            self.p = lo
            self.hi = hi

        def f32(self, n, shape=None):
            w0 = self.p
            self.p += n
            assert self.p <= self.hi, ("sbuf overflow", self.p, self.hi)
            ap = arena[:, w0:w0 + n]
            if shape:
                ap = ap.rearrange(shape[0], **shape[1])
            return Buf(ap, "sb", w0 * 4, (w0 + n) * 4, 4)

        def bf(self, n, shape=None):
            nw = (n + 1) // 2
            w0 = self.p
            self.p += nw
            assert self.p <= self.hi, ("sbuf overflow", self.p, self.hi)
            ap = arena[:, w0:w0 + nw].bitcast(BF16)
            if shape:
                ap = ap.rearrange(shape[0], **shape[1])
            return Buf(ap, "sb", w0 * 4, (w0 + nw) * 4, 2)

    def psum(i, bf=False):
        ap = psb[i][:]
        if bf:
            return Buf(ap.bitcast(BF16), "ps%d" % i, 0, 2048, 2)
        return Buf(ap, "ps%d" % i, 0, 2048, 4)

    PS = [psum(i) for i in range(8)]
    PSB = [psum(i, True) for i in range(8)]

    A = Alloc(0, NW)
    hT = A.bf(8 * NT, ("p (k t) -> p k t", dict(k=8)))
    NSLOT = 5
    ring = [A.bf(2048) for _ in range(NSLOT)]
    ident = A.f32(128)
    identb = A.bf(128)
    iota_r = A.f32(128)
    iota_c = A.f32(1)
    cols = A.f32(NCOLS)
    smalls = A.f32(NSM)
    lg = A.f32(8)
    nlg = A.f32(8)
    lg128 = A.f32(8)
    lg127 = A.f32(8)
    cdec = A.f32(8)
    ck = A.f32(8 * 16)
    kd = A.f32(8)
    Dm = A.f32(4 * 128, ("p (h i) -> p h i", dict(h=4)))
    rowd = A.bf(8 * 128, ("p (g i) -> p g i", dict(g=8)))
    Acol = A.f32(8)
    Bcol = A.f32(8)
    scl = A.f32(20)
    scl2 = A.f32(20)
    fw = A.f32(40)
    lruout = A.f32(80)
    one_c = A.f32(1)
    eps_c = A.f32(1)
    tiny_c = A.f32(1)
    scb = A.bf(8 * 128, ("p (k m) -> p k m", dict(k=8)))
    tiny = A.f32(64)
    PERS_END = A.p

    flagsF = lambda n: smalls.ap[:, 8 + n:9 + n]
    flagsB = lambda n: smalls.ap[:, 24 + n:25 + n]
    convflag = smalls.ap[:, 40:41]
    lrukeep = smalls.ap[:, 41:42]

    wslot = [0]

    def next_slot():
        s = ring[wslot[0] % NSLOT]
        wslot[0] += 1
        return s

    def load_w(src_ap, shape_str, **kw):
        slot = next_slot()
        n = 1
        for v in src_ap.shape[1:]:
            n *= v
        view = slot.ap[:, 0:n].rearrange(shape_str, **kw)
        S.dma("pool", view, src_ap, [], [slot.reg(0, n)])
        return slot, view

    win_v = win_d.rearrange("(k p) c -> p k c", p=128)

    S.dma("sp", cols.ap, cols_d, [], [cols.reg()])
    S.dma("sp", smalls.ap, bass.AP(sm_d.tensor, 0, [[0, 128], [1, NSM]]), [], [smalls.reg()])
    S.dma("sp", ident.ap, consts_d[:, 0:128], [], [ident.reg()])
    S.dma("sp", iota_r.ap, consts_d[:, 128:256], [], [iota_r.reg()])
    S.dma("sp", iota_c.ap, consts_d[:, 256:257], [], [iota_c.reg()], allow_slow_non_contiguous=True)
    S.dve(lambda e: e.memset(one_c.ap, 1.0), [], [one_c.reg()])
    S.dve(lambda e: e.memset(eps_c.ap, EPS), [], [eps_c.reg()])
    S.dve(lambda e: e.memset(tiny_c.ap, 1e-18), [], [tiny_c.reg()])
    S.dve(lambda e: e.tensor_copy(out=identb.ap, in_=ident.ap), [ident.reg()], [identb.reg()])
    t0 = Buf(tiny.ap[:, 0:8], "sb", tiny.lo, tiny.lo + 32, 4)
    S.act(lambda e: e.activation(out=t0.ap, in_=smalls.ap[:, 0:8], func=AF.Exp, scale=-1.0), [smalls.reg()], [t0.reg()])
    S.act(lambda e: e.activation(out=t0.ap, in_=t0.ap, func=AF.Ln, bias=one_c.ap), [t0.reg(), one_c.reg()], [t0.reg()])
    S.dve(lambda e: e.tensor_scalar(out=lg.ap, in0=t0.ap, scalar1=-1.0, scalar2=None, op0=ALU.mult), [t0.reg()], [lg.reg()])
    S.dve(lambda e: e.tensor_copy(out=nlg.ap, in_=t0.ap), [t0.reg()], [nlg.reg()])
    S.dve(lambda e: e.tensor_scalar(out=lg128.ap, in0=lg.ap, scalar1=128.0, scalar2=None, op0=ALU.mult), [lg.reg()], [lg128.reg()])
    S.dve(lambda e: e.tensor_scalar(out=lg127.ap, in0=lg.ap, scalar1=127.0, scalar2=None, op0=ALU.mult), [lg.reg()], [lg127.reg()])
    S.act(lambda e: e.activation(out=cdec.ap, in_=lg128.ap, func=AF.Exp), [lg128.reg()], [cdec.reg()])
    for d in range(2):
        for h in range(4):
            g = d * 4 + h
            S.dve(lambda e, g=g, d=d: e.tensor_scalar(out=ck.ap[:, g * 16:(g + 1) * 16], in0=smalls.ap[:, 8 + 16 * d:24 + 16 * d],
                                                        scalar1=cdec.ap[:, g:g + 1], scalar2=None, op0=ALU.mult),
                  [smalls.reg(), cdec.reg()], [ck.reg(g * 16, (g + 1) * 16)])
    for h in range(4):
        S.act(lambda e, h=h: e.activation(out=kd.ap[:, h:h + 1], in_=iota_c.ap, func=AF.Exp, scale=nlg.ap[:, h:h + 1], bias=lg127.ap[:, h:h + 1]),
              [iota_c.reg(), nlg.reg(), lg127.reg()], [kd.reg(h, h + 1)])
        S.act(lambda e, h=h: e.activation(out=kd.ap[:, 4 + h:5 + h], in_=iota_c.ap, func=AF.Exp, scale=lg.ap[:, 4 + h:5 + h]),
              [iota_c.reg(), lg.reg()], [kd.reg(4 + h, 5 + h)])
    S.dve(lambda e: e.tensor_scalar(out=kd.ap, in0=kd.ap, scalar1=1.0 / 16.0, scalar2=None, op0=ALU.mult), [kd.reg()], [kd.reg()])
    P0 = Alloc(PERS_END, NW)
    rtmp = P0.f32(128)
    for h in range(4):
        S.act(lambda e, h=h: e.activation(out=rtmp.ap, in_=iota_r.ap, func=AF.Exp, scale=lg.ap[:, h:h + 1], bias=lg.ap[:, h:h + 1]),
              [iota_r.reg(), lg.reg()], [rtmp.reg()])
        S.dve(lambda e, h=h: e.tensor_copy(out=rowd.ap[:, h, :], in_=rtmp.ap), [rtmp.reg()], [rowd.reg(h * 128, (h + 1) * 128)])
        S.act(lambda e, h=h: e.activation(out=rtmp.ap, in_=iota_r.ap, func=AF.Exp, scale=nlg.ap[:, 4 + h:5 + h], bias=lg128.ap[:, 4 + h:5 + h]),
              [iota_r.reg(), nlg.reg(), lg128.reg()], [rtmp.reg()])
        S.dve(lambda e, h=h: e.tensor_copy(out=rowd.ap[:, 4 + h, :], in_=rtmp.ap), [rtmp.reg()], [rowd.reg((4 + h) * 128, (5 + h) * 128)])
    delta = P0.f32(128)
    dpos = P0.f32(128)
    dneg = P0.f32(128)
    mF = P0.f32(128)
    mB = P0.f32(128)
    e1 = P0.f32(128)
    e2 = P0.f32(128)
    S.dve(lambda e: e.tensor_scalar(out=delta.ap, in0=iota_r.ap, scalar1=iota_c.ap, scalar2=None, op0=ALU.subtract), [iota_r.reg(), iota_c.reg()], [delta.reg()])
    S.dve(lambda e: e.tensor_scalar(out=dpos.ap, in0=delta.ap, scalar1=0.0, scalar2=None, op0=ALU.max), [delta.reg()], [dpos.reg()])
    S.dve(lambda e: e.tensor_scalar(out=dneg.ap, in0=delta.ap, scalar1=-1.0, scalar2=0.0, op0=ALU.mult, op1=ALU.max), [delta.reg()], [dneg.reg()])
    S.dve(lambda e: e.tensor_scalar(out=mF.ap, in0=delta.ap, scalar1=0.0, scalar2=None, op0=ALU.is_ge), [delta.reg()], [mF.reg()])
    S.dve(lambda e: e.tensor_scalar(out=mB.ap, in0=delta.ap, scalar1=0.0, scalar2=None, op0=ALU.is_lt), [delta.reg()], [mB.reg()])
    for h in range(4):
        S.act(lambda e, h=h: e.activation(out=e1.ap, in_=dpos.ap, func=AF.Exp, scale=lg.ap[:, h:h + 1]), [dpos.reg(), lg.reg()], [e1.reg()])
        S.act(lambda e, h=h: e.activation(out=e2.ap, in_=dneg.ap, func=AF.Exp, scale=lg.ap[:, 4 + h:5 + h]), [dneg.reg(), lg.reg()], [e2.reg()])
        S.dve(lambda e: e.tensor_tensor(out=e1.ap, in0=e1.ap, in1=mF.ap, op=ALU.mult), [e1.reg(), mF.reg()], [e1.reg()])
        S.dve(lambda e: e.tensor_tensor(out=e2.ap, in0=e2.ap, in1=mB.ap, op=ALU.mult), [e2.reg(), mB.reg()], [e2.reg()])
        S.dve(lambda e, h=h: e.tensor_tensor(out=Dm.ap[:, h, :], in0=e1.ap, in1=e2.ap, op=ALU.add), [e1.reg(), e2.reg()], [Dm.reg(h * 128, (h + 1) * 128)])
    t1 = Buf(tiny.ap[:, 16:36], "sb", tiny.lo + 64, tiny.lo + 144, 4)
    S.act(lambda e: e.activation(out=t1.ap, in_=cols.ap[:, C_AP:C_AP + 20], func=AF.Exp, scale=-1.0), [cols.reg()], [t1.reg()])
    S.act(lambda e: e.activation(out=t1.ap, in_=t1.ap, func=AF.Ln, bias=one_c.ap), [t1.reg(), one_c.reg()], [t1.reg()])
    S.dve(lambda e: e.tensor_scalar(out=scl.ap, in0=t1.ap, scalar1=-8.0, scalar2=None, op0=ALU.mult), [t1.reg()], [scl.reg()])
    S.dve(lambda e: e.tensor_scalar(out=scl2.ap, in0=t1.ap, scalar1=-16.0, scalar2=None, op0=ALU.mult), [t1.reg()], [scl2.reg()])
    S.dve(lambda e: e.tensor_scalar(out=fw.ap, in0=cols.ap[:, C_CW:C_CW + 40], scalar1=convflag, scalar2=None, op0=ALU.mult), [cols.reg(), smalls.reg()], [fw.reg()])
    sc32 = Buf(tiny.ap[:, 40:48], "sb", tiny.lo + 160, tiny.lo + 192, 4)
    S.act(lambda e: e.activation(out=sc32.ap, in_=cols.ap[:, C_COND:C_COND + 8], func=AF.Silu), [cols.reg()], [sc32.reg()])
    S.dve(lambda e: e.tensor_copy(out=scb.ap, in_=bcast_ap(sc32.ap, [[1, 8], [0, 128]])), [sc32.reg()], [scb.reg()])

    wada_v = wada_d.rearrange("(k p) c -> p k c", p=128)
    modps = PS[7]
    for sl in range(8):
        slot, wv = load_w(wada_v[:, :, sl * 256:(sl + 1) * 256], "p (k c) -> p k c", k=8)
        for j in range(2):
            ft = sl * 2 + j
            S.mm([(modps.ap[:, ft:ft + 1], wv[:, k, j * 128:(j + 1) * 128], scb.ap[:, k, 0:1]) for k in range(8)],
                 [slot.reg(), scb.reg()], [modps.reg(ft, ft + 1)])
    S.dve(lambda e: e.tensor_tensor(out=Bcol.ap, in0=modps.ap[:, 0:8], in1=cols.ap[:, C_BSH:C_BSH + 8], op=ALU.add), [modps.reg(0, 16), cols.reg()], [Bcol.reg()])
    S.dve(lambda e: e.tensor_tensor(out=Acol.ap, in0=modps.ap[:, 8:16], in1=cols.ap[:, C_BSC:C_BSC + 8], op=ALU.add), [modps.reg(0, 16), cols.reg()], [Acol.reg()])
    S.dve(lambda e: e.scalar_tensor_tensor(out=Acol.ap, in0=Acol.ap, scalar=1.0, in1=cols.ap[:, C_NW:C_NW + 8], op0=ALU.add, op1=ALU.mult), [Acol.reg(), cols.reg()], [Acol.reg()])

    def checkpoint(k):
        if limit == k:
            S.enabled = False

    checkpoint(0)
    xs = [P0.f32(1024) for _ in range(2)]
    xn = [P0.f32(1024) for _ in range(2)]
    junk = P0.bf(1024)
    ssq = [P0.f32(1) for _ in range(2)]
    rstd = [P0.f32(1) for _ in range(2)]
    for n in range(NCH):
        b = n % 2
        S.dma("sp", xs[b].ap, x_d[n * 128:(n + 1) * 128, :], [], [xs[b].reg()])
        S.act(lambda e, b=b: e.activation(out=junk.ap, in_=xs[b].ap, func=AF.Square, accum_out=ssq[b].ap), [xs[b].reg()], [junk.reg(), ssq[b].reg()])
        S.act(lambda e, b=b: e.activation(out=rstd[b].ap, in_=ssq[b].ap, func=AF.Ln, scale=1.0 / D, bias=eps_c.ap), [ssq[b].reg(), eps_c.reg()], [rstd[b].reg()])
        S.act(lambda e, b=b: e.activation(out=rstd[b].ap, in_=rstd[b].ap, func=AF.Exp, scale=-0.5), [rstd[b].reg()], [rstd[b].reg()])
        S.act(lambda e, b=b: e.activation(out=xn[b].ap, in_=xs[b].ap, func=AF.Copy, scale=rstd[b].ap), [xs[b].reg(), rstd[b].reg()], [xn[b].reg()])
        if n == 0:
            checkpoint(10)
        if n == 1:
            checkpoint(13)
        for half in range(2):
            pb = PS[half]
            for j in range(4):
                k = half * 4 + j
                if n == 0 and half == 0 and j == 1:
                    checkpoint(11)
                S.pe(lambda e, k=k, j=j, pb=pb, b=b: e.transpose(pb.ap[:, j * 128:(j + 1) * 128], xn[b].ap[:, k * 128:(k + 1) * 128], ident.ap),
                     [xn[b].reg(k * 128, (k + 1) * 128), ident.reg()], [pb.reg(j * 128, (j + 1) * 128)])
            for j in range(4):
                k = half * 4 + j
                S.dve(lambda e, k=k, j=j, pb=pb, n=n: e.tensor_scalar(out=hT.ap[:, k, n * 128:(n + 1) * 128], in0=pb.ap[:, j * 128:(j + 1) * 128],
                                                                          scalar1=Acol.ap[:, k:k + 1], scalar2=Bcol.ap[:, k:k + 1], op0=ALU.mult, op1=ALU.add),
                      [pb.reg(j * 128, (j + 1) * 128), Acol.reg(), Bcol.reg()], [hT.reg(k * NT + n * 128, k * NT + (n + 1) * 128)])

        if n == 0:
            checkpoint(12)
    dump('hT', hT, 8 * NT, BF16)
    checkpoint(1)
    ipb = [0]

    def inproj_fm(col0, ncols, consumer, banks=(0, 1)):
        slot, wv = load_w(win_v[:, :, col0:col0 + ncols], "p (k c) -> p k c", k=8)

        def compute():
            for ct in range(ncols // 128):
                for tb in range(4):
                    pb = PS[banks[ipb[0] % len(banks)]]
                    ipb[0] += 1
                    S.mm([(pb.ap, wv[:, k, ct * 128:(ct + 1) * 128], hT.ap[:, k, tb * 512:(tb + 1) * 512]) for k in range(8)],
                         [slot.reg(), hT.reg()], [pb.reg()])
                    consumer(ct, tb, pb)
        return compute

    P1 = Alloc(PERS_END, NW)
    oT = P1.bf(16 * NT, ("p (k t) -> p k t", dict(k=16)))
    qT = P1.bf(2 * NT, ("p (k t) -> p k t", dict(k=2)))
    kT = P1.bf(2 * NT, ("p (k t) -> p k t", dict(k=2)))
    vt = P1.bf(NCH * 512, ("p (n e) -> p n e", dict(n=NCH)))
    U = [[P1.f32(1024, ("p (k e) -> p k e", dict(k=2))) for _ in range(2)] for _ in range(2)]
    SFb = [P1.bf(1024, ("p (k e) -> p k e", dict(k=2))) for _ in range(2)]
    TBw = [P1.bf(1024, ("p (k e) -> p k e", dict(k=2))) for _ in range(2)]
    TBr = [P1.bf(1024, ("p (k e) -> p k e", dict(k=2))) for _ in range(2)]
    kpr = [P1.bf(256) for _ in range(2)]
    Pm = [P1.bf(128) for _ in range(2)]
    qF = [P1.bf(256, ("p (k i) -> p k i", dict(k=2))) for _ in range(2)]
    qB = [P1.bf(256, ("p (k i) -> p k i", dict(k=2))) for _ in range(2)]
    og = [P1.bf(512) for _ in range(2)]
    junk1 = P1.bf(512)
    ss1 = [P1.f32(1) for _ in range(2)]
    rs1 = [P1.f32(1) for _ in range(2)]
    sgt = [P1.bf(512) for _ in range(2)]
    grow = P1.f32(1024)
    growb = P1.f32(1024)
    P1_END = P1.p

    tb_v = tb_d.rearrange("(h n p) (k e) -> h n p k e", h=4, n=16, k=2)

    def tb_reg(h, n):
        base = ((h * 16 + n) * 128) * 1024 * 2
        return ("scr", base, base + 128 * 1024 * 2)

    actdve = [0]

    def evac_copy(out_ap, out_reg, pb, scale=None):
        if actdve[0] % 2 == 0:
            if scale is None:
                S.act(lambda e: e.activation(out=out_ap, in_=pb.ap, func=AF.Copy), [pb.reg()], [out_reg])
            else:
                S.act(lambda e: e.activation(out=out_ap, in_=pb.ap, func=AF.Copy, scale=scale), [pb.reg()], [out_reg])
        else:
            if scale is None:
                S.dve(lambda e: e.tensor_copy(out=out_ap, in_=pb.ap), [pb.reg()], [out_reg])
            else:
                S.dve(lambda e: e.tensor_scalar(out=out_ap, in0=pb.ap, scalar1=scale, scalar2=None, op0=ALU.mult), [pb.reg()], [out_reg])
        actdve[0] += 1

    def gate_row_job():
        S.dma("sp", growb.ap[0:1, :], rows_d[0:1, :], [], [growb.reg()])
        for s_ in range(4):
            slot, wv = load_w(wada_v[:, :, 2048 + s_ * 256:2048 + (s_ + 1) * 256], "p (k c) -> p k c", k=8)
            pbg = PS[4 + s_ % 2]
            S.mm([(pbg.ap[:, 0:256], scb.ap[:, k, :], wv[:, k, :]) for k in range(8)], [slot.reg(), scb.reg()], [pbg.reg(0, 256)])
            S.dve(lambda e, s_=s_, pbg=pbg: e.tensor_tensor(out=grow.ap[0:1, s_ * 256:(s_ + 1) * 256], in0=pbg.ap[0:1, 0:256], in1=growb.ap[0:1, s_ * 256:(s_ + 1) * 256], op=ALU.add),
                  [pbg.reg(0, 256), growb.reg()], [grow.reg(s_ * 256, (s_ + 1) * 256)])
        S.dma("sp", gate_d, grow.ap[0:1, :], [grow.reg()], [("scr2", 0, 4096)])

    gate_ctr = [0]

    def make_gate_jobs(hg, banks):
        slabs = [load_w(win_v[:, :, OG + hg * 512 + s_ * 256:OG + hg * 512 + (s_ + 1) * 256], "p (k c) -> p k c", k=8) for s_ in range(2)]
        jobs = []
        for s_ in range(2):
            slot, wv = slabs[s_]
            for ct in range(2):
                for tb in range(4):
                    def job(slot=slot, wv=wv, s_=s_, ct=ct, tb=tb):
                        i_ = gate_ctr[0]
                        gate_ctr[0] += 1
                        b2 = i_ % 2
                        pb = PS[banks[i_ % len(banks)]]
                        S.mm([(pb.ap, wv[:, k, ct * 128:(ct + 1) * 128], hT.ap[:, k, tb * 512:(tb + 1) * 512]) for k in range(8)],
                             [slot.reg(), hT.reg()], [pb.reg()])
                        S.act(lambda e: e.activation(out=sgt[b2].ap, in_=pb.ap, func=AF.Silu), [pb.reg()], [sgt[b2].reg()])
                        k_ = hg * 4 + s_ * 2 + ct
                        rg = oT.reg(k_ * NT + tb * 512, k_ * NT + (tb + 1) * 512)
                        S.pool(lambda e: e.tensor_tensor(out=oT.ap[:, k_, tb * 512:(tb + 1) * 512], in0=oT.ap[:, k_, tb * 512:(tb + 1) * 512], in1=sgt[b2].ap, op=ALU.mult),
                               [rg, sgt[b2].reg()], [rg])
                    jobs.append(job)
        return jobs

    for h in range(4):
        gF, gB = h, 4 + h
        def q_cons(ct, tb, pb):
            evac_copy(qT.ap[:, ct, tb * 512:(tb + 1) * 512], qT.reg(ct * NT + tb * 512, ct * NT + (tb + 1) * 512), pb)

        def k_cons(ct, tb, pb):
            evac_copy(kT.ap[:, ct, tb * 512:(tb + 1) * 512], kT.reg(ct * NT + tb * 512, ct * NT + (tb + 1) * 512), pb)
        cq = inproj_fm(OQ + h * 256, 256, q_cons)
        ckk = inproj_fm(OK_ + h * 256, 256, k_cons)
        vsl = [load_w(win_v[:, :, OV + h * 512 + s * 256: OV + h * 512 + (s + 1) * 256], "p (k c) -> p k c", k=8) for s in range(2)]
        cq()
        ckk()
        for n in range(NCH):
            pb = PS[n % 2]
            for s in range(2):
                slot, wv = vsl[s]
                S.mm([(pb.ap[:, s * 256:(s + 1) * 256], hT.ap[:, k, n * 128:(n + 1) * 128], wv[:, k, :]) for k in range(8)],
                     [slot.reg(), hT.reg()], [pb.reg(s * 256, (s + 1) * 256)])
            evac_copy(vt.ap[:, n, :], vt.reg(n * 512, (n + 1) * 512), pb)
        if h == 0:
            dump('qT', qT, 2 * NT, BF16)
            dump('kT', kT, 2 * NT, BF16)
            dump('vt', vt, NCH * 512, BF16)
            checkpoint(2)
        for d in range(2):
            S.dma("sp", U[d][0].ap, iret_d[d, h].rearrange("(k p) e -> p k e", p=128), [], [U[d][0].reg()])
        def T_(n, single=False):
            kps = PSB[0] if single else PSB[n % 2]
            for dtl in range(2):
                S.pe(lambda e, dtl=dtl, n=n, kps=kps: e.transpose(kps.ap[:, dtl * 128:(dtl + 1) * 128], kT.ap[:, dtl, n * 128:(n + 1) * 128], identb.ap),
                     [kT.reg(dtl * NT + n * 128, dtl * NT + (n + 1) * 128), identb.reg()], [kps.reg(dtl * 128, (dtl + 1) * 128)])

        def KPR_(n, g, single=False):
            kps = PSB[0] if single else PSB[n % 2]
            b = n % 2
            S.act(lambda e: e.activation(out=kpr[b].ap, in_=kps.ap[:, 0:256], func=AF.Copy, scale=kd.ap[:, g:g + 1]),
                  [kps.reg(0, 256), kd.reg()], [kpr[b].reg()])

        def KV_(n):
            b = n % 2
            for dtl in range(2):
                pb = PS[6 + dtl]
                S.mm([(pb.ap, kpr[b].ap[:, dtl * 128:(dtl + 1) * 128], vt.ap[:, n, :])], [kpr[b].reg(), vt.reg(n * 512, (n + 1) * 512)], [pb.reg()])

        def UPD_(n, d, g, cur):
            nxt = 1 - cur
            for dtl in range(2):
                pb = PS[6 + dtl]
                S.dve(lambda e, dtl=dtl, pb=pb: e.scalar_tensor_tensor(
                    out=U[d][nxt].ap[:, dtl, :], in0=U[d][cur].ap[:, dtl, :], scalar=ck.ap[:, g * 16 + n:g * 16 + n + 1], in1=pb.ap, op0=ALU.mult, op1=ALU.add),
                    [U[d][cur].reg(dtl * 512, (dtl + 1) * 512), ck.reg(), pb.reg()], [U[d][nxt].reg(dtl * 512, (dtl + 1) * 512)])
            return nxt

        def TBW_(n, cur):
            b = n % 2
            S.act(lambda e: e.activation(out=TBw[b].ap, in_=U[1][cur].ap, func=AF.Copy, scale=flagsB(n)),
                  [U[1][cur].reg(), smalls.reg()], [TBw[b].reg()])
            S.dma("sp", tb_v[h, n], TBw[b].ap, [TBw[b].reg()], [tb_reg(h, n)])

        cur = 0
        gjobs = make_gate_jobs(h - 1, (3,)) if h >= 1 else []
        if h == 0:
            gate_row_job()
        T_(NCH - 1)
        KPR_(NCH - 1, gB)
        TBW_(NCH - 1, cur)
        for n in range(NCH - 1, -1, -1):
            if n >= 1:
                T_(n - 1)
            KV_(n)
            if n >= 1:
                KPR_(n - 1, gB)
            cur = UPD_(n, 1, gB, cur)
            if n >= 1:
                TBW_(n - 1, cur)
            if gjobs:
                gjobs.pop(0)()
            if n in (0, 2, 4, 6):
                S.dma("sp", sret_d[n // 2, 1, h].rearrange("(k p) e -> p k e", p=128), U[1][cur].ap, [U[1][cur].reg()], [])
        if h == 0:
            checkpoint(3)

        def S_(n):
            sps = PS[2]
            S.mm([(sps.ap[:, 0:128], kT.ap[:, dtl, n * 128:(n + 1) * 128], qT.ap[:, dtl, n * 128:(n + 1) * 128]) for dtl in range(2)],
                 [kT.reg(), qT.reg()], [sps.reg(0, 128)])

        def P_(n):
            sps = PS[2]
            b = n % 2
            S.dve(lambda e, h=h: e.scalar_tensor_tensor(out=Pm[b].ap, in0=sps.ap[:, 0:128], scalar=1.0 / 16.0, in1=Dm.ap[:, h, :], op0=ALU.mult, op1=ALU.mult),
                  [sps.reg(0, 128), Dm.reg()], [Pm[b].reg()])

        def Q_(n):
            b = n % 2
            S.pool(lambda e, gF=gF: e.tensor_tensor(out=qF[b].ap, in0=qT.ap[:, :, n * 128:(n + 1) * 128],
                                             in1=bcast_ap(rowd.ap[:, gF, :], [[0, 2], [1, 128]]), op=ALU.mult),
                   [qT.reg(), rowd.reg()], [qF[b].reg()])
            S.pool(lambda e, gB=gB: e.tensor_tensor(out=qB[b].ap, in0=qT.ap[:, :, n * 128:(n + 1) * 128],
                                             in1=bcast_ap(rowd.ap[:, gB, :], [[0, 2], [1, 128]]), op=ALU.mult),
                   [qT.reg(), rowd.reg()], [qB[b].reg()])

        def SF_(n, cur):
            b = n % 2
            S.dve(lambda e: e.tensor_scalar(out=SFb[b].ap, in0=U[0][cur].ap, scalar1=flagsF(n), scalar2=None, op0=ALU.mult),
                  [U[0][cur].reg(), smalls.reg()], [SFb[b].reg()])

        def O_(n):
            b = n % 2
            ops_ = PS[4 + b]
            mms = [(ops_.ap, Pm[b].ap, vt.ap[:, n, :])]
            mms += [(ops_.ap, qB[b].ap[:, dtl, :], TBr[b].ap[:, dtl, :]) for dtl in range(2)]
            mms += [(ops_.ap, qF[b].ap[:, dtl, :], SFb[b].ap[:, dtl, :]) for dtl in range(2)]
            S.mm(mms, [Pm[b].reg(), vt.reg(n * 512, (n + 1) * 512), qF[b].reg(), qB[b].reg(), SFb[b].reg(), TBr[b].reg()], [ops_.reg()])

        def NORM_(n):
            b = n % 2
            ops_ = PS[4 + b]
            S.act(lambda e: e.activation(out=junk1.ap, in_=ops_.ap, func=AF.Square, accum_out=ss1[b].ap), [ops_.reg()], [junk1.reg(), ss1[b].reg()])
            S.act(lambda e: e.activation(out=rs1[b].ap, in_=ss1[b].ap, func=AF.Ln, scale=1.0 / 512.0, bias=eps_c.ap), [ss1[b].reg(), eps_c.reg()], [rs1[b].reg()])
            S.act(lambda e: e.activation(out=rs1[b].ap, in_=rs1[b].ap, func=AF.Exp, scale=-0.5), [rs1[b].reg()], [rs1[b].reg()])
            S.act(lambda e: e.activation(out=og[b].ap, in_=ops_.ap, func=AF.Copy, scale=rs1[b].ap), [ops_.reg(), rs1[b].reg()], [og[b].reg()])

        def OGT_(n):
            b = n % 2
            tps = PSB[1]
            for et in range(4):
                S.pe(lambda e, et=et: e.transpose(tps.ap[:, et * 128:(et + 1) * 128], og[b].ap[:, et * 128:(et + 1) * 128], identb.ap),
                     [og[b].reg(et * 128, (et + 1) * 128), identb.reg()], [tps.reg(et * 128, (et + 1) * 128)])

        def OTE_(n):
            tps = PSB[1]
            S.dve(lambda e, h=h: e.tensor_tensor(out=oT.ap[:, h * 4:(h + 1) * 4, n * 128:(n + 1) * 128],
                                            in0=tps.ap[:, 0:512].rearrange("p (a t) -> p a t", a=4),
                                            in1=bcast_ap(cols.ap[:, C_GN + h * 4:C_GN + h * 4 + 4], [[1, 4], [0, 128]]), op=ALU.mult),
                  [tps.reg(0, 512), cols.reg()], [oT.reg((h * 4) * NT, (h * 4 + 4) * NT)])

        cur = 0
        S.dma("sp", TBr[0].ap, tb_v[h, 0], [tb_reg(h, 0)], [TBr[0].reg()])
        T_(0, True)
        KPR_(0, gF, True)
        S_(0)
        P_(0)
        Q_(0)
        SF_(0, cur)
        for n in range(NCH):
            last = (n + 1 == NCH)
            if not last:
                S.dma("sp", TBr[(n + 1) % 2].ap, tb_v[h, n + 1], [tb_reg(h, n + 1)], [TBr[(n + 1) % 2].reg()])
                T_(n + 1, True)
                S_(n + 1)
            KV_(n)
            O_(n)
            if n >= 1:
                OGT_(n - 1)
            cur = UPD_(n, 0, gF, cur)
            if n in (1, 3, 5, 7):
                S.dma("sp", sret_d[n // 2, 0, h].rearrange("(k p) e -> p k e", p=128), U[0][cur].ap, [U[0][cur].reg()], [])
            if not last:
                SF_(n + 1, cur)
                KPR_(n + 1, gF, True)
                P_(n + 1)
                Q_(n + 1)
            if n >= 1:
                OTE_(n - 1)
            NORM_(n)
        OGT_(NCH - 1)
        OTE_(NCH - 1)
        if h == 0:
            dump('ss1a', ss1[0], 1, F32)
            dump('ss1b', ss1[1], 1, F32)
            dump('rs1a', rs1[0], 1, F32)
            dump('rs1b', rs1[1], 1, F32)
            dump('og0', og[0], 512, BF16)
            dump('og1', og[1], 512, BF16)
            dump('oT0', oT, 4 * NT, BF16)
            checkpoint(4)
        if h == 3:
            for j_ in make_gate_jobs(3, (2, 3)):
                j_()

    dump('oT', oT, 16 * NT, BF16)
    checkpoint(5)
    P1b = Alloc(PERS_END + 16 * NT // 2, NW)
    PT = P1b.bf(8 * NT, ("p (k t) -> p k t", dict(k=8)))
    smr = [P1b.bf(512) for _ in range(2)]
    wrd_v = wrd_d.rearrange("(k p) c -> p k c", p=128)
    def rd_loads(ct):
        return (load_w(win_v[:, :, OMR + ct * 128:OMR + (ct + 1) * 128], "p (k c) -> p k c", k=8),
                load_w(wrd_v[:, :, ct * 128:(ct + 1) * 128], "p (k c) -> p k c", k=16))
    rd_next = rd_loads(0)
    for ct in range(8):
        (slot_m, wm), (slot_r, wr) = rd_next
        if ct + 1 < 8:
            rd_next = rd_loads(ct + 1)
        for tb in range(4):
            b2 = tb % 2
            pm = PS[b2]
            S.mm([(pm.ap, wm[:, k, :], hT.ap[:, k, tb * 512:(tb + 1) * 512]) for k in range(8)], [slot_m.reg(), hT.reg()], [pm.reg()])
            S.act(lambda e, b2=b2, pm=pm: e.activation(out=smr[b2].ap, in_=pm.ap, func=AF.Sigmoid), [pm.reg()], [smr[b2].reg()])
            pr = PS[2 + b2]
            S.mm([(pr.ap, wr[:, k, :], oT.ap[:, k, tb * 512:(tb + 1) * 512]) for k in range(16)], [slot_r.reg(), oT.reg()], [pr.reg()])
            S.dve(lambda e, b2=b2, pr=pr, ct=ct, tb=tb: e.tensor_tensor(out=PT.ap[:, ct, tb * 512:(tb + 1) * 512], in0=pr.ap, in1=smr[b2].ap, op=ALU.mult),
                  [pr.reg(), smr[b2].reg()], [PT.reg(ct * NT + tb * 512, ct * NT + (tb + 1) * 512)])

    dump('PTret', PT, 8 * NT, BF16)
    checkpoint(6)
    P2 = Alloc(PERS_END, PERS_END + 16 * NT // 2)
    yT = P2.bf(10 * NT, ("p (k t) -> p k t", dict(k=10)))
    hh = [P2.f32(NT) for _ in range(2)]
    sgq = [P2.bf(512) for _ in range(4)]
    xq_p2 = [P2.f32(512) for _ in range(2)]
    P2c = Alloc(P1b.p, NW)
    xq_a = [P2c.f32(512) for _ in range(4)]
    xcb0 = P2c.bf(NT)
    tgd = [[P2c.f32(NT // 2) for _ in range(2)] for _ in range(2)]
    aad = [[P2c.f32(NT // 2) for _ in range(2)] for _ in range(2)]
    hcol = P2c.f32(60)
    onep = P2c.f32(1)
    lnhalf = P2c.f32(1)
    xq_c = P2c.f32(512)
    S.dve(lambda e: e.tensor_scalar(out=hcol.ap[:, 0:40], in0=cols.ap[:, C_BA:C_BA + 40], scalar1=0.5, scalar2=None, op0=ALU.mult), [cols.reg()], [hcol.reg(0, 40)])
    S.dve(lambda e: e.tensor_scalar(out=hcol.ap[:, 40:60], in0=scl.ap, scalar1=0.5, scalar2=None, op0=ALU.mult), [scl.reg()], [hcol.reg(40, 60)])
    S.dve(lambda e: e.memset(onep.ap, 0.25), [], [onep.reg()])
    S.dve(lambda e: e.memset(lnhalf.ap, -0.6931471805599453), [], [lnhalf.reg()])

    def q3(buf, tb, c0, c1):
        ap = buf.ap
        return bass.AP(ap.tensor, ap.offset + tb * 512 + c0, [list(ap.ap[0]), [64, 8], [1, c1 - c0]])

    def qseq(buf, tb, r0, nr, c):
        ap = buf.ap
        return bass.AP(ap.tensor, ap.offset + tb * 512 + r0 * 64 + c, [list(ap.ap[0]), [256, 2], [64, nr]])

    def rev(ap2d, n):
        return bass.AP(ap2d.tensor, ap2d.offset + n - 1, [list(ap2d.ap[0]), [-1, n]])

    LW = Alloc(ring[0].lo // 4, ring[NSLOT - 1].hi // 4)
    lwsets = [(LW.bf(1024), LW.bf(256), LW.bf(256), LW.bf(1024)) for _ in range(2)]
    xcbs = [xcb0, xcb0]
    trd = [[t_, t_] for t_ in (LW.f32(NT // 2), LW.f32(NT // 2))]
    xq_l = LW.f32(512)
    xcq = [xq_a, [xq_p2[0], xq_p2[1], xq_c, xq_l]]

    def lru_loads(cb):
        bx_, ba_, bw_, bg_ = lwsets[cb % 2]
        vx = bx_.ap.rearrange("p (k c) -> p k c", k=8)
        va = ba_.ap.rearrange("p (d j) -> p d j", d=2)
        vw = bw_.ap.rearrange("p (d j) -> p d j", d=2)
        vg = bg_.ap.rearrange("p (k c) -> p k c", k=8)
        S.dma("pool", vx, win_v[:, :, OXL + cb * 128:OXL + (cb + 1) * 128], [], [bx_.reg()])
        S.dma("pool", va, wa_d[:, cb].rearrange("d i j -> i d j"), [], [ba_.reg()])
        S.dma("pool", vw, wx_d[:, cb].rearrange("d i j -> i d j"), [], [bw_.reg()])
        S.dma("pool", vg, win_v[:, :, OGL + cb * 128:OGL + (cb + 1) * 128], [], [bg_.reg()])
        return (bx_, vx), (ba_, va), (bw_, vw), (bg_, vg)

    XB = [PS[0], PS[1], PS[6], PS[7]]

    def lru_front_pe(cb, wts):
        (slot_x, wxl) = wts[0]
        for tb in range(4):
            pb = XB[tb]
            S.mm([(pb.ap, wxl[:, k, :], hT.ap[:, k, tb * 512:(tb + 1) * 512]) for k in range(8)], [slot_x.reg(), hT.reg()], [pb.reg()])

    def lru_front_conv(cb, tbs=(0, 1, 2, 3)):
        xq = xcq[cb % 2]
        cw = lambda j: cols.ap[:, C_CW + cb * 4 + j:C_CW + cb * 4 + j + 1]
        fwc = lambda j: fw.ap[:, cb * 4 + j:cb * 4 + j + 1]
        cbias = cols.ap[:, C_CB + cb:C_CB + cb + 1]
        for tb in tbs:
            pb = XB[tb]
            xc = xq[tb]
            xr = xc.reg()
            w2, w1, w0, w3 = cw(2), cw(1), cw(0), cw(3)
            S.act(lambda e, pb=pb, xc=xc, w2=w2: e.activation(out=xc.ap, in_=pb.ap, func=AF.Identity, scale=w2, bias=cbias),
                  [pb.reg(), cols.reg()], [xr])
            for (wj, o0, o1, i0, i1) in ((w1, 1, 64, 0, 63), (w0, 2, 64, 0, 62), (w3, 0, 63, 1, 64)):
                S.dve(lambda e, pb=pb, xc=xc, wj=wj, o0=o0, o1=o1, i0=i0, i1=i1: e.scalar_tensor_tensor(
                    out=q3(xc, 0, o0, o1), in0=q3(pb, 0, i0, i1), scalar=wj, in1=q3(xc, 0, o0, o1), op0=ALU.mult, op1=ALU.add),
                    [pb.reg(), xr, cols.reg()], [xr])
            f1, f0, f3 = fwc(1), fwc(0), fwc(3)
            for (fj, oc, ic, orow, irow) in ((f1, 0, 63, 1, 0), (f0, 0, 62, 1, 0), (f0, 1, 63, 1, 0), (f3, 63, 0, 0, 1)):
                S.dve(lambda e, pb=pb, xc=xc, fj=fj, oc=oc, ic=ic, orow=orow, irow=irow: e.scalar_tensor_tensor(
                    out=qseq(xc, 0, orow, 3, oc), in0=qseq(pb, 0, irow, 3, ic), scalar=fj, in1=qseq(xc, 0, orow, 3, oc), op0=ALU.mult, op1=ALU.add),
                    [pb.reg(), xr, fw.reg()], [xr])

    def lru_casts(cb):
        xq = xcq[cb % 2]
        xcb = xcbs[cb % 2]
        for tb in range(4):
            S.act(lambda e, tb=tb: e.activation(out=xcb.ap[:, tb * 512:(tb + 1) * 512], in_=xq[tb].ap, func=AF.Copy),
                  [xq[tb].reg()], [xcb.reg(tb * 512, (tb + 1) * 512)])

    def lru_step(cb, st, wts, part):
        (slot_a, wav), (slot_w, wxv) = wts[1], wts[2]
        xq = xcq[cb % 2]
        xcb = xcbs[cb % 2]
        items = []
        for d in range(2):
            half = st if d == 0 else 1 - st
            qs = [2 * half, 2 * half + 1] if d == 0 else [2 * half + 1, 2 * half]
            items.append((d, half, qs))
        col = lambda base, d: hcol.ap[:, base + d * 10 + cb:base + d * 10 + cb + 1]
        lr = lambda b_, half, tb: b_.reg((tb - 2 * half) * 512, (tb - 2 * half + 1) * 512)
        la = lambda b_, half, tb: b_.ap[:, (tb - 2 * half) * 512:(tb - 2 * half + 1) * 512]
        gq = lambda b_, tb: b_.ap[:, tb * 512:(tb + 1) * 512]
        gr = lambda b_, tb: b_.reg(tb * 512, (tb + 1) * 512)
        pi = [0]
        doA = (part == 'A')
        for (d, half, qs) in (items if doA else []):
            for tb in qs:
                pa = PS[2 + pi[0] % 2]
                pg = PS[4 + pi[0] % 2]
                pi[0] += 1
                S.mm([(pa.ap, wav[:, d, :], gq(xcb, tb))], [slot_a.reg(), gr(xcb, tb)], [pa.reg()])
                S.act(lambda e, pa=pa, tb=tb, d=d, half=half: e.activation(out=la(trd[d][st], half, tb), in_=pa.ap, func=AF.Tanh, scale=0.5, bias=col(0, d)),
                      [pa.reg(), hcol.reg()], [lr(trd[d][st], half, tb)])
                S.mm([(pg.ap, wxv[:, d, :], gq(xcb, tb))], [slot_w.reg(), gr(xcb, tb)], [pg.reg()])
                S.act(lambda e, pg=pg, tb=tb, d=d, half=half: e.activation(out=la(tgd[d][st], half, tb), in_=pg.ap, func=AF.Tanh, scale=0.5, bias=col(20, d)),
                      [pg.reg(), hcol.reg()], [lr(tgd[d][st], half, tb)])
        for (d, half, qs) in (items if doA else []):
            for tb in qs:
                S.dve(lambda e, tb=tb, d=d, half=half: e.scalar_tensor_tensor(out=la(tgd[d][st], half, tb), in0=la(tgd[d][st], half, tb), scalar=1.0, in1=xq[tb].ap, op0=ALU.add, op1=ALU.mult),
                      [lr(tgd[d][st], half, tb), xq[tb].reg()], [lr(tgd[d][st], half, tb)])
        for (d, half, qs) in (items if doA else []):
            for tb in qs:
                S.act(lambda e, tb=tb, d=d, half=half: e.activation(out=la(aad[d][st], half, tb), in_=la(trd[d][st], half, tb), func=AF.Exp, scale=col(40, d), bias=col(40, d)),
                      [lr(trd[d][st], half, tb), hcol.reg()], [lr(aad[d][st], half, tb)])
        for (d, half, qs) in (items if doA else []):
            for tb in qs:
                S.dve(lambda e, tb=tb, d=d, half=half: e.scalar_tensor_tensor(out=la(trd[d][st], half, tb), in0=la(aad[d][st], half, tb), scalar=0.9999995, in1=la(aad[d][st], half, tb), op0=ALU.min, op1=ALU.mult),
                      [lr(aad[d][st], half, tb)], [lr(trd[d][st], half, tb)])
        if part == 'A':
            return
        for (d, half, qs) in items:
            for tb in qs:
                S.act(lambda e, tb=tb, d=d, half=half: e.activation(out=la(trd[d][st], half, tb), in_=la(trd[d][st], half, tb), func=AF.Sqrt, scale=-0.25, bias=onep.ap),
                      [lr(trd[d][st], half, tb), onep.reg()], [lr(trd[d][st], half, tb)])
        for (d, half, qs) in items:
            ap = aad[d][st].ap
            if d == 0:
                off, cnt = (256, 3) if half == 0 else (0, 4)
            else:
                off, cnt = (255, 4) if half == 0 else (255, 3)
            bv = bass.AP(ap.tensor, ap.offset + off, [list(ap.ap[0]), [256, cnt]])
            S.dve(lambda e, bv=bv: e.tensor_scalar(out=bv, in0=bv, scalar1=lrukeep, scalar2=None, op0=ALU.mult), [aad[d][st].reg(), smalls.reg()], [aad[d][st].reg()])
        for i_ in range(2):
            for (d, half, qs) in items:
                tb = qs[i_]
                h0 = cols.ap[:, C_H0 + d * 10 + cb:C_H0 + d * 10 + cb + 1]
                S.dve(lambda e, tb=tb, d=d, half=half: e.tensor_tensor(out=la(tgd[d][st], half, tb), in0=la(tgd[d][st], half, tb), in1=la(trd[d][st], half, tb), op=ALU.mult),
                      [lr(tgd[d][st], half, tb), lr(trd[d][st], half, tb)], [lr(tgd[d][st], half, tb)])
                first = (d == 0 and tb == 0) or (d == 1 and tb == 3)
                if first:
                    init, ireg = h0, cols.reg()
                elif d == 0:
                    init, ireg = hh[0].ap[:, tb * 512 - 1:tb * 512], hh[0].reg(tb * 512 - 1, tb * 512)
                else:
                    init, ireg = hh[1].ap[:, (tb + 1) * 512:(tb + 1) * 512 + 1], hh[1].reg((tb + 1) * 512, (tb + 1) * 512 + 1)
                if d == 0:
                    S.dve(lambda e, tb=tb, init=init, half=half: e.tensor_tensor_scan(out=gq(hh[0], tb), data0=la(aad[0][st], half, tb), data1=la(tgd[0][st], half, tb), initial=init, op0=ALU.mult, op1=ALU.add),
                          [lr(aad[0][st], half, tb), lr(tgd[0][st], half, tb), ireg], [gr(hh[0], tb)])
                else:
                    S.dve(lambda e, tb=tb, init=init, half=half: e.tensor_tensor_scan(out=rev(gq(hh[1], tb), 512), data0=rev(la(aad[1][st], half, tb), 512), data1=rev(la(tgd[1][st], half, tb), 512), initial=init, op0=ALU.mult, op1=ALU.add),
                          [lr(aad[1][st], half, tb), lr(tgd[1][st], half, tb), ireg], [gr(hh[1], tb)])

    def lru_back(cb, wts):
        (slot_g, wgl) = wts[3]
        fo = cb * 8
        S.pool(lambda e: e.tensor_copy(out=lruout.ap[:, fo:fo + 4], in_=bass.AP(hh[0].ap.tensor, hh[0].ap.offset + 255, [list(hh[0].ap.ap[0]), [256, 4]])),
               [hh[0].reg()], [lruout.reg(fo, fo + 4)])
        S.pool(lambda e: e.tensor_copy(out=lruout.ap[:, fo + 4:fo + 8], in_=bass.AP(hh[1].ap.tensor, hh[1].ap.offset, [list(hh[1].ap.ap[0]), [256, 4]])),
               [hh[1].reg()], [lruout.reg(fo + 4, fo + 8)])
        for tb in range(4):
            b2 = tb % 2
            pb = PS[2 + b2]
            S.mm([(pb.ap, wgl[:, k, :], hT.ap[:, k, tb * 512:(tb + 1) * 512]) for k in range(8)], [slot_g.reg(), hT.reg()], [pb.reg()])
            S.act(lambda e, pb=pb, tb=tb: e.activation(out=sgq[tb].ap, in_=pb.ap, func=AF.Silu), [pb.reg()], [sgq[tb].reg()])
            S.pool(lambda e, tb=tb: e.tensor_tensor(out=hh[0].ap[:, tb * 512:(tb + 1) * 512], in0=hh[0].ap[:, tb * 512:(tb + 1) * 512], in1=hh[1].ap[:, tb * 512:(tb + 1) * 512], op=ALU.add),
                   [hh[0].reg(tb * 512, (tb + 1) * 512), hh[1].reg(tb * 512, (tb + 1) * 512)], [hh[0].reg(tb * 512, (tb + 1) * 512)])
            S.pool(lambda e, tb=tb: e.tensor_tensor(out=yT.ap[:, cb, tb * 512:(tb + 1) * 512], in0=hh[0].ap[:, tb * 512:(tb + 1) * 512], in1=sgq[tb].ap, op=ALU.mult),
                  [hh[0].reg(tb * 512, (tb + 1) * 512), sgq[tb].reg()], [yT.reg(cb * NT + tb * 512, cb * NT + (tb + 1) * 512)])

    wts_cur = lru_loads(0)
    lru_front_pe(0, wts_cur)
    lru_front_conv(0)
    lru_casts(0)
    for cb in range(10):
        nxt = cb + 1 < 10
        lru_step(cb, 0, wts_cur, 'A')
        wts_nxt = None
        if nxt:
            wts_nxt = lru_loads(cb + 1)
            lru_front_pe(cb + 1, wts_nxt)
            lru_front_conv(cb + 1, (0,))
        lru_step(cb, 0, wts_cur, 'B')
        if nxt:
            lru_front_conv(cb + 1, (1,))
        lru_step(cb, 1, wts_cur, 'A')
        if nxt:
            lru_front_conv(cb + 1, (2,))
        lru_step(cb, 1, wts_cur, 'B')
        if nxt:
            lru_front_conv(cb + 1, (3,))
        lru_back(cb, wts_cur)
        if cb + 1 < 10:
            lru_casts(cb + 1)
        if cb == 0:
            dump('hf0', hh[0], NT, F32)
            dump('hb0', hh[1], NT, F32)
        wts_cur = wts_nxt
    S.dma("sp", slru_d, lruout.ap, [lruout.reg()], [])

    dump('yT', yT, 10 * NT, BF16)
    checkpoint(7)
    wld_v = wld_d.rearrange("(k p) c -> p k c", p=128)
    P2d = Alloc(P1b.p, NW)
    sml = [P2d.f32(512) for _ in range(2)]
    tml = [P2d.f32(512) for _ in range(2)]
    def ld_loads(ct):
        return (load_w(win_v[:, :, OML + ct * 128:OML + (ct + 1) * 128], "p (k c) -> p k c", k=8),
                load_w(wld_v[:, :, ct * 128:(ct + 1) * 128], "p (k c) -> p k c", k=10))
    ld_next = ld_loads(0)
    WOA = Alloc(hh[0].lo // 4, PERS_END + 16 * NT // 2)
    wo = WOA.bf(8 * 1024, ("p (k c) -> p k c", dict(k=8)))
    wout_v = wout_d.rearrange("(k p) c -> p k c", p=128)
    S.dma("pool", wo.ap, wout_v, [], [wo.reg()])
    for ct in range(8):
        (slot_m, wm), (slot_r, wr) = ld_next
        if ct + 1 < 8:
            ld_next = ld_loads(ct + 1)
        for tb in range(4):
            b2 = tb % 2
            pm = PS[b2]
            S.mm([(pm.ap, wm[:, k, :], hT.ap[:, k, tb * 512:(tb + 1) * 512]) for k in range(8)], [slot_m.reg(), hT.reg()], [pm.reg()])
            S.act(lambda e, b2=b2, pm=pm: e.activation(out=sml[b2].ap, in_=pm.ap, func=AF.Sigmoid), [pm.reg()], [sml[b2].reg()])
            pr = PS[2 + b2]
            S.mm([(pr.ap, wr[:, k, :], yT.ap[:, k, tb * 512:(tb + 1) * 512]) for k in range(10)], [slot_r.reg(), yT.reg()], [pr.reg()])
            S.dve(lambda e, b2=b2, pr=pr: e.tensor_tensor(out=tml[b2].ap, in0=pr.ap, in1=sml[b2].ap, op=ALU.mult), [pr.reg(), sml[b2].reg()], [tml[b2].reg()])
            S.pool(lambda e, b2=b2, ct=ct, tb=tb: e.tensor_tensor(out=PT.ap[:, ct, tb * 512:(tb + 1) * 512], in0=PT.ap[:, ct, tb * 512:(tb + 1) * 512], in1=tml[b2].ap, op=ALU.add),
                   [PT.reg(ct * NT + tb * 512, ct * NT + (tb + 1) * 512), tml[b2].reg()], [PT.reg(ct * NT + tb * 512, ct * NT + (tb + 1) * 512)])

    dump('PT', PT, 8 * NT, BF16)
    checkpoint(8)
    P3 = Alloc(PERS_END, PERS_END + 16 * NT // 2)
    gate_bc = P3.f32(1024)
    fnw_bc = P3.f32(1024)
    x3 = [P3.f32(1024) for _ in range(4)]
    y3 = [P3.f32(1024) for _ in range(4)]
    junk3 = P2d.bf(1024)
    ss3 = [P2d.f32(1) for _ in range(2)]
    rs3 = [P2d.f32(1) for _ in range(2)]
    assert P3.p <= PERS_END + 10 * NT // 2, "output-phase tiles must stay inside the dead yT region"
    wout_v = wout_d.rearrange("(k p) c -> p k c", p=128)
    S.dma("sp", gate_bc.ap, bass.AP(gate_d.tensor, 0, [[0, 128], [1, 1024]]), [("scr2", 0, 4096)], [gate_bc.reg()])
    S.dma("sp", fnw_bc.ap, bass.AP(rows_d.tensor, 1024, [[0, 128], [1, 1024]]), [], [fnw_bc.reg()])
    def p3_load(n):
        S.dma("sp", x3[n % 4].ap, x_d[n * 128:(n + 1) * 128, :], [], [x3[n % 4].reg()])

    def p3_front(n):
        b = n % 2
        yb = y3[n % 4]
        if n + 3 < NCH:
            p3_load(n + 3)
        for half in range(2):
            pb = PS[2 * b + half]
            S.mm([(pb.ap, PT.ap[:, k, n * 128:(n + 1) * 128], wo.ap[:, k, half * 512:(half + 1) * 512]) for k in range(8)],
                 [PT.reg(), wo.reg()], [pb.reg()])
            S.dve(lambda e, half=half, pb=pb: e.tensor_tensor(out=yb.ap[:, half * 512:(half + 1) * 512], in0=pb.ap, in1=gate_bc.ap[:, half * 512:(half + 1) * 512], op=ALU.mult),
                  [pb.reg(), gate_bc.reg()], [yb.reg(half * 512, (half + 1) * 512)])
        S.pool(lambda e: e.tensor_tensor(out=yb.ap, in0=yb.ap, in1=x3[n % 4].ap, op=ALU.add), [yb.reg(), x3[n % 4].reg()], [yb.reg()])

    def p3_back(n):
        b = n % 2
        yb = y3[n % 4]
        S.act(lambda e: e.activation(out=junk3.ap, in_=yb.ap, func=AF.Square, accum_out=ss3[b].ap), [yb.reg()], [junk3.reg(), ss3[b].reg()])
        S.act(lambda e: e.activation(out=rs3[b].ap, in_=ss3[b].ap, func=AF.Ln, scale=1.0 / D, bias=eps_c.ap), [ss3[b].reg(), eps_c.reg()], [rs3[b].reg()])
        S.act(lambda e: e.activation(out=rs3[b].ap, in_=rs3[b].ap, func=AF.Exp, scale=-0.5), [rs3[b].reg()], [rs3[b].reg()])
        S.dve(lambda e: e.scalar_tensor_tensor(out=yb.ap, in0=yb.ap, scalar=rs3[b].ap, in1=fnw_bc.ap, op0=ALU.mult, op1=ALU.mult),
              [yb.reg(), rs3[b].reg(), fnw_bc.reg()], [yb.reg()])
        S.dma("sp", y_d[n * 128:(n + 1) * 128, :], yb.ap, [yb.reg()], [])

    for n_ in range(3):
        p3_load(n_)
    p3_front(0)
    for n in range(NCH):
        if n + 1 < NCH:
            p3_front(n + 1)
        p3_back(n)

    S.emit_all(sems)
    es.close()
    S.dumps = dumps
    return nc, S


_CACHE = {}


def _consts():
    c = np.zeros((128, 257), np.float32)
    c[:, 0:128] = np.eye(128, dtype=np.float32)
    c[:, 128:256] = np.arange(128, dtype=np.float32)[None, :]
    c[:, 256] = np.arange(128, dtype=np.float32)
    return c


def _colv(v, nt):
    return np.ascontiguousarray(np.asarray(v, np.float32).reshape(nt, 128).T)


def make_in_maps(x_prompt, x_sample, state_ret, state_lru, c, c_ctx, norm_w, w_ada, b_ada, w_in,
                 ret_decay_logit, ret_gn_w, w_ret_down, conv_w, conv_b, lru_wa, lru_ba, lru_wx, lru_bx,
                 lru_a_param, w_lru_down, w_out, final_norm_w):
    f = lambda a: np.ascontiguousarray(np.asarray(a, np.float32))
    consts = _consts()
    rows = np.stack([f(b_ada)[0, 2048:3072], f(final_norm_w)], 0)
    shared = dict(consts=consts, rows=rows, w_ada=f(w_ada)[0], w_in=f(w_in)[0], w_ret_down=f(w_ret_down)[0],
                  w_lru_down=f(w_lru_down)[0], w_out=f(w_out)[0], lru_wa=f(lru_wa)[0], lru_wx=f(lru_wx)[0])
    in_maps = []
    for core in range(8):
        cols = np.zeros((128, NCOLS), np.float32)
        cols[:, C_NW:C_NW + 8] = _colv(norm_w[0], 8)
        cols[:, C_BSH:C_BSH + 8] = _colv(b_ada[0, 0:1024], 8)
        cols[:, C_BSC:C_BSC + 8] = _colv(b_ada[0, 1024:2048], 8)
        cols[:, C_GN:C_GN + 16] = _colv(ret_gn_w[0], 16)
        cw = np.asarray(conv_w, np.float32)[0]
        for cb in range(10):
            for j in range(4):
                cols[:, C_CW + cb * 4 + j] = cw[j, cb * 128:(cb + 1) * 128]
        cols[:, C_CB:C_CB + 10] = _colv(conv_b[0], 10)
        for d in range(2):
            cols[:, C_BA + d * 10:C_BA + d * 10 + 10] = _colv(lru_ba[0, d], 10)
            cols[:, C_BX + d * 10:C_BX + d * 10 + 10] = _colv(lru_bx[0, d], 10)
            cols[:, C_AP + d * 10:C_AP + d * 10 + 10] = _colv(lru_a_param[0, d], 10)
        smalls = np.zeros((1, NSM), np.float32)
        smalls[0, 0:8] = np.asarray(ret_decay_logit, np.float32)[0].reshape(8)
        if core < 4:
            x = f(x_sample[core])
            cols[:, C_COND:C_COND + 8] = _colv(c[core], 8)
            init_ret = f(state_ret[core, 0])
            for d in range(2):
                cols[:, C_H0 + d * 10:C_H0 + d * 10 + 10] = _colv(state_lru[core, 0, d], 10)
            smalls[0, 8:24] = 1.0
            smalls[0, 24:40] = 1.0
            smalls[0, 40] = 0.0
            smalls[0, 41] = 1.0
        else:
            p0 = (core - 4) * 4
            x = np.zeros((NT, D), np.float32)
            x[:1024] = np.asarray(x_prompt[p0:p0 + 4], np.float32).reshape(1024, D)
            cols[:, C_COND:C_COND + 8] = _colv(c_ctx, 8)
            init_ret = np.zeros((2, 4, 256, 512), np.float32)
            kf = np.array([1.0 if (n % 2 == 1) else 0.0 for n in range(16)], np.float32)
            kf[0] = 1.0
            kb = np.array([1.0 if (n % 2 == 0) else 0.0 for n in range(16)], np.float32)
            kb[15] = 1.0
            smalls[0, 8:24] = kf
            smalls[0, 24:40] = kb
            smalls[0, 40] = 1.0
            smalls[0, 41] = 0.0
        m = dict(shared)
        m.update(x=x, cols=cols, smalls=smalls, init_ret=init_ret)
        in_maps.append(m)
    return in_maps


def kernel(**inputs):
    if "nc" not in _CACHE:
        _CACHE["nc"] = build_program(DEBUG)[0]
    nc = _CACHE["nc"]
    in_maps = make_in_maps(**inputs)
    res = run_bass_kernel_spmd(nc, in_maps, core_ids=list(range(8)))
    r = res.results
    y_sample = np.stack([r[i]["y"] for i in range(4)], 0).astype(np.float32)
    y_prompt = np.concatenate([r[i]["y"][:1024].reshape(4, 256, D) for i in range(4, 8)], 0).astype(np.float32)
    st_ret = np.concatenate([r[i]["st_ret"] for i in range(4, 8)], 0)[:, None].astype(np.float32)
    lr = []
    for i in range(4, 8):
        a = r[i]["st_lru"].reshape(128, 10, 2, 4)
        lr.append(a.transpose(3, 2, 1, 0).reshape(4, 2, 1280))
    st_lru = np.concatenate(lr, 0)[:, None].astype(np.float32)
    if DEBUG:
        _CACHE["dbg"] = r
    return (y_prompt, y_sample, st_ret, st_lru)
```

```python
import numpy as np
import concourse.bass as bass
import concourse.mybir as mybir
from concourse.bass_utils import run_bass_kernel_spmd

F32 = mybir.dt.float32
BF16 = mybir.dt.bfloat16
AF = mybir.ActivationFunctionType
ALU = mybir.AluOpType

D = 1024
DIN = 10752
NT = 2048
NCH = 16
EPS = 1e-6
OQ, OK_, OV, OG, OXL, OGL, OMR, OML = 0, 1024, 2048, 4096, 6144, 7424, 8704, 9728
NCOLS = 178
C_NW, C_COND, C_BSH, C_BSC, C_GN, C_CW, C_CB, C_BA, C_BX, C_AP, C_H0 = 0, 8, 16, 24, 32, 48, 88, 98, 118, 138, 158
NSM = 8 + 34

DEBUG = False


class Buf:
    def __init__(self, ap, arena, lo, hi, esize):
        self.ap = ap
        self.arena = arena
        self.lo = lo
        self.hi = hi
        self.esize = esize

    def reg(self, elo=None, ehi=None):
        if elo is None:
            return (self.arena, self.lo, self.hi)
        return (self.arena, self.lo + elo * self.esize, self.lo + ehi * self.esize)


class Op:
    __slots__ = ("eng", "emit", "deps", "is_dma", "signal", "sigcount", "dsem", "dval", "dprev", "gid")


class Sched:
    ENGS = ("pe", "act", "dve", "pool", "sp")

    def __init__(self, nc):
        self.nc = nc
        self.ops = []
        self.eng_ops = {e: [] for e in self.ENGS}
        self.recs = {}
        self.enabled = True

    def _access(self, opid, eng, is_dma, regs, is_write, deps):
        for (a, lo, hi) in regs:
            if a.startswith("ps"):
                r = self.recs.get(a)
                if r is not None:
                    rop, rw, reng = r
                    if rop != opid:
                        if is_write or rw or reng != eng:
                            deps.add(rop)
                    else:
                        is_write = is_write or rw
                self.recs[a] = (opid, is_write, eng)
                continue
            recs = self.recs.get(a)
            if recs is None:
                recs = []
                self.recs[a] = recs
            new = []
            for r in recs:
                rlo, rhi, rop, rw, reng, rdma = r
                if rhi <= lo or rlo >= hi:
                    new.append(r)
                    continue
                if (is_write or rw) and rop != opid:
                    deps.add(rop)
                covered = lo <= rlo and rhi <= hi
                if covered and is_write:
                    continue
                if covered and (not is_write) and (not rw) and reng == eng and not rdma and not is_dma:
                    continue
                new.append(r)
            new.append((lo, hi, opid, is_write, eng, is_dma))
            self.recs[a] = new

    def add(self, eng, emit, reads=(), writes=(), dma=False):
        if not self.enabled:
            return -1
        op = Op()
        op.eng = eng
        op.emit = emit
        op.is_dma = dma
        op.signal = False
        opid = len(self.ops)
        deps = set()
        self._access(opid, eng, dma, writes, True, deps)
        self._access(opid, eng, dma, reads, False, deps)
        op.deps = deps
        self.ops.append(op)
        self.eng_ops[eng].append(opid)
        return opid

    def pe(self, emit, reads, writes):
        return self.add("pe", emit, reads, writes)

    def act(self, emit, reads, writes):
        return self.add("act", emit, reads, writes)

    def dve(self, emit, reads, writes):
        return self.add("dve", emit, reads, writes)

    def pool(self, emit, reads, writes):
        return self.add("pool", emit, reads, writes)

    def dma(self, q, out, in_, reads, writes, **kw):
        return self.add(q, lambda e: e.dma_start(out=out, in_=in_, **kw), reads, writes, dma=True)

    def mm(self, mms, reads, writes):
        n = len(mms)

        def emit(e):
            ins = None
            for i, (o, l, r) in enumerate(mms):
                ins = e.matmul(o, l, r, start=(i == 0), stop=(i == n - 1))
            return ins
        return self.add("pe", emit, reads, writes)

    def emit_all(self, sems):
        nc = self.nc
        ops = self.ops
        for op in ops:
            for d in op.deps:
                if not ops[d].is_dma:
                    ops[d].signal = True
        for e in self.ENGS:
            comp = [i for i in self.eng_ops[e] if not ops[i].is_dma]
            if comp:
                ops[comp[-1]].signal = True
        esem = {e: sems.pop() for e in ("pe", "act", "dve", "pool")}
        final = {}
        for e in ("pe", "act", "dve", "pool"):
            c = 0
            for i in self.eng_ops[e]:
                op = ops[i]
                if op.is_dma:
                    continue
                if op.signal:
                    c += 1
                op.sigcount = c if op.signal else None
            final[esem[e]] = c
        KD = 20
        dpool = {q: [sems.pop() for _ in range(KD)] for q in ("sp", "pool")}
        for q in ("sp", "pool"):
            cnt = [0] * KD
            k = 0
            for i in self.eng_ops[q]:
                op = ops[i]
                if not op.is_dma:
                    continue
                s = k % KD
                op.dsem = dpool[q][s]
                op.dprev = cnt[s] * 16
                cnt[s] += 1
                op.dval = cnt[s] * 16
                k += 1
            for s in range(KD):
                final[dpool[q][s]] = cnt[s] * 16
        engobj = {"pe": "tensor", "act": "scalar", "dve": "vector", "pool": "gpsimd", "sp": "sync"}
        self.nwaits = 0

        self.trace = {}

        def run_engine(ename, e):
            waited = {}
            tr = []
            self.trace[ename] = tr

            def wait(sem, val):
                if val <= 0:
                    return
                if waited.get(sem, 0) >= val:
                    return
                waited[sem] = val
                e.wait_ge(sem, val)
                tr.append(("w", id(sem), val))
                self.nwaits += 1
            for i in self.eng_ops[ename]:
                op = ops[i]
                need = {}
                for d in op.deps:
                    dop = ops[d]
                    if dop.is_dma:
                        sm, vl = dop.dsem, dop.dval
                    else:
                        if dop.eng == "pe" and ename == "pe":
                            continue
                        sm, vl = esem[dop.eng], dop.sigcount
                    if need.get(id(sm), (None, 0))[1] < vl:
                        need[id(sm)] = (sm, vl)
                for sm, vl in need.values():
                    wait(sm, vl)
                if op.is_dma:
                    wait(op.dsem, op.dprev)
                    ins = op.emit(e)
                    ins.then_inc(op.dsem, 16)
                    tr.append(("i", id(op.dsem), 16, i))
                else:
                    ins = op.emit(e)
                    if op.signal:
                        ins.then_inc(esem[ename], 1)
                        tr.append(("i", id(esem[ename]), 1, i))
                    else:
                        tr.append(("n", i))
            if ename == "sp":
                for sem, val in final.items():
                    wait(sem, val)

        with nc.Block() as block:
            @block.tensor
            def _(e):
                run_engine("pe", e)

            @block.scalar
            def _(e):
                run_engine("act", e)

            @block.vector
            def _(e):
                run_engine("dve", e)

            @block.gpsimd
            def _(e):
                run_engine("pool", e)

            @block.sync
            def _(e):
                run_engine("sp", e)


def bcast_ap(ap, dims):
    return bass.AP(ap.tensor, ap.offset, [list(ap.ap[0])] + [list(d) for d in dims])


class _Stop(Exception):
    pass


def build_program(debug=False, limit=None):
    nc = bass.Bass("TRN2", target_bir_lowering=False)
    dt = nc.dram_tensor
    x_d = dt("x", [NT, D], F32, kind="ExternalInput").ap()
    cols_d = dt("cols", [128, NCOLS], F32, kind="ExternalInput").ap()
    rows_d = dt("rows", [2, D], F32, kind="ExternalInput").ap()
    sm_d = dt("smalls", [1, NSM], F32, kind="ExternalInput").ap()
    consts_d = dt("consts", [128, 257], F32, kind="ExternalInput").ap()
    iret_d = dt("init_ret", [2, 4, 256, 512], F32, kind="ExternalInput").ap()
    wada_d = dt("w_ada", [D, 3 * D], F32, kind="ExternalInput").ap()
    win_d = dt("w_in", [D, DIN], F32, kind="ExternalInput").ap()
    wrd_d = dt("w_ret_down", [2048, D], F32, kind="ExternalInput").ap()
    wld_d = dt("w_lru_down", [1280, D], F32, kind="ExternalInput").ap()
    wout_d = dt("w_out", [D, D], F32, kind="ExternalInput").ap()
    wa_d = dt("lru_wa", [2, 10, 128, 128], F32, kind="ExternalInput").ap()
    wx_d = dt("lru_wx", [2, 10, 128, 128], F32, kind="ExternalInput").ap()
    y_d = dt("y", [NT, D], F32, kind="ExternalOutput").ap()
    sret_d = dt("st_ret", [4, 2, 4, 256, 512], F32, kind="ExternalOutput").ap()
    slru_d = dt("st_lru", [128, 80], F32, kind="ExternalOutput").ap()
    tb_d = dt("tb_scr", [4 * 16 * 128, 1024], BF16, kind="Internal").ap()
    gate_d = dt("gate_scr", [1, D], F32, kind="Internal").ap()
    dbg_d = None
    if debug:
        dbg_d = dt("dbg", [128, 8 * 2048], F32, kind="ExternalOutput").ap()

    S = Sched(nc)
    NW = 52800
    dumps = {}

    def dump(name, buf, n, dtype):
        if not debug:
            return
        dten = dt("dbg_" + name, [128, n], dtype, kind="ExternalOutput").ap()
        dumps[name] = (n, dtype)
        src = bass.AP(buf.ap.tensor, buf.ap.offset, [list(buf.ap.ap[0]), [1, n]])
        S.dma("sp", dten, src, [buf.reg()], [])
    from contextlib import ExitStack
    es = ExitStack()
    arena = es.enter_context(nc.sbuf_tensor("arena", [128, NW], F32))
    psb = [es.enter_context(nc.psum_tensor("ps%d" % i, [128, 512], F32)) for i in range(8)]
    sems = [es.enter_context(nc.semaphore("s%d" % i)) for i in range(48)]

    class Alloc:
        def __init__(self, lo, hi):
            self.p = lo
            self.hi = hi

        def f32(self, n, shape=None):
            w0 = self.p
            self.p += n
            assert self.p <= self.hi, ("sbuf overflow", self.p, self.hi)
            ap = arena[:, w0:w0 + n]
            if shape:
                ap = ap.rearrange(shape[0], **shape[1])
            return Buf(ap, "sb", w0 * 4, (w0 + n) * 4, 4)

        def bf(self, n, shape=None):
            nw = (n + 1) // 2
            w0 = self.p
            self.p += nw
            assert self.p <= self.hi, ("sbuf overflow", self.p, self.hi)
            ap = arena[:, w0:w0 + nw].bitcast(BF16)
            if shape:
                ap = ap.rearrange(shape[0], **shape[1])
            return Buf(ap, "sb", w0 * 4, (w0 + nw) * 4, 2)

    def psum(i, bf=False):
        ap = psb[i][:]
        if bf:
            return Buf(ap.bitcast(BF16), "ps%d" % i, 0, 2048, 2)
        return Buf(ap, "ps%d" % i, 0, 2048, 4)

    PS = [psum(i) for i in range(8)]
    PSB = [psum(i, True) for i in range(8)]

    A = Alloc(0, NW)
    hT = A.bf(8 * NT, ("p (k t) -> p k t", dict(k=8)))
    NSLOT = 5
    ring = [A.bf(2048) for _ in range(NSLOT)]
    ident = A.f32(128)
    identb = A.bf(128)
    iota_r = A.f32(128)
    iota_c = A.f32(1)
    cols = A.f32(NCOLS)
    smalls = A.f32(NSM)
    lg = A.f32(8)
    nlg = A.f32(8)
    lg128 = A.f32(8)
    lg127 = A.f32(8)
    cdec = A.f32(8)
    ck = A.f32(8 * 16)
    kd = A.f32(8)
    Dm = A.f32(4 * 128, ("p (h i) -> p h i", dict(h=4)))
    rowd = A.bf(8 * 128, ("p (g i) -> p g i", dict(g=8)))
    Acol = A.f32(8)
    Bcol = A.f32(8)
    scl = A.f32(20)
    scl2 = A.f32(20)
    fw = A.f32(40)
    lruout = A.f32(80)
    one_c = A.f32(1)
    eps_c = A.f32(1)
    tiny_c = A.f32(1)
    scb = A.bf(8 * 128, ("p (k m) -> p k m", dict(k=8)))
    tiny = A.f32(64)
    PERS_END = A.p

    flagsF = lambda n: smalls.ap[:, 8 + n:9 + n]
    flagsB = lambda n: smalls.ap[:, 24 + n:25 + n]
    convflag = smalls.ap[:, 40:41]
    lrukeep = smalls.ap[:, 41:42]

    wslot = [0]

    def next_slot():
        s = ring[wslot[0] % NSLOT]
        wslot[0] += 1
        return s

    def load_w(src_ap, shape_str, **kw):
        slot = next_slot()
        n = 1
        for v in src_ap.shape[1:]:
            n *= v
        view = slot.ap[:, 0:n].rearrange(shape_str, **kw)
        S.dma("pool", view, src_ap, [], [slot.reg(0, n)])
        return slot, view

    win_v = win_d.rearrange("(k p) c -> p k c", p=128)

    S.dma("sp", cols.ap, cols_d, [], [cols.reg()])
    S.dma("sp", smalls.ap, bass.AP(sm_d.tensor, 0, [[0, 128], [1, NSM]]), [], [smalls.reg()])
    S.dma("sp", ident.ap, consts_d[:, 0:128], [], [ident.reg()])
    S.dma("sp", iota_r.ap, consts_d[:, 128:256], [], [iota_r.reg()])
    S.dma("sp", iota_c.ap, consts_d[:, 256:257], [], [iota_c.reg()], allow_slow_non_contiguous=True)
    S.dve(lambda e: e.memset(one_c.ap, 1.0), [], [one_c.reg()])
    S.dve(lambda e: e.memset(eps_c.ap, EPS), [], [eps_c.reg()])
    S.dve(lambda e: e.memset(tiny_c.ap, 1e-18), [], [tiny_c.reg()])
    S.dve(lambda e: e.tensor_copy(out=identb.ap, in_=ident.ap), [ident.reg()], [identb.reg()])
    t0 = Buf(tiny.ap[:, 0:8], "sb", tiny.lo, tiny.lo + 32, 4)
    S.act(lambda e: e.activation(out=t0.ap, in_=smalls.ap[:, 0:8], func=AF.Exp, scale=-1.0), [smalls.reg()], [t0.reg()])
    S.act(lambda e: e.activation(out=t0.ap, in_=t0.ap, func=AF.Ln, bias=one_c.ap), [t0.reg(), one_c.reg()], [t0.reg()])
    S.dve(lambda e: e.tensor_scalar(out=lg.ap, in0=t0.ap, scalar1=-1.0, scalar2=None, op0=ALU.mult), [t0.reg()], [lg.reg()])
    S.dve(lambda e: e.tensor_copy(out=nlg.ap, in_=t0.ap), [t0.reg()], [nlg.reg()])
    S.dve(lambda e: e.tensor_scalar(out=lg128.ap, in0=lg.ap, scalar1=128.0, scalar2=None, op0=ALU.mult), [lg.reg()], [lg128.reg()])
    S.dve(lambda e: e.tensor_scalar(out=lg127.ap, in0=lg.ap, scalar1=127.0, scalar2=None, op0=ALU.mult), [lg.reg()], [lg127.reg()])
    S.act(lambda e: e.activation(out=cdec.ap, in_=lg128.ap, func=AF.Exp), [lg128.reg()], [cdec.reg()])
    for d in range(2):
        for h in range(4):
            g = d * 4 + h
            S.dve(lambda e, g=g, d=d: e.tensor_scalar(out=ck.ap[:, g * 16:(g + 1) * 16], in0=smalls.ap[:, 8 + 16 * d:24 + 16 * d],
                                                        scalar1=cdec.ap[:, g:g + 1], scalar2=None, op0=ALU.mult),
                  [smalls.reg(), cdec.reg()], [ck.reg(g * 16, (g + 1) * 16)])
    for h in range(4):
        S.act(lambda e, h=h: e.activation(out=kd.ap[:, h:h + 1], in_=iota_c.ap, func=AF.Exp, scale=nlg.ap[:, h:h + 1], bias=lg127.ap[:, h:h + 1]),
              [iota_c.reg(), nlg.reg(), lg127.reg()], [kd.reg(h, h + 1)])
        S.act(lambda e, h=h: e.activation(out=kd.ap[:, 4 + h:5 + h], in_=iota_c.ap, func=AF.Exp, scale=lg.ap[:, 4 + h:5 + h]),
              [iota_c.reg(), lg.reg()], [kd.reg(4 + h, 5 + h)])
    S.dve(lambda e: e.tensor_scalar(out=kd.ap, in0=kd.ap, scalar1=1.0 / 16.0, scalar2=None, op0=ALU.mult), [kd.reg()], [kd.reg()])
    P0 = Alloc(PERS_END, NW)
    rtmp = P0.f32(128)
    for h in range(4):
        S.act(lambda e, h=h: e.activation(out=rtmp.ap, in_=iota_r.ap, func=AF.Exp, scale=lg.ap[:, h:h + 1], bias=lg.ap[:, h:h + 1]),
              [iota_r.reg(), lg.reg()], [rtmp.reg()])
        S.dve(lambda e, h=h: e.tensor_copy(out=rowd.ap[:, h, :], in_=rtmp.ap), [rtmp.reg()], [rowd.reg(h * 128, (h + 1) * 128)])
        S.act(lambda e, h=h: e.activation(out=rtmp.ap, in_=iota_r.ap, func=AF.Exp, scale=nlg.ap[:, 4 + h:5 + h], bias=lg128.ap[:, 4 + h:5 + h]),
              [iota_r.reg(), nlg.reg(), lg128.reg()], [rtmp.reg()])
        S.dve(lambda e, h=h: e.tensor_copy(out=rowd.ap[:, 4 + h, :], in_=rtmp.ap), [rtmp.reg()], [rowd.reg((4 + h) * 128, (5 + h) * 128)])
    delta = P0.f32(128)
    dpos = P0.f32(128)
    dneg = P0.f32(128)
    mF = P0.f32(128)
    mB = P0.f32(128)
    e1 = P0.f32(128)
    e2 = P0.f32(128)
    S.dve(lambda e: e.tensor_scalar(out=delta.ap, in0=iota_r.ap, scalar1=iota_c.ap, scalar2=None, op0=ALU.subtract), [iota_r.reg(), iota_c.reg()], [delta.reg()])
    S.dve(lambda e: e.tensor_scalar(out=dpos.ap, in0=delta.ap, scalar1=0.0, scalar2=None, op0=ALU.max), [delta.reg()], [dpos.reg()])
    S.dve(lambda e: e.tensor_scalar(out=dneg.ap, in0=delta.ap, scalar1=-1.0, scalar2=0.0, op0=ALU.mult, op1=ALU.max), [delta.reg()], [dneg.reg()])
    S.dve(lambda e: e.tensor_scalar(out=mF.ap, in0=delta.ap, scalar1=0.0, scalar2=None, op0=ALU.is_ge), [delta.reg()], [mF.reg()])
    S.dve(lambda e: e.tensor_scalar(out=mB.ap, in0=delta.ap, scalar1=0.0, scalar2=None, op0=ALU.is_lt), [delta.reg()], [mB.reg()])
    for h in range(4):
        S.act(lambda e, h=h: e.activation(out=e1.ap, in_=dpos.ap, func=AF.Exp, scale=lg.ap[:, h:h + 1]), [dpos.reg(), lg.reg()], [e1.reg()])
        S.act(lambda e, h=h: e.activation(out=e2.ap, in_=dneg.ap, func=AF.Exp, scale=lg.ap[:, 4 + h:5 + h]), [dneg.reg(), lg.reg()], [e2.reg()])
        S.dve(lambda e: e.tensor_tensor(out=e1.ap, in0=e1.ap, in1=mF.ap, op=ALU.mult), [e1.reg(), mF.reg()], [e1.reg()])
        S.dve(lambda e: e.tensor_tensor(out=e2.ap, in0=e2.ap, in1=mB.ap, op=ALU.mult), [e2.reg(), mB.reg()], [e2.reg()])
        S.dve(lambda e, h=h: e.tensor_tensor(out=Dm.ap[:, h, :], in0=e1.ap, in1=e2.ap, op=ALU.add), [e1.reg(), e2.reg()], [Dm.reg(h * 128, (h + 1) * 128)])
    t1 = Buf(tiny.ap[:, 16:36], "sb", tiny.lo + 64, tiny.lo + 144, 4)
    S.act(lambda e: e.activation(out=t1.ap, in_=cols.ap[:, C_AP:C_AP + 20], func=AF.Exp, scale=-1.0), [cols.reg()], [t1.reg()])
    S.act(lambda e: e.activation(out=t1.ap, in_=t1.ap, func=AF.Ln, bias=one_c.ap), [t1.reg(), one_c.reg()], [t1.reg()])
    S.dve(lambda e: e.tensor_scalar(out=scl.ap, in0=t1.ap, scalar1=-8.0, scalar2=None, op0=ALU.mult), [t1.reg()], [scl.reg()])
    S.dve(lambda e: e.tensor_scalar(out=scl2.ap, in0=t1.ap, scalar1=-16.0, scalar2=None, op0=ALU.mult), [t1.reg()], [scl2.reg()])
    S.dve(lambda e: e.tensor_scalar(out=fw.ap, in0=cols.ap[:, C_CW:C_CW + 40], scalar1=convflag, scalar2=None, op0=ALU.mult), [cols.reg(), smalls.reg()], [fw.reg()])
    sc32 = Buf(tiny.ap[:, 40:48], "sb", tiny.lo + 160, tiny.lo + 192, 4)
    S.act(lambda e: e.activation(out=sc32.ap, in_=cols.ap[:, C_COND:C_COND + 8], func=AF.Silu), [cols.reg()], [sc32.reg()])
    S.dve(lambda e: e.tensor_copy(out=scb.ap, in_=bcast_ap(sc32.ap, [[1, 8], [0, 128]])), [sc32.reg()], [scb.reg()])

    wada_v = wada_d.rearrange("(k p) c -> p k c", p=128)
    modps = PS[7]
    for sl in range(8):
        slot, wv = load_w(wada_v[:, :, sl * 256:(sl + 1) * 256], "p (k c) -> p k c", k=8)
        for j in range(2):
            ft = sl * 2 + j
            S.mm([(modps.ap[:, ft:ft + 1], wv[:, k, j * 128:(j + 1) * 128], scb.ap[:, k, 0:1]) for k in range(8)],
                 [slot.reg(), scb.reg()], [modps.reg(ft, ft + 1)])
    S.dve(lambda e: e.tensor_tensor(out=Bcol.ap, in0=modps.ap[:, 0:8], in1=cols.ap[:, C_BSH:C_BSH + 8], op=ALU.add), [modps.reg(0, 16), cols.reg()], [Bcol.reg()])
    S.dve(lambda e: e.tensor_tensor(out=Acol.ap, in0=modps.ap[:, 8:16], in1=cols.ap[:, C_BSC:C_BSC + 8], op=ALU.add), [modps.reg(0, 16), cols.reg()], [Acol.reg()])
    S.dve(lambda e: e.scalar_tensor_tensor(out=Acol.ap, in0=Acol.ap, scalar=1.0, in1=cols.ap[:, C_NW:C_NW + 8], op0=ALU.add, op1=ALU.mult), [Acol.reg(), cols.reg()], [Acol.reg()])

    def checkpoint(k):
        if limit == k:
            S.enabled = False

    checkpoint(0)
    xs = [P0.f32(1024) for _ in range(2)]
    xn = [P0.f32(1024) for _ in range(2)]
    junk = P0.bf(1024)
    ssq = [P0.f32(1) for _ in range(2)]
    rstd = [P0.f32(1) for _ in range(2)]
    for n in range(NCH):
        b = n % 2
        S.dma("sp", xs[b].ap, x_d[n * 128:(n + 1) * 128, :], [], [xs[b].reg()])
        S.act(lambda e, b=b: e.activation(out=junk.ap, in_=xs[b].ap, func=AF.Square, accum_out=ssq[b].ap), [xs[b].reg()], [junk.reg(), ssq[b].reg()])
        S.act(lambda e, b=b: e.activation(out=rstd[b].ap, in_=ssq[b].ap, func=AF.Ln, scale=1.0 / D, bias=eps_c.ap), [ssq[b].reg(), eps_c.reg()], [rstd[b].reg()])
        S.act(lambda e, b=b: e.activation(out=rstd[b].ap, in_=rstd[b].ap, func=AF.Exp, scale=-0.5), [rstd[b].reg()], [rstd[b].reg()])
        S.act(lambda e, b=b: e.activation(out=xn[b].ap, in_=xs[b].ap, func=AF.Copy, scale=rstd[b].ap), [xs[b].reg(), rstd[b].reg()], [xn[b].reg()])
        if n == 0:
            checkpoint(10)
        if n == 1:
            checkpoint(13)
        for half in range(2):
            pb = PS[half]
            for j in range(4):
                k = half * 4 + j
                if n == 0 and half == 0 and j == 1:
                    checkpoint(11)
                S.pe(lambda e, k=k, j=j, pb=pb, b=b: e.transpose(pb.ap[:, j * 128:(j + 1) * 128], xn[b].ap[:, k * 128:(k + 1) * 128], ident.ap),
                     [xn[b].reg(k * 128, (k + 1) * 128), ident.reg()], [pb.reg(j * 128, (j + 1) * 128)])
            for j in range(4):
                k = half * 4 + j
                S.dve(lambda e, k=k, j=j, pb=pb, n=n: e.tensor_scalar(out=hT.ap[:, k, n * 128:(n + 1) * 128], in0=pb.ap[:, j * 128:(j + 1) * 128],
                                                                          scalar1=Acol.ap[:, k:k + 1], scalar2=Bcol.ap[:, k:k + 1], op0=ALU.mult, op1=ALU.add),
                      [pb.reg(j * 128, (j + 1) * 128), Acol.reg(), Bcol.reg()], [hT.reg(k * NT + n * 128, k * NT + (n + 1) * 128)])

        if n == 0:
            checkpoint(12)
    dump('hT', hT, 8 * NT, BF16)
    checkpoint(1)
    ipb = [0]

    def inproj_fm(col0, ncols, consumer, banks=(0, 1)):
        slot, wv = load_w(win_v[:, :, col0:col0 + ncols], "p (k c) -> p k c", k=8)

        def compute():
            for ct in range(ncols // 128):
                for tb in range(4):
                    pb = PS[banks[ipb[0] % len(banks)]]
                    ipb[0] += 1
                    S.mm([(pb.ap, wv[:, k, ct * 128:(ct + 1) * 128], hT.ap[:, k, tb * 512:(tb + 1) * 512]) for k in range(8)],
                         [slot.reg(), hT.reg()], [pb.reg()])
                    consumer(ct, tb, pb)
        return compute

    P1 = Alloc(PERS_END, NW)
    oT = P1.bf(16 * NT, ("p (k t) -> p k t", dict(k=16)))
    qT = P1.bf(2 * NT, ("p (k t) -> p k t", dict(k=2)))
    kT = P1.bf(2 * NT, ("p (k t) -> p k t", dict(k=2)))
    vt = P1.bf(NCH * 512, ("p (n e) -> p n e", dict(n=NCH)))
    U = [[P1.f32(1024, ("p (k e) -> p k e", dict(k=2))) for _ in range(2)] for _ in range(2)]
    SFb = [P1.bf(1024, ("p (k e) -> p k e", dict(k=2))) for _ in range(2)]
    TBw = [P1.bf(1024, ("p (k e) -> p k e", dict(k=2))) for _ in range(2)]
    TBr = [P1.bf(1024, ("p (k e) -> p k e", dict(k=2))) for _ in range(2)]
    kpr = [P1.bf(256) for _ in range(2)]
    Pm = [P1.bf(128) for _ in range(2)]
    qF = [P1.bf(256, ("p (k i) -> p k i", dict(k=2))) for _ in range(2)]
    qB = [P1.bf(256, ("p (k i) -> p k i", dict(k=2))) for _ in range(2)]
    og = [P1.bf(512) for _ in range(2)]
    junk1 = P1.bf(512)
    ss1 = [P1.f32(1) for _ in range(2)]
    rs1 = [P1.f32(1) for _ in range(2)]
    sgt = [P1.bf(512) for _ in range(2)]
    grow = P1.f32(1024)
    growb = P1.f32(1024)
    P1_END = P1.p

    tb_v = tb_d.rearrange("(h n p) (k e) -> h n p k e", h=4, n=16, k=2)

    def tb_reg(h, n):
        base = ((h * 16 + n) * 128) * 1024 * 2
        return ("scr", base, base + 128 * 1024 * 2)

    actdve = [0]

    def evac_copy(out_ap, out_reg, pb, scale=None):
        if actdve[0] % 2 == 0:
            if scale is None:
                S.act(lambda e: e.activation(out=out_ap, in_=pb.ap, func=AF.Copy), [pb.reg()], [out_reg])
            else:
                S.act(lambda e: e.activation(out=out_ap, in_=pb.ap, func=AF.Copy, scale=scale), [pb.reg()], [out_reg])
        else:
            if scale is None:
                S.dve(lambda e: e.tensor_copy(out=out_ap, in_=pb.ap), [pb.reg()], [out_reg])
            else:
                S.dve(lambda e: e.tensor_scalar(out=out_ap, in0=pb.ap, scalar1=scale, scalar2=None, op0=ALU.mult), [pb.reg()], [out_reg])
        actdve[0] += 1

    def gate_row_job():
        S.dma("sp", growb.ap[0:1, :], rows_d[0:1, :], [], [growb.reg()])
        for s_ in range(4):
            slot, wv = load_w(wada_v[:, :, 2048 + s_ * 256:2048 + (s_ + 1) * 256], "p (k c) -> p k c", k=8)
            pbg = PS[4 + s_ % 2]
            S.mm([(pbg.ap[:, 0:256], scb.ap[:, k, :], wv[:, k, :]) for k in range(8)], [slot.reg(), scb.reg()], [pbg.reg(0, 256)])
            S.dve(lambda e, s_=s_, pbg=pbg: e.tensor_tensor(out=grow.ap[0:1, s_ * 256:(s_ + 1) * 256], in0=pbg.ap[0:1, 0:256], in1=growb.ap[0:1, s_ * 256:(s_ + 1) * 256], op=ALU.add),
                  [pbg.reg(0, 256), growb.reg()], [grow.reg(s_ * 256, (s_ + 1) * 256)])
        S.dma("sp", gate_d, grow.ap[0:1, :], [grow.reg()], [("scr2", 0, 4096)])

    gate_ctr = [0]

    def make_gate_jobs(hg, banks):
        slabs = [load_w(win_v[:, :, OG + hg * 512 + s_ * 256:OG + hg * 512 + (s_ + 1) * 256], "p (k c) -> p k c", k=8) for s_ in range(2)]
        jobs = []
        for s_ in range(2):
            slot, wv = slabs[s_]
            for ct in range(2):
                for tb in range(4):
                    def job(slot=slot, wv=wv, s_=s_, ct=ct, tb=tb):
                        i_ = gate_ctr[0]
                        gate_ctr[0] += 1
                        b2 = i_ % 2
                        pb = PS[banks[i_ % len(banks)]]
                        S.mm([(pb.ap, wv[:, k, ct * 128:(ct + 1) * 128], hT.ap[:, k, tb * 512:(tb + 1) * 512]) for k in range(8)],
                             [slot.reg(), hT.reg()], [pb.reg()])
                        S.act(lambda e: e.activation(out=sgt[b2].ap, in_=pb.ap, func=AF.Silu), [pb.reg()], [sgt[b2].reg()])
                        k_ = hg * 4 + s_ * 2 + ct
                        rg = oT.reg(k_ * NT + tb * 512, k_ * NT + (tb + 1) * 512)
                        S.pool(lambda e: e.tensor_tensor(out=oT.ap[:, k_, tb * 512:(tb + 1) * 512], in0=oT.ap[:, k_, tb * 512:(tb + 1) * 512], in1=sgt[b2].ap, op=ALU.mult),
                               [rg, sgt[b2].reg()], [rg])
                    jobs.append(job)
        return jobs

    for h in range(4):
        gF, gB = h, 4 + h
        def q_cons(ct, tb, pb):
            evac_copy(qT.ap[:, ct, tb * 512:(tb + 1) * 512], qT.reg(ct * NT + tb * 512, ct * NT + (tb + 1) * 512), pb)

        def k_cons(ct, tb, pb):
            evac_copy(kT.ap[:, ct, tb * 512:(tb + 1) * 512], kT.reg(ct * NT + tb * 512, ct * NT + (tb + 1) * 512), pb)
        cq = inproj_fm(OQ + h * 256, 256, q_cons)
        ckk = inproj_fm(OK_ + h * 256, 256, k_cons)
        vsl = [load_w(win_v[:, :, OV + h * 512 + s * 256: OV + h * 512 + (s + 1) * 256], "p (k c) -> p k c", k=8) for s in range(2)]
        cq()
        ckk()
        for n in range(NCH):
            pb = PS[n % 2]
            for s in range(2):
                slot, wv = vsl[s]
                S.mm([(pb.ap[:, s * 256:(s + 1) * 256], hT.ap[:, k, n * 128:(n + 1) * 128], wv[:, k, :]) for k in range(8)],
                     [slot.reg(), hT.reg()], [pb.reg(s * 256, (s + 1) * 256)])
            evac_copy(vt.ap[:, n, :], vt.reg(n * 512, (n + 1) * 512), pb)
        if h == 0:
            dump('qT', qT, 2 * NT, BF16)
            dump('kT', kT, 2 * NT, BF16)
            dump('vt', vt, NCH * 512, BF16)
            checkpoint(2)
        for d in range(2):
            S.dma("sp", U[d][0].ap, iret_d[d, h].rearrange("(k p) e -> p k e", p=128), [], [U[d][0].reg()])
        def T_(n, single=False):
            kps = PSB[0] if single else PSB[n % 2]
            for dtl in range(2):
                S.pe(lambda e, dtl=dtl, n=n, kps=kps: e.transpose(kps.ap[:, dtl * 128:(dtl + 1) * 128], kT.ap[:, dtl, n * 128:(n + 1) * 128], identb.ap),
                     [kT.reg(dtl * NT + n * 128, dtl * NT + (n + 1) * 128), identb.reg()], [kps.reg(dtl * 128, (dtl + 1) * 128)])

        def KPR_(n, g, single=False):
            kps = PSB[0] if single else PSB[n % 2]
            b = n % 2
            S.act(lambda e: e.activation(out=kpr[b].ap, in_=kps.ap[:, 0:256], func=AF.Copy, scale=kd.ap[:, g:g + 1]),
                  [kps.reg(0, 256), kd.reg()], [kpr[b].reg()])

        def KV_(n):
            b = n % 2
            for dtl in range(2):
                pb = PS[6 + dtl]
                S.mm([(pb.ap, kpr[b].ap[:, dtl * 128:(dtl + 1) * 128], vt.ap[:, n, :])], [kpr[b].reg(), vt.reg(n * 512, (n + 1) * 512)], [pb.reg()])

        def UPD_(n, d, g, cur):
            nxt = 1 - cur
            for dtl in range(2):
                pb = PS[6 + dtl]
                S.dve(lambda e, dtl=dtl, pb=pb: e.scalar_tensor_tensor(
                    out=U[d][nxt].ap[:, dtl, :], in0=U[d][cur].ap[:, dtl, :], scalar=ck.ap[:, g * 16 + n:g * 16 + n + 1], in1=pb.ap, op0=ALU.mult, op1=ALU.add),
                    [U[d][cur].reg(dtl * 512, (dtl + 1) * 512), ck.reg(), pb.reg()], [U[d][nxt].reg(dtl * 512, (dtl + 1) * 512)])
            return nxt

        def TBW_(n, cur):
            b = n % 2
            S.act(lambda e: e.activation(out=TBw[b].ap, in_=U[1][cur].ap, func=AF.Copy, scale=flagsB(n)),
                  [U[1][cur].reg(), smalls.reg()], [TBw[b].reg()])
            S.dma("sp", tb_v[h, n], TBw[b].ap, [TBw[b].reg()], [tb_reg(h, n)])

        cur = 0
        gjobs = make_gate_jobs(h - 1, (3,)) if h >= 1 else []
        if h == 0:
            gate_row_job()
        T_(NCH - 1)
        KPR_(NCH - 1, gB)
        TBW_(NCH - 1, cur)
        for n in range(NCH - 1, -1, -1):
            if n >= 1:
                T_(n - 1)
            KV_(n)
            if n >= 1:
                KPR_(n - 1, gB)
            cur = UPD_(n, 1, gB, cur)
            if n >= 1:
                TBW_(n - 1, cur)
            if gjobs:
                gjobs.pop(0)()
            if n in (0, 2, 4, 6):
                S.dma("sp", sret_d[n // 2, 1, h].rearrange("(k p) e -> p k e", p=128), U[1][cur].ap, [U[1][cur].reg()], [])
        if h == 0:
            checkpoint(3)

        def S_(n):
            sps = PS[2]
            S.mm([(sps.ap[:, 0:128], kT.ap[:, dtl, n * 128:(n + 1) * 128], qT.ap[:, dtl, n * 128:(n + 1) * 128]) for dtl in range(2)],
                 [kT.reg(), qT.reg()], [sps.reg(0, 128)])

        def P_(n):
            sps = PS[2]
            b = n % 2
            S.dve(lambda e, h=h: e.scalar_tensor_tensor(out=Pm[b].ap, in0=sps.ap[:, 0:128], scalar=1.0 / 16.0, in1=Dm.ap[:, h, :], op0=ALU.mult, op1=ALU.mult),
                  [sps.reg(0, 128), Dm.reg()], [Pm[b].reg()])

        def Q_(n):
            b = n % 2
            S.pool(lambda e, gF=gF: e.tensor_tensor(out=qF[b].ap, in0=qT.ap[:, :, n * 128:(n + 1) * 128],
                                             in1=bcast_ap(rowd.ap[:, gF, :], [[0, 2], [1, 128]]), op=ALU.mult),
                   [qT.reg(), rowd.reg()], [qF[b].reg()])
            S.pool(lambda e, gB=gB: e.tensor_tensor(out=qB[b].ap, in0=qT.ap[:, :, n * 128:(n + 1) * 128],
                                             in1=bcast_ap(rowd.ap[:, gB, :], [[0, 2], [1, 128]]), op=ALU.mult),
                   [qT.reg(), rowd.reg()], [qB[b].reg()])

        def SF_(n, cur):
            b = n % 2
            S.dve(lambda e: e.tensor_scalar(out=SFb[b].ap, in0=U[0][cur].ap, scalar1=flagsF(n), scalar2=None, op0=ALU.mult),
                  [U[0][cur].reg(), smalls.reg()], [SFb[b].reg()])

        def O_(n):
            b = n % 2
            ops_ = PS[4 + b]
            mms = [(ops_.ap, Pm[b].ap, vt.ap[:, n, :])]
            mms += [(ops_.ap, qB[b].ap[:, dtl, :], TBr[b].ap[:, dtl, :]) for dtl in range(2)]
            mms += [(ops_.ap, qF[b].ap[:, dtl, :], SFb[b].ap[:, dtl, :]) for dtl in range(2)]
            S.mm(mms, [Pm[b].reg(), vt.reg(n * 512, (n + 1) * 512), qF[b].reg(), qB[b].reg(), SFb[b].reg(), TBr[b].reg()], [ops_.reg()])

        def NORM_(n):
            b = n % 2
            ops_ = PS[4 + b]
            S.act(lambda e: e.activation(out=junk1.ap, in_=ops_.ap, func=AF.Square, accum_out=ss1[b].ap), [ops_.reg()], [junk1.reg(), ss1[b].reg()])
            S.act(lambda e: e.activation(out=rs1[b].ap, in_=ss1[b].ap, func=AF.Ln, scale=1.0 / 512.0, bias=eps_c.ap), [ss1[b].reg(), eps_c.reg()], [rs1[b].reg()])
            S.act(lambda e: e.activation(out=rs1[b].ap, in_=rs1[b].ap, func=AF.Exp, scale=-0.5), [rs1[b].reg()], [rs1[b].reg()])
            S.act(lambda e: e.activation(out=og[b].ap, in_=ops_.ap, func=AF.Copy, scale=rs1[b].ap), [ops_.reg(), rs1[b].reg()], [og[b].reg()])

        def OGT_(n):
            b = n % 2
            tps = PSB[1]
            for et in range(4):
                S.pe(lambda e, et=et: e.transpose(tps.ap[:, et * 128:(et + 1) * 128], og[b].ap[:, et * 128:(et + 1) * 128], identb.ap),
                     [og[b].reg(et * 128, (et + 1) * 128), identb.reg()], [tps.reg(et * 128, (et + 1) * 128)])

        def OTE_(n):
            tps = PSB[1]
            S.dve(lambda e, h=h: e.tensor_tensor(out=oT.ap[:, h * 4:(h + 1) * 4, n * 128:(n + 1) * 128],
                                            in0=tps.ap[:, 0:512].rearrange("p (a t) -> p a t", a=4),
                                            in1=bcast_ap(cols.ap[:, C_GN + h * 4:C_GN + h * 4 + 4], [[1, 4], [0, 128]]), op=ALU.mult),
                  [tps.reg(0, 512), cols.reg()], [oT.reg((h * 4) * NT, (h * 4 + 4) * NT)])

        cur = 0
        S.dma("sp", TBr[0].ap, tb_v[h, 0], [tb_reg(h, 0)], [TBr[0].reg()])
        T_(0, True)
        KPR_(0, gF, True)
        S_(0)
        P_(0)
        Q_(0)
        SF_(0, cur)
        for n in range(NCH):
            last = (n + 1 == NCH)
            if not last:
                S.dma("sp", TBr[(n + 1) % 2].ap, tb_v[h, n + 1], [tb_reg(h, n + 1)], [TBr[(n + 1) % 2].reg()])
                T_(n + 1, True)
                S_(n + 1)
            KV_(n)
            O_(n)
            if n >= 1:
                OGT_(n - 1)
            cur = UPD_(n, 0, gF, cur)
            if n in (1, 3, 5, 7):
                S.dma("sp", sret_d[n // 2, 0, h].rearrange("(k p) e -> p k e", p=128), U[0][cur].ap, [U[0][cur].reg()], [])
            if not last:
                SF_(n + 1, cur)
                KPR_(n + 1, gF, True)
                P_(n + 1)
                Q_(n + 1)
            if n >= 1:
                OTE_(n - 1)
            NORM_(n)
        OGT_(NCH - 1)
        OTE_(NCH - 1)
        if h == 0:
            dump('ss1a', ss1[0], 1, F32)
            dump('ss1b', ss1[1], 1, F32)
            dump('rs1a', rs1[0], 1, F32)
            dump('rs1b', rs1[1], 1, F32)
            dump('og0', og[0], 512, BF16)
            dump('og1', og[1], 512, BF16)
            dump('oT0', oT, 4 * NT, BF16)
            checkpoint(4)
        if h == 3:
            for j_ in make_gate_jobs(3, (2, 3)):
                j_()

    dump('oT', oT, 16 * NT, BF16)
    checkpoint(5)
    P1b = Alloc(PERS_END + 16 * NT // 2, NW)
    PT = P1b.bf(8 * NT, ("p (k t) -> p k t", dict(k=8)))
    smr = [P1b.bf(512) for _ in range(2)]
    wrd_v = wrd_d.rearrange("(k p) c -> p k c", p=128)
    def rd_loads(ct):
        return (load_w(win_v[:, :, OMR + ct * 128:OMR + (ct + 1) * 128], "p (k c) -> p k c", k=8),
                load_w(wrd_v[:, :, ct * 128:(ct + 1) * 128], "p (k c) -> p k c", k=16))
    rd_next = rd_loads(0)
    for ct in range(8):
        (slot_m, wm), (slot_r, wr) = rd_next
        if ct + 1 < 8:
            rd_next = rd_loads(ct + 1)
        for tb in range(4):
            b2 = tb % 2
            pm = PS[b2]
            S.mm([(pm.ap, wm[:, k, :], hT.ap[:, k, tb * 512:(tb + 1) * 512]) for k in range(8)], [slot_m.reg(), hT.reg()], [pm.reg()])
            S.act(lambda e, b2=b2, pm=pm: e.activation(out=smr[b2].ap, in_=pm.ap, func=AF.Sigmoid), [pm.reg()], [smr[b2].reg()])
            pr = PS[2 + b2]
            S.mm([(pr.ap, wr[:, k, :], oT.ap[:, k, tb * 512:(tb + 1) * 512]) for k in range(16)], [slot_r.reg(), oT.reg()], [pr.reg()])
            S.dve(lambda e, b2=b2, pr=pr, ct=ct, tb=tb: e.tensor_tensor(out=PT.ap[:, ct, tb * 512:(tb + 1) * 512], in0=pr.ap, in1=smr[b2].ap, op=ALU.mult),
                  [pr.reg(), smr[b2].reg()], [PT.reg(ct * NT + tb * 512, ct * NT + (tb + 1) * 512)])

    dump('PTret', PT, 8 * NT, BF16)
    checkpoint(6)
    P2 = Alloc(PERS_END, PERS_END + 16 * NT // 2)
    yT = P2.bf(10 * NT, ("p (k t) -> p k t", dict(k=10)))
    hh = [P2.f32(NT) for _ in range(2)]
    sgq = [P2.bf(512) for _ in range(2)]
    xq_p2 = [P2.f32(512) for _ in range(2)]
    P2c = Alloc(P1b.p, NW)
    xq_a = [P2c.f32(512) for _ in range(4)]
    xcb0 = P2c.bf(NT)
    tgd = [[P2c.f32(NT // 2) for _ in range(2)] for _ in range(2)]
    aad = [[P2c.f32(NT // 2) for _ in range(2)] for _ in range(2)]
    hcol = P2c.f32(60)
    onep = P2c.f32(1)
    lnhalf = P2c.f32(1)
    xq_c = P2c.f32(512)
    S.dve(lambda e: e.tensor_scalar(out=hcol.ap[:, 0:40], in0=cols.ap[:, C_BA:C_BA + 40], scalar1=0.5, scalar2=None, op0=ALU.mult), [cols.reg()], [hcol.reg(0, 40)])
    S.dve(lambda e: e.tensor_scalar(out=hcol.ap[:, 40:60], in0=scl.ap, scalar1=0.5, scalar2=None, op0=ALU.mult), [scl.reg()], [hcol.reg(40, 60)])
    S.dve(lambda e: e.memset(onep.ap, 0.25), [], [onep.reg()])
    S.dve(lambda e: e.memset(lnhalf.ap, -0.6931471805599453), [], [lnhalf.reg()])

    def q3(buf, tb, c0, c1):
        ap = buf.ap
        return bass.AP(ap.tensor, ap.offset + tb * 512 + c0, [list(ap.ap[0]), [64, 8], [1, c1 - c0]])

    def qseq(buf, tb, r0, nr, c):
        ap = buf.ap
        return bass.AP(ap.tensor, ap.offset + tb * 512 + r0 * 64 + c, [list(ap.ap[0]), [256, 2], [64, nr]])

    def rev(ap2d, n):
        return bass.AP(ap2d.tensor, ap2d.offset + n - 1, [list(ap2d.ap[0]), [-1, n]])

    LW = Alloc(ring[0].lo // 4, ring[NSLOT - 1].hi // 4)
    lwsets = [(LW.bf(1024), LW.bf(256), LW.bf(256), LW.bf(1024)) for _ in range(2)]
    xcbs = [xcb0, xcb0]
    trd = [[t_, t_] for t_ in (LW.f32(NT // 2), LW.f32(NT // 2))]
    xq_l = LW.f32(512)
    xcq = [xq_a, [xq_p2[0], xq_p2[1], xq_c, xq_l]]

    def lru_loads(cb):
        bx_, ba_, bw_, bg_ = lwsets[cb % 2]
        vx = bx_.ap.rearrange("p (k c) -> p k c", k=8)
        va = ba_.ap.rearrange("p (d j) -> p d j", d=2)
        vw = bw_.ap.rearrange("p (d j) -> p d j", d=2)
        vg = bg_.ap.rearrange("p (k c) -> p k c", k=8)
        S.dma("pool", vx, win_v[:, :, OXL + cb * 128:OXL + (cb + 1) * 128], [], [bx_.reg()])
        S.dma("pool", va, wa_d[:, cb].rearrange("d i j -> i d j"), [], [ba_.reg()])
        S.dma("pool", vw, wx_d[:, cb].rearrange("d i j -> i d j"), [], [bw_.reg()])
        S.dma("pool", vg, win_v[:, :, OGL + cb * 128:OGL + (cb + 1) * 128], [], [bg_.reg()])
        return (bx_, vx), (ba_, va), (bw_, vw), (bg_, vg)

    XB = [PS[0], PS[1], PS[6], PS[7]]

    def lru_front_pe(cb, wts):
        (slot_x, wxl) = wts[0]
        for tb in range(4):
            pb = XB[tb]
            S.mm([(pb.ap, wxl[:, k, :], hT.ap[:, k, tb * 512:(tb + 1) * 512]) for k in range(8)], [slot_x.reg(), hT.reg()], [pb.reg()])

    def lru_front_conv(cb, tbs=(0, 1, 2, 3)):
        xq = xcq[cb % 2]
        cw = lambda j: cols.ap[:, C_CW + cb * 4 + j:C_CW + cb * 4 + j + 1]
        fwc = lambda j: fw.ap[:, cb * 4 + j:cb * 4 + j + 1]
        cbias = cols.ap[:, C_CB + cb:C_CB + cb + 1]
        for tb in tbs:
            pb = XB[tb]
            xc = xq[tb]
            xr = xc.reg()
            w2, w1, w0, w3 = cw(2), cw(1), cw(0), cw(3)
            S.act(lambda e, pb=pb, xc=xc, w2=w2: e.activation(out=xc.ap, in_=pb.ap, func=AF.Identity, scale=w2, bias=cbias),
                  [pb.reg(), cols.reg()], [xr])
            for (wj, o0, o1, i0, i1) in ((w1, 1, 64, 0, 63), (w0, 2, 64, 0, 62), (w3, 0, 63, 1, 64)):
                S.dve(lambda e, pb=pb, xc=xc, wj=wj, o0=o0, o1=o1, i0=i0, i1=i1: e.scalar_tensor_tensor(
                    out=q3(xc, 0, o0, o1), in0=q3(pb, 0, i0, i1), scalar=wj, in1=q3(xc, 0, o0, o1), op0=ALU.mult, op1=ALU.add),
                    [pb.reg(), xr, cols.reg()], [xr])
            f1, f0, f3 = fwc(1), fwc(0), fwc(3)
            for (fj, oc, ic, orow, irow) in ((f1, 0, 63, 1, 0), (f0, 0, 62, 1, 0), (f0, 1, 63, 1, 0), (f3, 63, 0, 0, 1)):
                S.dve(lambda e, pb=pb, xc=xc, fj=fj, oc=oc, ic=ic, orow=orow, irow=irow: e.scalar_tensor_tensor(
                    out=qseq(xc, 0, orow, 3, oc), in0=qseq(pb, 0, irow, 3, ic), scalar=fj, in1=qseq(xc, 0, orow, 3, oc), op0=ALU.mult, op1=ALU.add),
                    [pb.reg(), xr, fw.reg()], [xr])

    def lru_casts(cb, tbs=(0, 1, 2, 3)):
        xq = xcq[cb % 2]
        xcb = xcbs[cb % 2]
        for tb in tbs:
            S.act(lambda e, tb=tb: e.activation(out=xcb.ap[:, tb * 512:(tb + 1) * 512], in_=xq[tb].ap, func=AF.Copy),
                  [xq[tb].reg()], [xcb.reg(tb * 512, (tb + 1) * 512)])

    def lru_step(cb, st, wts, part):
        (slot_a, wav), (slot_w, wxv) = wts[1], wts[2]
        xq = xcq[cb % 2]
        xcb = xcbs[cb % 2]
        items = []
        for d in range(2):
            half = st if d == 0 else 1 - st
            qs = [2 * half, 2 * half + 1] if d == 0 else [2 * half + 1, 2 * half]
            items.append((d, half, qs))
        col = lambda base, d: hcol.ap[:, base + d * 10 + cb:base + d * 10 + cb + 1]
        lr = lambda b_, half, tb: b_.reg((tb - 2 * half) * 512, (tb - 2 * half + 1) * 512)
        la = lambda b_, half, tb: b_.ap[:, (tb - 2 * half) * 512:(tb - 2 * half + 1) * 512]
        gq = lambda b_, tb: b_.ap[:, tb * 512:(tb + 1) * 512]
        gr = lambda b_, tb: b_.reg(tb * 512, (tb + 1) * 512)
        pi = [0]
        doA = (part == 'A')
        for (d, half, qs) in (items if doA else []):
            for tb in qs:
                pa = PS[2 + pi[0] % 2]
                pg = PS[4 + pi[0] % 2]
                pi[0] += 1
                S.mm([(pa.ap, wav[:, d, :], gq(xcb, tb))], [slot_a.reg(), gr(xcb, tb)], [pa.reg()])
                S.act(lambda e, pa=pa, tb=tb, d=d, half=half: e.activation(out=la(trd[d][st], half, tb), in_=pa.ap, func=AF.Tanh, scale=0.5, bias=col(0, d)),
                      [pa.reg(), hcol.reg()], [lr(trd[d][st], half, tb)])
                S.mm([(pg.ap, wxv[:, d, :], gq(xcb, tb))], [slot_w.reg(), gr(xcb, tb)], [pg.reg()])
                S.act(lambda e, pg=pg, tb=tb, d=d, half=half: e.activation(out=la(tgd[d][st], half, tb), in_=pg.ap, func=AF.Tanh, scale=0.5, bias=col(20, d)),
                      [pg.reg(), hcol.reg()], [lr(tgd[d][st], half, tb)])
        for (d, half, qs) in (items if doA else []):
            for tb in qs:
                S.dve(lambda e, tb=tb, d=d, half=half: e.scalar_tensor_tensor(out=la(tgd[d][st], half, tb), in0=la(tgd[d][st], half, tb), scalar=1.0, in1=xq[tb].ap, op0=ALU.add, op1=ALU.mult),
                      [lr(tgd[d][st], half, tb), xq[tb].reg()], [lr(tgd[d][st], half, tb)])
        for (d, half, qs) in (items if doA else []):
            for tb in qs:
                S.act(lambda e, tb=tb, d=d, half=half: e.activation(out=la(aad[d][st], half, tb), in_=la(trd[d][st], half, tb), func=AF.Exp, scale=col(40, d), bias=col(40, d)),
                      [lr(trd[d][st], half, tb), hcol.reg()], [lr(aad[d][st], half, tb)])
        for (d, half, qs) in (items if doA else []):
            for tb in qs:
                S.dve(lambda e, tb=tb, d=d, half=half: e.scalar_tensor_tensor(out=la(trd[d][st], half, tb), in0=la(aad[d][st], half, tb), scalar=0.9999995, in1=la(aad[d][st], half, tb), op0=ALU.min, op1=ALU.mult),
                      [lr(aad[d][st], half, tb)], [lr(trd[d][st], half, tb)])
        if part == 'A':
            return
        for (d, half, qs) in items:
            for tb in qs:
                S.act(lambda e, tb=tb, d=d, half=half: e.activation(out=la(trd[d][st], half, tb), in_=la(trd[d][st], half, tb), func=AF.Sqrt, scale=-0.25, bias=onep.ap),
                      [lr(trd[d][st], half, tb), onep.reg()], [lr(trd[d][st], half, tb)])
        for (d, half, qs) in items:
            ap = aad[d][st].ap
            if d == 0:
                off, cnt = (256, 3) if half == 0 else (0, 4)
            else:
                off, cnt = (255, 4) if half == 0 else (255, 3)
            bv = bass.AP(ap.tensor, ap.offset + off, [list(ap.ap[0]), [256, cnt]])
            S.dve(lambda e, bv=bv: e.tensor_scalar(out=bv, in0=bv, scalar1=lrukeep, scalar2=None, op0=ALU.mult), [aad[d][st].reg(), smalls.reg()], [aad[d][st].reg()])
        for i_ in range(2):
            for (d, half, qs) in items:
                tb = qs[i_]
                h0 = cols.ap[:, C_H0 + d * 10 + cb:C_H0 + d * 10 + cb + 1]
                S.dve(lambda e, tb=tb, d=d, half=half: e.tensor_tensor(out=la(tgd[d][st], half, tb), in0=la(tgd[d][st], half, tb), in1=la(trd[d][st], half, tb), op=ALU.mult),
                      [lr(tgd[d][st], half, tb), lr(trd[d][st], half, tb)], [lr(tgd[d][st], half, tb)])
                first = (d == 0 and tb == 0) or (d == 1 and tb == 3)
                if first:
                    init, ireg = h0, cols.reg()
                elif d == 0:
                    init, ireg = hh[0].ap[:, tb * 512 - 1:tb * 512], hh[0].reg(tb * 512 - 1, tb * 512)
                else:
                    init, ireg = hh[1].ap[:, (tb + 1) * 512:(tb + 1) * 512 + 1], hh[1].reg((tb + 1) * 512, (tb + 1) * 512 + 1)
                if d == 0:
                    S.dve(lambda e, tb=tb, init=init, half=half: e.tensor_tensor_scan(out=gq(hh[0], tb), data0=la(aad[0][st], half, tb), data1=la(tgd[0][st], half, tb), initial=init, op0=ALU.mult, op1=ALU.add),
                          [lr(aad[0][st], half, tb), lr(tgd[0][st], half, tb), ireg], [gr(hh[0], tb)])
                else:
                    S.dve(lambda e, tb=tb, init=init, half=half: e.tensor_tensor_scan(out=rev(gq(hh[1], tb), 512), data0=rev(la(aad[1][st], half, tb), 512), data1=rev(la(tgd[1][st], half, tb), 512), initial=init, op0=ALU.mult, op1=ALU.add),
                          [lr(aad[1][st], half, tb), lr(tgd[1][st], half, tb), ireg], [gr(hh[1], tb)])

    def lru_back(cb, wts):
        (slot_g, wgl) = wts[3]
        fo = cb * 8
        S.pool(lambda e: e.tensor_copy(out=lruout.ap[:, fo:fo + 4], in_=bass.AP(hh[0].ap.tensor, hh[0].ap.offset + 255, [list(hh[0].ap.ap[0]), [256, 4]])),
               [hh[0].reg()], [lruout.reg(fo, fo + 4)])
        S.pool(lambda e: e.tensor_copy(out=lruout.ap[:, fo + 4:fo + 8], in_=bass.AP(hh[1].ap.tensor, hh[1].ap.offset, [list(hh[1].ap.ap[0]), [256, 4]])),
               [hh[1].reg()], [lruout.reg(fo + 4, fo + 8)])
        for tb in range(4):
            b2 = tb % 2
            pb = PS[2 + b2]
            S.mm([(pb.ap, wgl[:, k, :], hT.ap[:, k, tb * 512:(tb + 1) * 512]) for k in range(8)], [slot_g.reg(), hT.reg()], [pb.reg()])
            S.act(lambda e, pb=pb, b2=b2: e.activation(out=sgq[b2].ap, in_=pb.ap, func=AF.Silu), [pb.reg()], [sgq[b2].reg()])
            S.pool(lambda e, tb=tb: e.tensor_tensor(out=hh[0].ap[:, tb * 512:(tb + 1) * 512], in0=hh[0].ap[:, tb * 512:(tb + 1) * 512], in1=hh[1].ap[:, tb * 512:(tb + 1) * 512], op=ALU.add),
                   [hh[0].reg(tb * 512, (tb + 1) * 512), hh[1].reg(tb * 512, (tb + 1) * 512)], [hh[0].reg(tb * 512, (tb + 1) * 512)])
            S.pool(lambda e, tb=tb, b2=b2: e.tensor_tensor(out=yT.ap[:, cb, tb * 512:(tb + 1) * 512], in0=hh[0].ap[:, tb * 512:(tb + 1) * 512], in1=sgq[b2].ap, op=ALU.mult),
                  [hh[0].reg(tb * 512, (tb + 1) * 512), sgq[b2].reg()], [yT.reg(cb * NT + tb * 512, cb * NT + (tb + 1) * 512)])

    wts_cur = lru_loads(0)
    lru_front_pe(0, wts_cur)
    lru_front_conv(0)
    lru_casts(0)
    for cb in range(10):
        nxt = cb + 1 < 10
        lru_step(cb, 0, wts_cur, 'A')
        wts_nxt = None
        if nxt:
            wts_nxt = lru_loads(cb + 1)
            lru_front_pe(cb + 1, wts_nxt)
            lru_front_conv(cb + 1, (0,))
        lru_step(cb, 0, wts_cur, 'B')
        if nxt:
            lru_front_conv(cb + 1, (1,))
        lru_step(cb, 1, wts_cur, 'A')
        if nxt:
            lru_casts(cb + 1, (0, 1))
            lru_front_conv(cb + 1, (2,))
            lru_casts(cb + 1, (2,))
        lru_step(cb, 1, wts_cur, 'B')
        if nxt:
            lru_front_conv(cb + 1, (3,))
        lru_back(cb, wts_cur)
        if nxt:
            lru_casts(cb + 1, (3,))
        if cb == 0:
            dump('hf0', hh[0], NT, F32)
            dump('hb0', hh[1], NT, F32)
        wts_cur = wts_nxt
    S.dma("sp", slru_d, lruout.ap, [lruout.reg()], [])

    dump('yT', yT, 10 * NT, BF16)
    checkpoint(7)
    wld_v = wld_d.rearrange("(k p) c -> p k c", p=128)
    P2d = Alloc(P1b.p, NW)
    sml = [P2d.f32(512) for _ in range(2)]
    tml = [P2d.f32(512) for _ in range(2)]
    def ld_loads(ct):
        return (load_w(win_v[:, :, OML + ct * 128:OML + (ct + 1) * 128], "p (k c) -> p k c", k=8),
                load_w(wld_v[:, :, ct * 128:(ct + 1) * 128], "p (k c) -> p k c", k=10))
    ld_next = ld_loads(0)
    WOA = Alloc(hh[0].lo // 4, PERS_END + 16 * NT // 2)
    wo = WOA.bf(8 * 1024, ("p (k c) -> p k c", dict(k=8)))
    wout_v = wout_d.rearrange("(k p) c -> p k c", p=128)
    S.dma("pool", wo.ap, wout_v, [], [wo.reg()])
    for ct in range(8):
        (slot_m, wm), (slot_r, wr) = ld_next
        if ct + 1 < 8:
            ld_next = ld_loads(ct + 1)
        for tb in range(4):
            b2 = tb % 2
            pm = PS[b2]
            S.mm([(pm.ap, wm[:, k, :], hT.ap[:, k, tb * 512:(tb + 1) * 512]) for k in range(8)], [slot_m.reg(), hT.reg()], [pm.reg()])
            S.act(lambda e, b2=b2, pm=pm: e.activation(out=sml[b2].ap, in_=pm.ap, func=AF.Sigmoid), [pm.reg()], [sml[b2].reg()])
            pr = PS[2 + b2]
            S.mm([(pr.ap, wr[:, k, :], yT.ap[:, k, tb * 512:(tb + 1) * 512]) for k in range(10)], [slot_r.reg(), yT.reg()], [pr.reg()])
            S.dve(lambda e, b2=b2, pr=pr: e.tensor_tensor(out=tml[b2].ap, in0=pr.ap, in1=sml[b2].ap, op=ALU.mult), [pr.reg(), sml[b2].reg()], [tml[b2].reg()])
            S.pool(lambda e, b2=b2, ct=ct, tb=tb: e.tensor_tensor(out=PT.ap[:, ct, tb * 512:(tb + 1) * 512], in0=PT.ap[:, ct, tb * 512:(tb + 1) * 512], in1=tml[b2].ap, op=ALU.add),
                   [PT.reg(ct * NT + tb * 512, ct * NT + (tb + 1) * 512), tml[b2].reg()], [PT.reg(ct * NT + tb * 512, ct * NT + (tb + 1) * 512)])

    dump('PT', PT, 8 * NT, BF16)
    checkpoint(8)
    P3 = Alloc(PERS_END, PERS_END + 16 * NT // 2)
    gate_bc = P3.f32(1024)
    fnw_bc = P3.f32(1024)
    x3 = [P3.f32(1024) for _ in range(4)]
    y3 = [P3.f32(1024) for _ in range(4)]
    junk3 = P2d.bf(1024)
    ss3 = [P2d.f32(1) for _ in range(2)]
    rs3 = [P2d.f32(1) for _ in range(2)]
    assert P3.p <= PERS_END + 10 * NT // 2, "output-phase tiles must stay inside the dead yT region"
    wout_v = wout_d.rearrange("(k p) c -> p k c", p=128)
    S.dma("sp", gate_bc.ap, bass.AP(gate_d.tensor, 0, [[0, 128], [1, 1024]]), [("scr2", 0, 4096)], [gate_bc.reg()])
    S.dma("sp", fnw_bc.ap, bass.AP(rows_d.tensor, 1024, [[0, 128], [1, 1024]]), [], [fnw_bc.reg()])
    def p3_load(n):
        S.dma("sp", x3[n % 4].ap, x_d[n * 128:(n + 1) * 128, :], [], [x3[n % 4].reg()])

    def p3_front(n):
        b = n % 2
        yb = y3[n % 4]
        if n + 3 < NCH:
            p3_load(n + 3)
        for half in range(2):
            pb = PS[2 * b + half]
            S.mm([(pb.ap, PT.ap[:, k, n * 128:(n + 1) * 128], wo.ap[:, k, half * 512:(half + 1) * 512]) for k in range(8)],
                 [PT.reg(), wo.reg()], [pb.reg()])
            S.dve(lambda e, half=half, pb=pb: e.tensor_tensor(out=yb.ap[:, half * 512:(half + 1) * 512], in0=pb.ap, in1=gate_bc.ap[:, half * 512:(half + 1) * 512], op=ALU.mult),
                  [pb.reg(), gate_bc.reg()], [yb.reg(half * 512, (half + 1) * 512)])
        S.pool(lambda e: e.tensor_tensor(out=yb.ap, in0=yb.ap, in1=x3[n % 4].ap, op=ALU.add), [yb.reg(), x3[n % 4].reg()], [yb.reg()])

    def p3_back(n):
        b = n % 2
        yb = y3[n % 4]
        S.act(lambda e: e.activation(out=junk3.ap, in_=yb.ap, func=AF.Square, accum_out=ss3[b].ap), [yb.reg()], [junk3.reg(), ss3[b].reg()])
        S.act(lambda e: e.activation(out=rs3[b].ap, in_=ss3[b].ap, func=AF.Ln, scale=1.0 / D, bias=eps_c.ap), [ss3[b].reg(), eps_c.reg()], [rs3[b].reg()])
        S.act(lambda e: e.activation(out=rs3[b].ap, in_=rs3[b].ap, func=AF.Exp, scale=-0.5), [rs3[b].reg()], [rs3[b].reg()])
        S.dve(lambda e: e.scalar_tensor_tensor(out=yb.ap, in0=yb.ap, scalar=rs3[b].ap, in1=fnw_bc.ap, op0=ALU.mult, op1=ALU.mult),
              [yb.reg(), rs3[b].reg(), fnw_bc.reg()], [yb.reg()])
        S.dma("sp", y_d[n * 128:(n + 1) * 128, :], yb.ap, [yb.reg()], [])

    for n_ in range(3):
        p3_load(n_)
    p3_front(0)
    for n in range(NCH):
        if n + 1 < NCH:
            p3_front(n + 1)
        p3_back(n)

    S.emit_all(sems)
    es.close()
    S.dumps = dumps
    return nc, S


_CACHE = {}


def _consts():
    c = np.zeros((128, 257), np.float32)
    c[:, 0:128] = np.eye(128, dtype=np.float32)
    c[:, 128:256] = np.arange(128, dtype=np.float32)[None, :]
    c[:, 256] = np.arange(128, dtype=np.float32)
    return c


def _colv(v, nt):
    return np.ascontiguousarray(np.asarray(v, np.float32).reshape(nt, 128).T)


def make_in_maps(x_prompt, x_sample, state_ret, state_lru, c, c_ctx, norm_w, w_ada, b_ada, w_in,
                 ret_decay_logit, ret_gn_w, w_ret_down, conv_w, conv_b, lru_wa, lru_ba, lru_wx, lru_bx,
                 lru_a_param, w_lru_down, w_out, final_norm_w):
    f = lambda a: np.ascontiguousarray(np.asarray(a, np.float32))
    consts = _consts()
    rows = np.stack([f(b_ada)[0, 2048:3072], f(final_norm_w)], 0)
    shared = dict(consts=consts, rows=rows, w_ada=f(w_ada)[0], w_in=f(w_in)[0], w_ret_down=f(w_ret_down)[0],
                  w_lru_down=f(w_lru_down)[0], w_out=f(w_out)[0], lru_wa=f(lru_wa)[0], lru_wx=f(lru_wx)[0])
    in_maps = []
    for core in range(8):
        cols = np.zeros((128, NCOLS), np.float32)
        cols[:, C_NW:C_NW + 8] = _colv(norm_w[0], 8)
        cols[:, C_BSH:C_BSH + 8] = _colv(b_ada[0, 0:1024], 8)
        cols[:, C_BSC:C_BSC + 8] = _colv(b_ada[0, 1024:2048], 8)
        cols[:, C_GN:C_GN + 16] = _colv(ret_gn_w[0], 16)
        cw = np.asarray(conv_w, np.float32)[0]
        for cb in range(10):
            for j in range(4):
                cols[:, C_CW + cb * 4 + j] = cw[j, cb * 128:(cb + 1) * 128]
        cols[:, C_CB:C_CB + 10] = _colv(conv_b[0], 10)
        for d in range(2):
            cols[:, C_BA + d * 10:C_BA + d * 10 + 10] = _colv(lru_ba[0, d], 10)
            cols[:, C_BX + d * 10:C_BX + d * 10 + 10] = _colv(lru_bx[0, d], 10)
            cols[:, C_AP + d * 10:C_AP + d * 10 + 10] = _colv(lru_a_param[0, d], 10)
        smalls = np.zeros((1, NSM), np.float32)
        smalls[0, 0:8] = np.asarray(ret_decay_logit, np.float32)[0].reshape(8)
        if core < 4:
            x = f(x_sample[core])
            cols[:, C_COND:C_COND + 8] = _colv(c[core], 8)
            init_ret = f(state_ret[core, 0])
            for d in range(2):
                cols[:, C_H0 + d * 10:C_H0 + d * 10 + 10] = _colv(state_lru[core, 0, d], 10)
            smalls[0, 8:24] = 1.0
            smalls[0, 24:40] = 1.0
            smalls[0, 40] = 0.0
            smalls[0, 41] = 1.0
        else:
            p0 = (core - 4) * 4
            x = np.zeros((NT, D), np.float32)
            x[:1024] = np.asarray(x_prompt[p0:p0 + 4], np.float32).reshape(1024, D)
            cols[:, C_COND:C_COND + 8] = _colv(c_ctx, 8)
            init_ret = np.zeros((2, 4, 256, 512), np.float32)
            kf = np.array([1.0 if (n % 2 == 1) else 0.0 for n in range(16)], np.float32)
            kf[0] = 1.0
            kb = np.array([1.0 if (n % 2 == 0) else 0.0 for n in range(16)], np.float32)
            kb[15] = 1.0
            smalls[0, 8:24] = kf
            smalls[0, 24:40] = kb
            smalls[0, 40] = 1.0
            smalls[0, 41] = 0.0
        m = dict(shared)
        m.update(x=x, cols=cols, smalls=smalls, init_ret=init_ret)
        in_maps.append(m)
    return in_maps


def kernel(**inputs):
    if "nc" not in _CACHE:
        _CACHE["nc"] = build_program(DEBUG)[0]
    nc = _CACHE["nc"]
    in_maps = make_in_maps(**inputs)
    res = run_bass_kernel_spmd(nc, in_maps, core_ids=list(range(8)))
    r = res.results
    y_sample = np.stack([r[i]["y"] for i in range(4)], 0).astype(np.float32)
    y_prompt = np.concatenate([r[i]["y"][:1024].reshape(4, 256, D) for i in range(4, 8)], 0).astype(np.float32)
    st_ret = np.concatenate([r[i]["st_ret"] for i in range(4, 8)], 0)[:, None].astype(np.float32)
    lr = []
    for i in range(4, 8):
        a = r[i]["st_lru"].reshape(128, 10, 2, 4)
        lr.append(a.transpose(3, 2, 1, 0).reshape(4, 2, 1280))
    st_lru = np.concatenate(lr, 0)[:, None].astype(np.float32)
    if DEBUG:
        _CACHE["dbg"] = r
    return (y_prompt, y_sample, st_ret, st_lru)
```

```python
import numpy as np
import concourse.bass as bass
import concourse.mybir as mybir
from concourse.bass_utils import run_bass_kernel_spmd

F32 = mybir.dt.float32
BF16 = mybir.dt.bfloat16
AF = mybir.ActivationFunctionType
ALU = mybir.AluOpType

D = 1024
DIN = 10752
NT = 2048
NCH = 16
EPS = 1e-6
OQ, OK_, OV, OG, OXL, OGL, OMR, OML = 0, 1024, 2048, 4096, 6144, 7424, 8704, 9728
NCOLS = 178
C_NW, C_COND, C_BSH, C_BSC, C_GN, C_CW, C_CB, C_BA, C_BX, C_AP, C_H0 = 0, 8, 16, 24, 32, 48, 88, 98, 118, 138, 158
NSM = 8 + 34

DEBUG = False


class Buf:
    def __init__(self, ap, arena, lo, hi, esize):
        self.ap = ap
        self.arena = arena
        self.lo = lo
        self.hi = hi
        self.esize = esize

    def reg(self, elo=None, ehi=None):
        if elo is None:
            return (self.arena, self.lo, self.hi)
        return (self.arena, self.lo + elo * self.esize, self.lo + ehi * self.esize)


class Op:
    __slots__ = ("eng", "emit", "deps", "is_dma", "signal", "sigcount", "dsem", "dval", "dprev", "gid")


class Sched:
    ENGS = ("pe", "act", "dve", "pool", "sp")

    def __init__(self, nc):
        self.nc = nc
        self.ops = []
        self.eng_ops = {e: [] for e in self.ENGS}
        self.recs = {}
        self.enabled = True

    def _access(self, opid, eng, is_dma, regs, is_write, deps):
        for (a, lo, hi) in regs:
            if a.startswith("ps"):
                r = self.recs.get(a)
                if r is not None:
                    rop, rw, reng = r
                    if rop != opid:
                        if is_write or rw or reng != eng:
                            deps.add(rop)
                    else:
                        is_write = is_write or rw
                self.recs[a] = (opid, is_write, eng)
                continue
            recs = self.recs.get(a)
            if recs is None:
                recs = []
                self.recs[a] = recs
            new = []
            for r in recs:
                rlo, rhi, rop, rw, reng, rdma = r
                if rhi <= lo or rlo >= hi:
                    new.append(r)
                    continue
                if (is_write or rw) and rop != opid:
                    deps.add(rop)
                covered = lo <= rlo and rhi <= hi
                if covered and is_write:
                    continue
                if covered and (not is_write) and (not rw) and reng == eng and not rdma and not is_dma:
                    continue
                new.append(r)
            new.append((lo, hi, opid, is_write, eng, is_dma))
            self.recs[a] = new

    def add(self, eng, emit, reads=(), writes=(), dma=False):
        if not self.enabled:
            return -1
        op = Op()
        op.eng = eng
        op.emit = emit
        op.is_dma = dma
        op.signal = False
        opid = len(self.ops)
        deps = set()
        self._access(opid, eng, dma, writes, True, deps)
        self._access(opid, eng, dma, reads, False, deps)
        op.deps = deps
        self.ops.append(op)
        self.eng_ops[eng].append(opid)
        return opid

    def pe(self, emit, reads, writes):
        return self.add("pe", emit, reads, writes)

    def act(self, emit, reads, writes):
        return self.add("act", emit, reads, writes)

    def dve(self, emit, reads, writes):
        return self.add("dve", emit, reads, writes)

    def pool(self, emit, reads, writes):
        return self.add("pool", emit, reads, writes)

    def dma(self, q, out, in_, reads, writes, **kw):
        return self.add(q, lambda e: e.dma_start(out=out, in_=in_, **kw), reads, writes, dma=True)

    def mm(self, mms, reads, writes):
        n = len(mms)

        def emit(e):
            ins = None
            for i, (o, l, r) in enumerate(mms):
                ins = e.matmul(o, l, r, start=(i == 0), stop=(i == n - 1))
            return ins
        return self.add("pe", emit, reads, writes)

    def emit_all(self, sems):
        nc = self.nc
        ops = self.ops
        for op in ops:
            for d in op.deps:
                if not ops[d].is_dma:
                    ops[d].signal = True
        for e in self.ENGS:
            comp = [i for i in self.eng_ops[e] if not ops[i].is_dma]
            if comp:
                ops[comp[-1]].signal = True
        esem = {e: sems.pop() for e in ("pe", "act", "dve", "pool")}
        final = {}
        for e in ("pe", "act", "dve", "pool"):
            c = 0
            for i in self.eng_ops[e]:
                op = ops[i]
                if op.is_dma:
                    continue
                if op.signal:
                    c += 1
                op.sigcount = c if op.signal else None
            final[esem[e]] = c
        KD = 20
        dpool = {q: [sems.pop() for _ in range(KD)] for q in ("sp", "pool")}
        for q in ("sp", "pool"):
            cnt = [0] * KD
            k = 0
            for i in self.eng_ops[q]:
                op = ops[i]
                if not op.is_dma:
                    continue
                s = k % KD
                op.dsem = dpool[q][s]
                op.dprev = cnt[s] * 16
                cnt[s] += 1
                op.dval = cnt[s] * 16
                k += 1
            for s in range(KD):
                final[dpool[q][s]] = cnt[s] * 16
        engobj = {"pe": "tensor", "act": "scalar", "dve": "vector", "pool": "gpsimd", "sp": "sync"}
        self.nwaits = 0

        self.trace = {}

        def run_engine(ename, e):
            waited = {}
            tr = []
            self.trace[ename] = tr

            def wait(sem, val):
                if val <= 0:
                    return
                if waited.get(sem, 0) >= val:
                    return
                waited[sem] = val
                e.wait_ge(sem, val)
                tr.append(("w", id(sem), val))
                self.nwaits += 1
            for i in self.eng_ops[ename]:
                op = ops[i]
                need = {}
                for d in op.deps:
                    dop = ops[d]
                    if dop.is_dma:
                        sm, vl = dop.dsem, dop.dval
                    else:
                        if dop.eng == "pe" and ename == "pe":
                            continue
                        sm, vl = esem[dop.eng], dop.sigcount
                    if need.get(id(sm), (None, 0))[1] < vl:
                        need[id(sm)] = (sm, vl)
                for sm, vl in need.values():
                    wait(sm, vl)
                if op.is_dma:
                    wait(op.dsem, op.dprev)
                    ins = op.emit(e)
                    ins.then_inc(op.dsem, 16)
                    tr.append(("i", id(op.dsem), 16, i))
                else:
                    ins = op.emit(e)
                    if op.signal:
                        ins.then_inc(esem[ename], 1)
                        tr.append(("i", id(esem[ename]), 1, i))
                    else:
                        tr.append(("n", i))
            if ename == "sp":
                for sem, val in final.items():
                    wait(sem, val)

        with nc.Block() as block:
            @block.tensor
            def _(e):
                run_engine("pe", e)

            @block.scalar
            def _(e):
                run_engine("act", e)

            @block.vector
            def _(e):
                run_engine("dve", e)

            @block.gpsimd
            def _(e):
                run_engine("pool", e)

            @block.sync
            def _(e):
                run_engine("sp", e)


def bcast_ap(ap, dims):
    return bass.AP(ap.tensor, ap.offset, [list(ap.ap[0])] + [list(d) for d in dims])


class _Stop(Exception):
    pass


def build_program(debug=False, limit=None):
    nc = bass.Bass("TRN2", target_bir_lowering=False)
    dt = nc.dram_tensor
    x_d = dt("x", [NT, D], F32, kind="ExternalInput").ap()
    cols_d = dt("cols", [128, NCOLS], F32, kind="ExternalInput").ap()
    rows_d = dt("rows", [2, D], F32, kind="ExternalInput").ap()
    sm_d = dt("smalls", [1, NSM], F32, kind="ExternalInput").ap()
    consts_d = dt("consts", [128, 257], F32, kind="ExternalInput").ap()
    iret_d = dt("init_ret", [2, 4, 256, 512], F32, kind="ExternalInput").ap()
    wada_d = dt("w_ada", [D, 3 * D], F32, kind="ExternalInput").ap()
    win_d = dt("w_in", [D, DIN], F32, kind="ExternalInput").ap()
    wrd_d = dt("w_ret_down", [2048, D], F32, kind="ExternalInput").ap()
    wld_d = dt("w_lru_down", [1280, D], F32, kind="ExternalInput").ap()
    wout_d = dt("w_out", [D, D], F32, kind="ExternalInput").ap()
    wa_d = dt("lru_wa", [2, 10, 128, 128], F32, kind="ExternalInput").ap()
    wx_d = dt("lru_wx", [2, 10, 128, 128], F32, kind="ExternalInput").ap()
    y_d = dt("y", [NT, D], F32, kind="ExternalOutput").ap()
    sret_d = dt("st_ret", [4, 2, 4, 256, 512], F32, kind="ExternalOutput").ap()
    slru_d = dt("st_lru", [128, 80], F32, kind="ExternalOutput").ap()
    tb_d = dt("tb_scr", [4 * 16 * 128, 1024], BF16, kind="Internal").ap()
    gate_d = dt("gate_scr", [1, D], F32, kind="Internal").ap()
    dbg_d = None
    if debug:
        dbg_d = dt("dbg", [128, 8 * 2048], F32, kind="ExternalOutput").ap()

    S = Sched(nc)
    NW = 52800
    dumps = {}

    def dump(name, buf, n, dtype):
        if not debug:
            return
        dten = dt("dbg_" + name, [128, n], dtype, kind="ExternalOutput").ap()
        dumps[name] = (n, dtype)
        src = bass.AP(buf.ap.tensor, buf.ap.offset, [list(buf.ap.ap[0]), [1, n]])
        S.dma("sp", dten, src, [buf.reg()], [])
    from contextlib import ExitStack
    es = ExitStack()
    arena = es.enter_context(nc.sbuf_tensor("arena", [128, NW], F32))
    psb = [es.enter_context(nc.psum_tensor("ps%d" % i, [128, 512], F32)) for i in range(8)]
    sems = [es.enter_context(nc.semaphore("s%d" % i)) for i in range(48)]

    class Alloc:
        def __init__(self, lo, hi):
            self.p = lo
            self.hi = hi

        def f32(self, n, shape=None):
            w0 = self.p
            self.p += n
            assert self.p <= self.hi, ("sbuf overflow", self.p, self.hi)
            ap = arena[:, w0:w0 + n]
            if shape:
                ap = ap.rearrange(shape[0], **shape[1])
            return Buf(ap, "sb", w0 * 4, (w0 + n) * 4, 4)

        def bf(self, n, shape=None):
            nw = (n + 1) // 2
            w0 = self.p
            self.p += nw
            assert self.p <= self.hi, ("sbuf overflow", self.p, self.hi)
            ap = arena[:, w0:w0 + nw].bitcast(BF16)
            if shape:
                ap = ap.rearrange(shape[0], **shape[1])
            return Buf(ap, "sb", w0 * 4, (w0 + nw) * 4, 2)

    def psum(i, bf=False):
        ap = psb[i][:]
        if bf:
            return Buf(ap.bitcast(BF16), "ps%d" % i, 0, 2048, 2)
        return Buf(ap, "ps%d" % i, 0, 2048, 4)

    PS = [psum(i) for i in range(8)]
    PSB = [psum(i, True) for i in range(8)]

    A = Alloc(0, NW)
    hT = A.bf(8 * NT, ("p (k t) -> p k t", dict(k=8)))
    NSLOT = 5
    ring = [A.bf(2048) for _ in range(NSLOT)]
    ident = A.f32(128)
    identb = A.bf(128)
    iota_r = A.f32(128)
    iota_c = A.f32(1)
    cols = A.f32(NCOLS)
    smalls = A.f32(NSM)
    lg = A.f32(8)
    nlg = A.f32(8)
    lg128 = A.f32(8)
    lg127 = A.f32(8)
    cdec = A.f32(8)
    ck = A.f32(8 * 16)
    kd = A.f32(8)
    Dm = A.f32(4 * 128, ("p (h i) -> p h i", dict(h=4)))
    rowd = A.bf(8 * 128, ("p (g i) -> p g i", dict(g=8)))
    Acol = A.f32(8)
    Bcol = A.f32(8)
    scl = A.f32(20)
    scl2 = A.f32(20)
    fw = A.f32(40)
    lruout = A.f32(80)
    one_c = A.f32(1)
    eps_c = A.f32(1)
    tiny_c = A.f32(1)
    scb = A.bf(8 * 128, ("p (k m) -> p k m", dict(k=8)))
    tiny = A.f32(64)
    PERS_END = A.p

    flagsF = lambda n: smalls.ap[:, 8 + n:9 + n]
    flagsB = lambda n: smalls.ap[:, 24 + n:25 + n]
    convflag = smalls.ap[:, 40:41]
    lrukeep = smalls.ap[:, 41:42]

    wslot = [0]

    def next_slot():
        s = ring[wslot[0] % NSLOT]
        wslot[0] += 1
        return s

    def load_w(src_ap, shape_str, **kw):
        slot = next_slot()
        n = 1
        for v in src_ap.shape[1:]:
            n *= v
        view = slot.ap[:, 0:n].rearrange(shape_str, **kw)
        S.dma("pool", view, src_ap, [], [slot.reg(0, n)])
        return slot, view

    win_v = win_d.rearrange("(k p) c -> p k c", p=128)

    S.dma("sp", cols.ap, cols_d, [], [cols.reg()])
    S.dma("sp", smalls.ap, bass.AP(sm_d.tensor, 0, [[0, 128], [1, NSM]]), [], [smalls.reg()])
    S.dma("sp", ident.ap, consts_d[:, 0:128], [], [ident.reg()])
    S.dma("sp", iota_r.ap, consts_d[:, 128:256], [], [iota_r.reg()])
    S.dma("sp", iota_c.ap, consts_d[:, 256:257], [], [iota_c.reg()], allow_slow_non_contiguous=True)
    S.dve(lambda e: e.memset(one_c.ap, 1.0), [], [one_c.reg()])
    S.dve(lambda e: e.memset(eps_c.ap, EPS), [], [eps_c.reg()])
    S.dve(lambda e: e.memset(tiny_c.ap, 1e-18), [], [tiny_c.reg()])
    S.dve(lambda e: e.tensor_copy(out=identb.ap, in_=ident.ap), [ident.reg()], [identb.reg()])
    t0 = Buf(tiny.ap[:, 0:8], "sb", tiny.lo, tiny.lo + 32, 4)
    S.act(lambda e: e.activation(out=t0.ap, in_=smalls.ap[:, 0:8], func=AF.Exp, scale=-1.0), [smalls.reg()], [t0.reg()])
    S.act(lambda e: e.activation(out=t0.ap, in_=t0.ap, func=AF.Ln, bias=one_c.ap), [t0.reg(), one_c.reg()], [t0.reg()])
    S.dve(lambda e: e.tensor_scalar(out=lg.ap, in0=t0.ap, scalar1=-1.0, scalar2=None, op0=ALU.mult), [t0.reg()], [lg.reg()])
    S.dve(lambda e: e.tensor_copy(out=nlg.ap, in_=t0.ap), [t0.reg()], [nlg.reg()])
    S.dve(lambda e: e.tensor_scalar(out=lg128.ap, in0=lg.ap, scalar1=128.0, scalar2=None, op0=ALU.mult), [lg.reg()], [lg128.reg()])
    S.dve(lambda e: e.tensor_scalar(out=lg127.ap, in0=lg.ap, scalar1=127.0, scalar2=None, op0=ALU.mult), [lg.reg()], [lg127.reg()])
    S.act(lambda e: e.activation(out=cdec.ap, in_=lg128.ap, func=AF.Exp), [lg128.reg()], [cdec.reg()])
    for d in range(2):
        for h in range(4):
            g = d * 4 + h
            S.dve(lambda e, g=g, d=d: e.tensor_scalar(out=ck.ap[:, g * 16:(g + 1) * 16], in0=smalls.ap[:, 8 + 16 * d:24 + 16 * d],
                                                        scalar1=cdec.ap[:, g:g + 1], scalar2=None, op0=ALU.mult),
                  [smalls.reg(), cdec.reg()], [ck.reg(g * 16, (g + 1) * 16)])
    for h in range(4):
        S.act(lambda e, h=h: e.activation(out=kd.ap[:, h:h + 1], in_=iota_c.ap, func=AF.Exp, scale=nlg.ap[:, h:h + 1], bias=lg127.ap[:, h:h + 1]),
              [iota_c.reg(), nlg.reg(), lg127.reg()], [kd.reg(h, h + 1)])
        S.act(lambda e, h=h: e.activation(out=kd.ap[:, 4 + h:5 + h], in_=iota_c.ap, func=AF.Exp, scale=lg.ap[:, 4 + h:5 + h]),
              [iota_c.reg(), lg.reg()], [kd.reg(4 + h, 5 + h)])
    S.dve(lambda e: e.tensor_scalar(out=kd.ap, in0=kd.ap, scalar1=1.0 / 16.0, scalar2=None, op0=ALU.mult), [kd.reg()], [kd.reg()])
    P0 = Alloc(PERS_END, NW)
    rtmp = P0.f32(128)
    for h in range(4):
        S.act(lambda e, h=h: e.activation(out=rtmp.ap, in_=iota_r.ap, func=AF.Exp, scale=lg.ap[:, h:h + 1], bias=lg.ap[:, h:h + 1]),
              [iota_r.reg(), lg.reg()], [rtmp.reg()])
        S.dve(lambda e, h=h: e.tensor_copy(out=rowd.ap[:, h, :], in_=rtmp.ap), [rtmp.reg()], [rowd.reg(h * 128, (h + 1) * 128)])
        S.act(lambda e, h=h: e.activation(out=rtmp.ap, in_=iota_r.ap, func=AF.Exp, scale=nlg.ap[:, 4 + h:5 + h], bias=lg128.ap[:, 4 + h:5 + h]),
              [iota_r.reg(), nlg.reg(), lg128.reg()], [rtmp.reg()])
        S.dve(lambda e, h=h: e.tensor_copy(out=rowd.ap[:, 4 + h, :], in_=rtmp.ap), [rtmp.reg()], [rowd.reg((4 + h) * 128, (5 + h) * 128)])
    delta = P0.f32(128)
    dpos = P0.f32(128)
    dneg = P0.f32(128)
    mF = P0.f32(128)
    mB = P0.f32(128)
    e1 = P0.f32(128)
    e2 = P0.f32(128)
    S.dve(lambda e: e.tensor_scalar(out=delta.ap, in0=iota_r.ap, scalar1=iota_c.ap, scalar2=None, op0=ALU.subtract), [iota_r.reg(), iota_c.reg()], [delta.reg()])
    S.dve(lambda e: e.tensor_scalar(out=dpos.ap, in0=delta.ap, scalar1=0.0, scalar2=None, op0=ALU.max), [delta.reg()], [dpos.reg()])
    S.dve(lambda e: e.tensor_scalar(out=dneg.ap, in0=delta.ap, scalar1=-1.0, scalar2=0.0, op0=ALU.mult, op1=ALU.max), [delta.reg()], [dneg.reg()])
    S.dve(lambda e: e.tensor_scalar(out=mF.ap, in0=delta.ap, scalar1=0.0, scalar2=None, op0=ALU.is_ge), [delta.reg()], [mF.reg()])
    S.dve(lambda e: e.tensor_scalar(out=mB.ap, in0=delta.ap, scalar1=0.0, scalar2=None, op0=ALU.is_lt), [delta.reg()], [mB.reg()])
    for h in range(4):
        S.act(lambda e, h=h: e.activation(out=e1.ap, in_=dpos.ap, func=AF.Exp, scale=lg.ap[:, h:h + 1]), [dpos.reg(), lg.reg()], [e1.reg()])
        S.act(lambda e, h=h: e.activation(out=e2.ap, in_=dneg.ap, func=AF.Exp, scale=lg.ap[:, 4 + h:5 + h]), [dneg.reg(), lg.reg()], [e2.reg()])
        S.dve(lambda e: e.tensor_tensor(out=e1.ap, in0=e1.ap, in1=mF.ap, op=ALU.mult), [e1.reg(), mF.reg()], [e1.reg()])
        S.dve(lambda e: e.tensor_tensor(out=e2.ap, in0=e2.ap, in1=mB.ap, op=ALU.mult), [e2.reg(), mB.reg()], [e2.reg()])
        S.dve(lambda e, h=h: e.tensor_tensor(out=Dm.ap[:, h, :], in0=e1.ap, in1=e2.ap, op=ALU.add), [e1.reg(), e2.reg()], [Dm.reg(h * 128, (h + 1) * 128)])
    t1 = Buf(tiny.ap[:, 16:36], "sb", tiny.lo + 64, tiny.lo + 144, 4)
    S.act(lambda e: e.activation(out=t1.ap, in_=cols.ap[:, C_AP:C_AP + 20], func=AF.Exp, scale=-1.0), [cols.reg()], [t1.reg()])
    S.act(lambda e: e.activation(out=t1.ap, in_=t1.ap, func=AF.Ln, bias=one_c.ap), [t1.reg(), one_c.reg()], [t1.reg()])
    S.dve(lambda e: e.tensor_scalar(out=scl.ap, in0=t1.ap, scalar1=-8.0, scalar2=None, op0=ALU.mult), [t1.reg()], [scl.reg()])
    S.dve(lambda e: e.tensor_scalar(out=scl2.ap, in0=t1.ap, scalar1=-16.0, scalar2=None, op0=ALU.mult), [t1.reg()], [scl2.reg()])
    S.dve(lambda e: e.tensor_scalar(out=fw.ap, in0=cols.ap[:, C_CW:C_CW + 40], scalar1=convflag, scalar2=None, op0=ALU.mult), [cols.reg(), smalls.reg()], [fw.reg()])
    sc32 = Buf(tiny.ap[:, 40:48], "sb", tiny.lo + 160, tiny.lo + 192, 4)
    S.act(lambda e: e.activation(out=sc32.ap, in_=cols.ap[:, C_COND:C_COND + 8], func=AF.Silu), [cols.reg()], [sc32.reg()])
    S.dve(lambda e: e.tensor_copy(out=scb.ap, in_=bcast_ap(sc32.ap, [[1, 8], [0, 128]])), [sc32.reg()], [scb.reg()])

    wada_v = wada_d.rearrange("(k p) c -> p k c", p=128)
    modps = PS[7]

    def matvec(sl):
        slot, wv = load_w(wada_v[:, :, sl * 256:(sl + 1) * 256], "p (k c) -> p k c", k=8)
        for j in range(2):
            ft = sl * 2 + j
            S.mm([(modps.ap[:, ft:ft + 1], wv[:, k, j * 128:(j + 1) * 128], scb.ap[:, k, 0:1]) for k in range(8)],
                 [slot.reg(), scb.reg()], [modps.reg(ft, ft + 1)])

    def checkpoint(k):
        if limit == k:
            S.enabled = False

    checkpoint(0)
    xs = [P0.f32(1024) for _ in range(2)]
    xn = [P0.f32(1024) for _ in range(2)]
    junk = P0.bf(1024)
    ssq = [P0.f32(1) for _ in range(2)]
    rstd = [P0.f32(1) for _ in range(2)]
    hTf = P0.f32(8 * NT, ("p (k t) -> p k t", dict(k=8)))
    for n in range(NCH):
        b = n % 2
        S.dma("sp", xs[b].ap, x_d[n * 128:(n + 1) * 128, :], [], [xs[b].reg()])
        S.act(lambda e, b=b: e.activation(out=junk.ap, in_=xs[b].ap, func=AF.Square, accum_out=ssq[b].ap), [xs[b].reg()], [junk.reg(), ssq[b].reg()])
        S.act(lambda e, b=b: e.activation(out=rstd[b].ap, in_=ssq[b].ap, func=AF.Ln, scale=1.0 / D, bias=eps_c.ap), [ssq[b].reg(), eps_c.reg()], [rstd[b].reg()])
        S.act(lambda e, b=b: e.activation(out=rstd[b].ap, in_=rstd[b].ap, func=AF.Exp, scale=-0.5), [rstd[b].reg()], [rstd[b].reg()])
        S.act(lambda e, b=b: e.activation(out=xn[b].ap, in_=xs[b].ap, func=AF.Copy, scale=rstd[b].ap), [xs[b].reg(), rstd[b].reg()], [xn[b].reg()])
        for half in range(2):
            pb = PS[half]
            for j in range(4):
                k = half * 4 + j
                S.pe(lambda e, k=k, j=j, pb=pb, b=b: e.transpose(pb.ap[:, j * 128:(j + 1) * 128], xn[b].ap[:, k * 128:(k + 1) * 128], ident.ap),
                     [xn[b].reg(k * 128, (k + 1) * 128), ident.reg()], [pb.reg(j * 128, (j + 1) * 128)])
            S.dve(lambda e, half=half, pb=pb, n=n: e.tensor_copy(out=hTf.ap[:, half * 4:(half + 1) * 4, n * 128:(n + 1) * 128],
                                                                in_=pb.ap.rearrange("p (a t) -> p a t", a=4)),
                  [pb.reg()], [hTf.reg((half * 4) * NT, (half * 4 + 4) * NT)])
        if n % 2 == 1:
            matvec(n // 2)
    S.dve(lambda e: e.tensor_tensor(out=Bcol.ap, in0=modps.ap[:, 0:8], in1=cols.ap[:, C_BSH:C_BSH + 8], op=ALU.add), [modps.reg(0, 16), cols.reg()], [Bcol.reg()])
    S.dve(lambda e: e.tensor_tensor(out=Acol.ap, in0=modps.ap[:, 8:16], in1=cols.ap[:, C_BSC:C_BSC + 8], op=ALU.add), [modps.reg(0, 16), cols.reg()], [Acol.reg()])
    S.dve(lambda e: e.scalar_tensor_tensor(out=Acol.ap, in0=Acol.ap, scalar=1.0, in1=cols.ap[:, C_NW:C_NW + 8], op0=ALU.add, op1=ALU.mult), [Acol.reg(), cols.reg()], [Acol.reg()])
    for k in range(8):
        hreg = hT.reg(k * NT, (k + 1) * NT)
        freg = hTf.reg(k * NT, (k + 1) * NT)
        if k % 2 == 0:
            S.dve(lambda e, k=k: e.tensor_scalar(out=hT.ap[:, k, :], in0=hTf.ap[:, k, :], scalar1=Acol.ap[:, k:k + 1], scalar2=Bcol.ap[:, k:k + 1], op0=ALU.mult, op1=ALU.add),
                  [freg, Acol.reg(), Bcol.reg()], [hreg])
        else:
            S.act(lambda e, k=k: e.activation(out=hT.ap[:, k, :], in_=hTf.ap[:, k, :], func=AF.Identity, scale=Acol.ap[:, k:k + 1], bias=Bcol.ap[:, k:k + 1]),
                  [freg, Acol.reg(), Bcol.reg()], [hreg])
    dump('hT', hT, 8 * NT, BF16)
    checkpoint(1)
    ipb = [0]

    def inproj_fm(col0, ncols, consumer, banks=(0, 1)):
        slot, wv = load_w(win_v[:, :, col0:col0 + ncols], "p (k c) -> p k c", k=8)

        def compute():
            for ct in range(ncols // 128):
                for tb in range(4):
                    pb = PS[banks[ipb[0] % len(banks)]]
                    ipb[0] += 1
                    S.mm([(pb.ap, wv[:, k, ct * 128:(ct + 1) * 128], hT.ap[:, k, tb * 512:(tb + 1) * 512]) for k in range(8)],
                         [slot.reg(), hT.reg()], [pb.reg()])
                    consumer(ct, tb, pb)
        return compute

    P1 = Alloc(PERS_END, NW)
    oT = P1.bf(16 * NT, ("p (k t) -> p k t", dict(k=16)))
    qT = P1.bf(2 * NT, ("p (k t) -> p k t", dict(k=2)))
    kT = P1.bf(2 * NT, ("p (k t) -> p k t", dict(k=2)))
    vt = P1.bf(NCH * 512, ("p (n e) -> p n e", dict(n=NCH)))
    U = [[P1.f32(1024, ("p (k e) -> p k e", dict(k=2))) for _ in range(2)] for _ in range(2)]
    SFb = [P1.bf(1024, ("p (k e) -> p k e", dict(k=2))) for _ in range(2)]
    TBw = [P1.bf(1024, ("p (k e) -> p k e", dict(k=2))) for _ in range(2)]
    TBr = [P1.bf(1024, ("p (k e) -> p k e", dict(k=2))) for _ in range(2)]
    kpr = [P1.bf(256) for _ in range(2)]
    Pm = [P1.bf(128) for _ in range(2)]
    qF = [P1.bf(256, ("p (k i) -> p k i", dict(k=2))) for _ in range(2)]
    qB = [P1.bf(256, ("p (k i) -> p k i", dict(k=2))) for _ in range(2)]
    og = [P1.bf(512) for _ in range(2)]
    junk1 = P1.bf(512)
    ss1 = [P1.f32(1) for _ in range(2)]
    rs1 = [P1.f32(1) for _ in range(2)]
    sgt = [P1.bf(512) for _ in range(2)]
    grow = P1.f32(1024)
    growb = P1.f32(1024)
    P1_END = P1.p

    tb_v = tb_d.rearrange("(h n p) (k e) -> h n p k e", h=4, n=16, k=2)

    def tb_reg(h, n):
        base = ((h * 16 + n) * 128) * 1024 * 2
        return ("scr", base, base + 128 * 1024 * 2)

    actdve = [0]

    def evac_copy(out_ap, out_reg, pb, scale=None):
        if actdve[0] % 2 == 0:
            if scale is None:
                S.act(lambda e: e.activation(out=out_ap, in_=pb.ap, func=AF.Copy), [pb.reg()], [out_reg])
            else:
                S.act(lambda e: e.activation(out=out_ap, in_=pb.ap, func=AF.Copy, scale=scale), [pb.reg()], [out_reg])
        else:
            if scale is None:
                S.dve(lambda e: e.tensor_copy(out=out_ap, in_=pb.ap), [pb.reg()], [out_reg])
            else:
                S.dve(lambda e: e.tensor_scalar(out=out_ap, in0=pb.ap, scalar1=scale, scalar2=None, op0=ALU.mult), [pb.reg()], [out_reg])
        actdve[0] += 1

    def gate_row_job():
        S.dma("sp", growb.ap[0:1, :], rows_d[0:1, :], [], [growb.reg()])
        for s_ in range(4):
            slot, wv = load_w(wada_v[:, :, 2048 + s_ * 256:2048 + (s_ + 1) * 256], "p (k c) -> p k c", k=8)
            pbg = PS[4 + s_ % 2]
            S.mm([(pbg.ap[:, 0:256], scb.ap[:, k, :], wv[:, k, :]) for k in range(8)], [slot.reg(), scb.reg()], [pbg.reg(0, 256)])
            S.dve(lambda e, s_=s_, pbg=pbg: e.tensor_tensor(out=grow.ap[0:1, s_ * 256:(s_ + 1) * 256], in0=pbg.ap[0:1, 0:256], in1=growb.ap[0:1, s_ * 256:(s_ + 1) * 256], op=ALU.add),
                  [pbg.reg(0, 256), growb.reg()], [grow.reg(s_ * 256, (s_ + 1) * 256)])
        S.dma("sp", gate_d, grow.ap[0:1, :], [grow.reg()], [("scr2", 0, 4096)])

    gate_ctr = [0]

    def make_gate_jobs(hg, banks):
        slabs = [load_w(win_v[:, :, OG + hg * 512 + s_ * 256:OG + hg * 512 + (s_ + 1) * 256], "p (k c) -> p k c", k=8) for s_ in range(2)]
        jobs = []
        for s_ in range(2):
            slot, wv = slabs[s_]
            for ct in range(2):
                for tb in range(4):
                    def job(slot=slot, wv=wv, s_=s_, ct=ct, tb=tb):
                        i_ = gate_ctr[0]
                        gate_ctr[0] += 1
                        b2 = i_ % 2
                        pb = PS[banks[i_ % len(banks)]]
                        S.mm([(pb.ap, wv[:, k, ct * 128:(ct + 1) * 128], hT.ap[:, k, tb * 512:(tb + 1) * 512]) for k in range(8)],
                             [slot.reg(), hT.reg()], [pb.reg()])
                        S.act(lambda e: e.activation(out=sgt[b2].ap, in_=pb.ap, func=AF.Silu), [pb.reg()], [sgt[b2].reg()])
                        k_ = hg * 4 + s_ * 2 + ct
                        rg = oT.reg(k_ * NT + tb * 512, k_ * NT + (tb + 1) * 512)
                        S.pool(lambda e: e.tensor_tensor(out=oT.ap[:, k_, tb * 512:(tb + 1) * 512], in0=oT.ap[:, k_, tb * 512:(tb + 1) * 512], in1=sgt[b2].ap, op=ALU.mult),
                               [rg, sgt[b2].reg()], [rg])
                    jobs.append(job)
        return jobs

    for h in range(4):
        gF, gB = h, 4 + h
        def q_cons(ct, tb, pb):
            evac_copy(qT.ap[:, ct, tb * 512:(tb + 1) * 512], qT.reg(ct * NT + tb * 512, ct * NT + (tb + 1) * 512), pb)

        def k_cons(ct, tb, pb):
            evac_copy(kT.ap[:, ct, tb * 512:(tb + 1) * 512], kT.reg(ct * NT + tb * 512, ct * NT + (tb + 1) * 512), pb)
        cq = inproj_fm(OQ + h * 256, 256, q_cons)
        ckk = inproj_fm(OK_ + h * 256, 256, k_cons)
        vsl = [load_w(win_v[:, :, OV + h * 512 + s * 256: OV + h * 512 + (s + 1) * 256], "p (k c) -> p k c", k=8) for s in range(2)]
        cq()
        ckk()
        for n in range(NCH):
            pb = PS[n % 2]
            for s in range(2):
                slot, wv = vsl[s]
                S.mm([(pb.ap[:, s * 256:(s + 1) * 256], hT.ap[:, k, n * 128:(n + 1) * 128], wv[:, k, :]) for k in range(8)],
                     [slot.reg(), hT.reg()], [pb.reg(s * 256, (s + 1) * 256)])
            evac_copy(vt.ap[:, n, :], vt.reg(n * 512, (n + 1) * 512), pb)
        if h == 0:
            dump('qT', qT, 2 * NT, BF16)
            dump('kT', kT, 2 * NT, BF16)
            dump('vt', vt, NCH * 512, BF16)
            checkpoint(2)
        for d in range(2):
            S.dma("sp", U[d][0].ap, iret_d[d, h].rearrange("(k p) e -> p k e", p=128), [], [U[d][0].reg()])
        def T_(n, single=False):
            kps = PSB[0] if single else PSB[n % 2]
            for dtl in range(2):
                S.pe(lambda e, dtl=dtl, n=n, kps=kps: e.transpose(kps.ap[:, dtl * 128:(dtl + 1) * 128], kT.ap[:, dtl, n * 128:(n + 1) * 128], identb.ap),
                     [kT.reg(dtl * NT + n * 128, dtl * NT + (n + 1) * 128), identb.reg()], [kps.reg(dtl * 128, (dtl + 1) * 128)])

        def KPR_(n, g, single=False):
            kps = PSB[0] if single else PSB[n % 2]
            b = n % 2
            S.act(lambda e: e.activation(out=kpr[b].ap, in_=kps.ap[:, 0:256], func=AF.Copy, scale=kd.ap[:, g:g + 1]),
                  [kps.reg(0, 256), kd.reg()], [kpr[b].reg()])

        def KV_(n):
            b = n % 2
            for dtl in range(2):
                pb = PS[6 + dtl]
                S.mm([(pb.ap, kpr[b].ap[:, dtl * 128:(dtl + 1) * 128], vt.ap[:, n, :])], [kpr[b].reg(), vt.reg(n * 512, (n + 1) * 512)], [pb.reg()])

        def UPD_(n, d, g, cur):
            nxt = 1 - cur
            for dtl in range(2):
                pb = PS[6 + dtl]
                S.dve(lambda e, dtl=dtl, pb=pb: e.scalar_tensor_tensor(
                    out=U[d][nxt].ap[:, dtl, :], in0=U[d][cur].ap[:, dtl, :], scalar=ck.ap[:, g * 16 + n:g * 16 + n + 1], in1=pb.ap, op0=ALU.mult, op1=ALU.add),
                    [U[d][cur].reg(dtl * 512, (dtl + 1) * 512), ck.reg(), pb.reg()], [U[d][nxt].reg(dtl * 512, (dtl + 1) * 512)])
            return nxt

        def TBW_(n, cur):
            b = n % 2
            S.act(lambda e: e.activation(out=TBw[b].ap, in_=U[1][cur].ap, func=AF.Copy, scale=flagsB(n)),
                  [U[1][cur].reg(), smalls.reg()], [TBw[b].reg()])
            S.dma("sp", tb_v[h, n], TBw[b].ap, [TBw[b].reg()], [tb_reg(h, n)])

        cur = 0
        gjobs = make_gate_jobs(h - 1, (3,)) if h >= 1 else []
        if h == 0:
            gate_row_job()
        T_(NCH - 1)
        KPR_(NCH - 1, gB)
        TBW_(NCH - 1, cur)
        for n in range(NCH - 1, -1, -1):
            if n >= 1:
                T_(n - 1)
            KV_(n)
            if n >= 1:
                KPR_(n - 1, gB)
            cur = UPD_(n, 1, gB, cur)
            if n >= 1:
                TBW_(n - 1, cur)
            if gjobs:
                gjobs.pop(0)()
            if n in (0, 2, 4, 6):
                S.dma("sp", sret_d[n // 2, 1, h].rearrange("(k p) e -> p k e", p=128), U[1][cur].ap, [U[1][cur].reg()], [])
        if h == 0:
            checkpoint(3)

        def S_(n):
            sps = PS[2]
            S.mm([(sps.ap[:, 0:128], kT.ap[:, dtl, n * 128:(n + 1) * 128], qT.ap[:, dtl, n * 128:(n + 1) * 128]) for dtl in range(2)],
                 [kT.reg(), qT.reg()], [sps.reg(0, 128)])

        def P_(n):
            sps = PS[2]
            b = n % 2
            S.dve(lambda e, h=h: e.scalar_tensor_tensor(out=Pm[b].ap, in0=sps.ap[:, 0:128], scalar=1.0 / 16.0, in1=Dm.ap[:, h, :], op0=ALU.mult, op1=ALU.mult),
                  [sps.reg(0, 128), Dm.reg()], [Pm[b].reg()])

        def Q_(n):
            b = n % 2
            S.pool(lambda e, gF=gF: e.tensor_tensor(out=qF[b].ap, in0=qT.ap[:, :, n * 128:(n + 1) * 128],
                                             in1=bcast_ap(rowd.ap[:, gF, :], [[0, 2], [1, 128]]), op=ALU.mult),
                   [qT.reg(), rowd.reg()], [qF[b].reg()])
            S.pool(lambda e, gB=gB: e.tensor_tensor(out=qB[b].ap, in0=qT.ap[:, :, n * 128:(n + 1) * 128],
                                             in1=bcast_ap(rowd.ap[:, gB, :], [[0, 2], [1, 128]]), op=ALU.mult),
                   [qT.reg(), rowd.reg()], [qB[b].reg()])

        def SF_(n, cur):
            b = n % 2
            S.dve(lambda e: e.tensor_scalar(out=SFb[b].ap, in0=U[0][cur].ap, scalar1=flagsF(n), scalar2=None, op0=ALU.mult),
                  [U[0][cur].reg(), smalls.reg()], [SFb[b].reg()])

        def O_(n):
            b = n % 2
            ops_ = PS[4 + b]
            mms = [(ops_.ap, Pm[b].ap, vt.ap[:, n, :])]
            mms += [(ops_.ap, qB[b].ap[:, dtl, :], TBr[b].ap[:, dtl, :]) for dtl in range(2)]
            mms += [(ops_.ap, qF[b].ap[:, dtl, :], SFb[b].ap[:, dtl, :]) for dtl in range(2)]
            S.mm(mms, [Pm[b].reg(), vt.reg(n * 512, (n + 1) * 512), qF[b].reg(), qB[b].reg(), SFb[b].reg(), TBr[b].reg()], [ops_.reg()])

        def NORM_(n):
            b = n % 2
            ops_ = PS[4 + b]
            S.act(lambda e: e.activation(out=junk1.ap, in_=ops_.ap, func=AF.Square, accum_out=ss1[b].ap), [ops_.reg()], [junk1.reg(), ss1[b].reg()])
            S.act(lambda e: e.activation(out=rs1[b].ap, in_=ss1[b].ap, func=AF.Ln, scale=1.0 / 512.0, bias=eps_c.ap), [ss1[b].reg(), eps_c.reg()], [rs1[b].reg()])
            S.act(lambda e: e.activation(out=rs1[b].ap, in_=rs1[b].ap, func=AF.Exp, scale=-0.5), [rs1[b].reg()], [rs1[b].reg()])
            S.act(lambda e: e.activation(out=og[b].ap, in_=ops_.ap, func=AF.Copy, scale=rs1[b].ap), [ops_.reg(), rs1[b].reg()], [og[b].reg()])

        def OGT_(n):
            b = n % 2
            tps = PSB[1]
            for et in range(4):
                S.pe(lambda e, et=et: e.transpose(tps.ap[:, et * 128:(et + 1) * 128], og[b].ap[:, et * 128:(et + 1) * 128], identb.ap),
                     [og[b].reg(et * 128, (et + 1) * 128), identb.reg()], [tps.reg(et * 128, (et + 1) * 128)])

        def OTE_(n):
            tps = PSB[1]
            S.dve(lambda e, h=h: e.tensor_tensor(out=oT.ap[:, h * 4:(h + 1) * 4, n * 128:(n + 1) * 128],
                                            in0=tps.ap[:, 0:512].rearrange("p (a t) -> p a t", a=4),
                                            in1=bcast_ap(cols.ap[:, C_GN + h * 4:C_GN + h * 4 + 4], [[1, 4], [0, 128]]), op=ALU.mult),
                  [tps.reg(0, 512), cols.reg()], [oT.reg((h * 4) * NT, (h * 4 + 4) * NT)])

        cur = 0
        S.dma("sp", TBr[0].ap, tb_v[h, 0], [tb_reg(h, 0)], [TBr[0].reg()])
        T_(0, True)
        KPR_(0, gF, True)
        S_(0)
        P_(0)
        Q_(0)
        SF_(0, cur)
        for n in range(NCH):
            last = (n + 1 == NCH)
            if not last:
                S.dma("sp", TBr[(n + 1) % 2].ap, tb_v[h, n + 1], [tb_reg(h, n + 1)], [TBr[(n + 1) % 2].reg()])
                T_(n + 1, True)
                S_(n + 1)
            KV_(n)
            O_(n)
            if n >= 1:
                OGT_(n - 1)
            cur = UPD_(n, 0, gF, cur)
            if n in (1, 3, 5, 7):
                S.dma("sp", sret_d[n // 2, 0, h].rearrange("(k p) e -> p k e", p=128), U[0][cur].ap, [U[0][cur].reg()], [])
            if not last:
                SF_(n + 1, cur)
                KPR_(n + 1, gF, True)
                P_(n + 1)
                Q_(n + 1)
            if n >= 1:
                OTE_(n - 1)
            NORM_(n)
        OGT_(NCH - 1)
        OTE_(NCH - 1)
        if h == 0:
            dump('ss1a', ss1[0], 1, F32)
            dump('ss1b', ss1[1], 1, F32)
            dump('rs1a', rs1[0], 1, F32)
            dump('rs1b', rs1[1], 1, F32)
            dump('og0', og[0], 512, BF16)
            dump('og1', og[1], 512, BF16)
            dump('oT0', oT, 4 * NT, BF16)
            checkpoint(4)
        if h == 3:
            for j_ in make_gate_jobs(3, (2, 3)):
                j_()

    dump('oT', oT, 16 * NT, BF16)
    checkpoint(5)
    P1b = Alloc(PERS_END + 16 * NT // 2, NW)
    PT = P1b.bf(8 * NT, ("p (k t) -> p k t", dict(k=8)))
    smr = [P1b.bf(512) for _ in range(2)]
    wrd_v = wrd_d.rearrange("(k p) c -> p k c", p=128)
    def rd_loads(ct):
        return (load_w(win_v[:, :, OMR + ct * 128:OMR + (ct + 1) * 128], "p (k c) -> p k c", k=8),
                load_w(wrd_v[:, :, ct * 128:(ct + 1) * 128], "p (k c) -> p k c", k=16))
    rd_next = rd_loads(0)
    for ct in range(8):
        (slot_m, wm), (slot_r, wr) = rd_next
        if ct + 1 < 8:
            rd_next = rd_loads(ct + 1)
        for tb in range(4):
            b2 = tb % 2
            pm = PS[b2]
            S.mm([(pm.ap, wm[:, k, :], hT.ap[:, k, tb * 512:(tb + 1) * 512]) for k in range(8)], [slot_m.reg(), hT.reg()], [pm.reg()])
            S.act(lambda e, b2=b2, pm=pm: e.activation(out=smr[b2].ap, in_=pm.ap, func=AF.Sigmoid), [pm.reg()], [smr[b2].reg()])
            pr = PS[2 + b2]
            S.mm([(pr.ap, wr[:, k, :], oT.ap[:, k, tb * 512:(tb + 1) * 512]) for k in range(16)], [slot_r.reg(), oT.reg()], [pr.reg()])
            S.dve(lambda e, b2=b2, pr=pr, ct=ct, tb=tb: e.tensor_tensor(out=PT.ap[:, ct, tb * 512:(tb + 1) * 512], in0=pr.ap, in1=smr[b2].ap, op=ALU.mult),
                  [pr.reg(), smr[b2].reg()], [PT.reg(ct * NT + tb * 512, ct * NT + (tb + 1) * 512)])

    dump('PTret', PT, 8 * NT, BF16)
    checkpoint(6)
    P2 = Alloc(PERS_END, PERS_END + 16 * NT // 2)
    yT = P2.bf(10 * NT, ("p (k t) -> p k t", dict(k=10)))
    hh = [P2.f32(NT) for _ in range(2)]
    sgq = [P2.bf(512) for _ in range(2)]
    xq_p2 = [P2.f32(512) for _ in range(2)]
    P2c = Alloc(P1b.p, NW)
    xq_a = [P2c.f32(512) for _ in range(4)]
    xcb0 = P2c.bf(NT)
    tgd = [[P2c.f32(NT // 2) for _ in range(2)] for _ in range(2)]
    aad = [[P2c.f32(NT // 2) for _ in range(2)] for _ in range(2)]
    hcol = P2c.f32(60)
    onep = P2c.f32(1)
    lnhalf = P2c.f32(1)
    xq_c = P2c.f32(512)
    S.dve(lambda e: e.tensor_scalar(out=hcol.ap[:, 0:40], in0=cols.ap[:, C_BA:C_BA + 40], scalar1=0.5, scalar2=None, op0=ALU.mult), [cols.reg()], [hcol.reg(0, 40)])
    S.dve(lambda e: e.tensor_scalar(out=hcol.ap[:, 40:60], in0=scl.ap, scalar1=0.5, scalar2=None, op0=ALU.mult), [scl.reg()], [hcol.reg(40, 60)])
    S.dve(lambda e: e.memset(onep.ap, 0.25), [], [onep.reg()])
    S.dve(lambda e: e.memset(lnhalf.ap, -0.6931471805599453), [], [lnhalf.reg()])

    def q3(buf, tb, c0, c1):
        ap = buf.ap
        return bass.AP(ap.tensor, ap.offset + tb * 512 + c0, [list(ap.ap[0]), [64, 8], [1, c1 - c0]])

    def qseq(buf, tb, r0, nr, c):
        ap = buf.ap
        return bass.AP(ap.tensor, ap.offset + tb * 512 + r0 * 64 + c, [list(ap.ap[0]), [256, 2], [64, nr]])

    def rev(ap2d, n):
        return bass.AP(ap2d.tensor, ap2d.offset + n - 1, [list(ap2d.ap[0]), [-1, n]])

    LW = Alloc(ring[0].lo // 4, ring[NSLOT - 1].hi // 4)
    lwsets = [(LW.bf(1024), LW.bf(256), LW.bf(256), LW.bf(1024)) for _ in range(2)]
    xcbs = [xcb0, xcb0]
    trd = [[t_, t_] for t_ in (LW.f32(NT // 2), LW.f32(NT // 2))]
    xq_l = LW.f32(512)
    xcq = [xq_a, [xq_p2[0], xq_p2[1], xq_c, xq_l]]

    def lru_loads(cb):
        bx_, ba_, bw_, bg_ = lwsets[cb % 2]
        vx = bx_.ap.rearrange("p (k c) -> p k c", k=8)
        va = ba_.ap.rearrange("p (d j) -> p d j", d=2)
        vw = bw_.ap.rearrange("p (d j) -> p d j", d=2)
        vg = bg_.ap.rearrange("p (k c) -> p k c", k=8)
        S.dma("pool", vx, win_v[:, :, OXL + cb * 128:OXL + (cb + 1) * 128], [], [bx_.reg()])
        S.dma("pool", va, wa_d[:, cb].rearrange("d i j -> i d j"), [], [ba_.reg()])
        S.dma("pool", vw, wx_d[:, cb].rearrange("d i j -> i d j"), [], [bw_.reg()])
        S.dma("pool", vg, win_v[:, :, OGL + cb * 128:OGL + (cb + 1) * 128], [], [bg_.reg()])
        return (bx_, vx), (ba_, va), (bw_, vw), (bg_, vg)

    XB = [PS[0], PS[1], PS[6], PS[7]]

    def lru_front_pe(cb, wts):
        (slot_x, wxl) = wts[0]
        for tb in range(4):
            pb = XB[tb]
            S.mm([(pb.ap, wxl[:, k, :], hT.ap[:, k, tb * 512:(tb + 1) * 512]) for k in range(8)], [slot_x.reg(), hT.reg()], [pb.reg()])

    def lru_front_conv(cb, tbs=(0, 1, 2, 3)):
        xq = xcq[cb % 2]
        cw = lambda j: cols.ap[:, C_CW + cb * 4 + j:C_CW + cb * 4 + j + 1]
        fwc = lambda j: fw.ap[:, cb * 4 + j:cb * 4 + j + 1]
        cbias = cols.ap[:, C_CB + cb:C_CB + cb + 1]
        for tb in tbs:
            pb = XB[tb]
            xc = xq[tb]
            xr = xc.reg()
            w2, w1, w0, w3 = cw(2), cw(1), cw(0), cw(3)
            S.act(lambda e, pb=pb, xc=xc, w2=w2: e.activation(out=xc.ap, in_=pb.ap, func=AF.Identity, scale=w2, bias=cbias),
                  [pb.reg(), cols.reg()], [xr])
            for (wj, o0, o1, i0, i1) in ((w1, 1, 64, 0, 63), (w0, 2, 64, 0, 62), (w3, 0, 63, 1, 64)):
                S.dve(lambda e, pb=pb, xc=xc, wj=wj, o0=o0, o1=o1, i0=i0, i1=i1: e.scalar_tensor_tensor(
                    out=q3(xc, 0, o0, o1), in0=q3(pb, 0, i0, i1), scalar=wj, in1=q3(xc, 0, o0, o1), op0=ALU.mult, op1=ALU.add),
                    [pb.reg(), xr, cols.reg()], [xr])
            f1, f0, f3 = fwc(1), fwc(0), fwc(3)
            for (fj, oc, ic, orow, irow) in ((f1, 0, 63, 1, 0), (f0, 0, 62, 1, 0), (f0, 1, 63, 1, 0), (f3, 63, 0, 0, 1)):
                S.dve(lambda e, pb=pb, xc=xc, fj=fj, oc=oc, ic=ic, orow=orow, irow=irow: e.scalar_tensor_tensor(
                    out=qseq(xc, 0, orow, 3, oc), in0=qseq(pb, 0, irow, 3, ic), scalar=fj, in1=qseq(xc, 0, orow, 3, oc), op0=ALU.mult, op1=ALU.add),
                    [pb.reg(), xr, fw.reg()], [xr])

    def lru_casts(cb):
        xq = xcq[cb % 2]
        xcb = xcbs[cb % 2]
        for tb in range(4):
            S.act(lambda e, tb=tb: e.activation(out=xcb.ap[:, tb * 512:(tb + 1) * 512], in_=xq[tb].ap, func=AF.Copy),
                  [xq[tb].reg()], [xcb.reg(tb * 512, (tb + 1) * 512)])

    def lru_step(cb, st, wts, part):
        (slot_a, wav), (slot_w, wxv) = wts[1], wts[2]
        xq = xcq[cb % 2]
        xcb = xcbs[cb % 2]
        items = []
        for d in range(2):
            half = st if d == 0 else 1 - st
            qs = [2 * half, 2 * half + 1] if d == 0 else [2 * half + 1, 2 * half]
            items.append((d, half, qs))
        col = lambda base, d: hcol.ap[:, base + d * 10 + cb:base + d * 10 + cb + 1]
        lr = lambda b_, half, tb: b_.reg((tb - 2 * half) * 512, (tb - 2 * half + 1) * 512)
        la = lambda b_, half, tb: b_.ap[:, (tb - 2 * half) * 512:(tb - 2 * half + 1) * 512]
        gq = lambda b_, tb: b_.ap[:, tb * 512:(tb + 1) * 512]
        gr = lambda b_, tb: b_.reg(tb * 512, (tb + 1) * 512)
        pi = [0]
        doA = (part == 'A')
        for (d, half, qs) in (items if doA else []):
            for tb in qs:
                pa = PS[2 + pi[0] % 2]
                pg = PS[4 + pi[0] % 2]
                pi[0] += 1
                S.mm([(pa.ap, wav[:, d, :], gq(xcb, tb))], [slot_a.reg(), gr(xcb, tb)], [pa.reg()])
                S.act(lambda e, pa=pa, tb=tb, d=d, half=half: e.activation(out=la(trd[d][st], half, tb), in_=pa.ap, func=AF.Tanh, scale=0.5, bias=col(0, d)),
                      [pa.reg(), hcol.reg()], [lr(trd[d][st], half, tb)])
                S.mm([(pg.ap, wxv[:, d, :], gq(xcb, tb))], [slot_w.reg(), gr(xcb, tb)], [pg.reg()])
                S.act(lambda e, pg=pg, tb=tb, d=d, half=half: e.activation(out=la(tgd[d][st], half, tb), in_=pg.ap, func=AF.Tanh, scale=0.5, bias=col(20, d)),
                      [pg.reg(), hcol.reg()], [lr(tgd[d][st], half, tb)])
        for (d, half, qs) in (items if doA else []):
            for tb in qs:
                S.dve(lambda e, tb=tb, d=d, half=half: e.scalar_tensor_tensor(out=la(tgd[d][st], half, tb), in0=la(tgd[d][st], half, tb), scalar=1.0, in1=xq[tb].ap, op0=ALU.add, op1=ALU.mult),
                      [lr(tgd[d][st], half, tb), xq[tb].reg()], [lr(tgd[d][st], half, tb)])
        for (d, half, qs) in (items if doA else []):
            for tb in qs:
                S.act(lambda e, tb=tb, d=d, half=half: e.activation(out=la(aad[d][st], half, tb), in_=la(trd[d][st], half, tb), func=AF.Exp, scale=col(40, d), bias=col(40, d)),
                      [lr(trd[d][st], half, tb), hcol.reg()], [lr(aad[d][st], half, tb)])
        for (d, half, qs) in (items if doA else []):
            for tb in qs:
                S.dve(lambda e, tb=tb, d=d, half=half: e.scalar_tensor_tensor(out=la(trd[d][st], half, tb), in0=la(aad[d][st], half, tb), scalar=0.9999995, in1=la(aad[d][st], half, tb), op0=ALU.min, op1=ALU.mult),
                      [lr(aad[d][st], half, tb)], [lr(trd[d][st], half, tb)])
        if part == 'A':
            return
        for (d, half, qs) in items:
            for tb in qs:
                S.act(lambda e, tb=tb, d=d, half=half: e.activation(out=la(trd[d][st], half, tb), in_=la(trd[d][st], half, tb), func=AF.Sqrt, scale=-0.25, bias=onep.ap),
                      [lr(trd[d][st], half, tb), onep.reg()], [lr(trd[d][st], half, tb)])
        for (d, half, qs) in items:
            ap = aad[d][st].ap
            if d == 0:
                off, cnt = (256, 3) if half == 0 else (0, 4)
            else:
                off, cnt = (255, 4) if half == 0 else (255, 3)
            bv = bass.AP(ap.tensor, ap.offset + off, [list(ap.ap[0]), [256, cnt]])
            S.dve(lambda e, bv=bv: e.tensor_scalar(out=bv, in0=bv, scalar1=lrukeep, scalar2=None, op0=ALU.mult), [aad[d][st].reg(), smalls.reg()], [aad[d][st].reg()])
        for i_ in range(2):
            for (d, half, qs) in items:
                tb = qs[i_]
                h0 = cols.ap[:, C_H0 + d * 10 + cb:C_H0 + d * 10 + cb + 1]
                S.dve(lambda e, tb=tb, d=d, half=half: e.tensor_tensor(out=la(tgd[d][st], half, tb), in0=la(tgd[d][st], half, tb), in1=la(trd[d][st], half, tb), op=ALU.mult),
                      [lr(tgd[d][st], half, tb), lr(trd[d][st], half, tb)], [lr(tgd[d][st], half, tb)])
                first = (d == 0 and tb == 0) or (d == 1 and tb == 3)
                if first:
                    init, ireg = h0, cols.reg()
                elif d == 0:
                    init, ireg = hh[0].ap[:, tb * 512 - 1:tb * 512], hh[0].reg(tb * 512 - 1, tb * 512)
                else:
                    init, ireg = hh[1].ap[:, (tb + 1) * 512:(tb + 1) * 512 + 1], hh[1].reg((tb + 1) * 512, (tb + 1) * 512 + 1)
                if d == 0:
                    S.dve(lambda e, tb=tb, init=init, half=half: e.tensor_tensor_scan(out=gq(hh[0], tb), data0=la(aad[0][st], half, tb), data1=la(tgd[0][st], half, tb), initial=init, op0=ALU.mult, op1=ALU.add),
                          [lr(aad[0][st], half, tb), lr(tgd[0][st], half, tb), ireg], [gr(hh[0], tb)])
                else:
                    S.dve(lambda e, tb=tb, init=init, half=half: e.tensor_tensor_scan(out=rev(gq(hh[1], tb), 512), data0=rev(la(aad[1][st], half, tb), 512), data1=rev(la(tgd[1][st], half, tb), 512), initial=init, op0=ALU.mult, op1=ALU.add),
                          [lr(aad[1][st], half, tb), lr(tgd[1][st], half, tb), ireg], [gr(hh[1], tb)])

    def lru_back(cb, wts):
        (slot_g, wgl) = wts[3]
        fo = cb * 8
        S.pool(lambda e: e.tensor_copy(out=lruout.ap[:, fo:fo + 4], in_=bass.AP(hh[0].ap.tensor, hh[0].ap.offset + 255, [list(hh[0].ap.ap[0]), [256, 4]])),
               [hh[0].reg()], [lruout.reg(fo, fo + 4)])
        S.pool(lambda e: e.tensor_copy(out=lruout.ap[:, fo + 4:fo + 8], in_=bass.AP(hh[1].ap.tensor, hh[1].ap.offset, [list(hh[1].ap.ap[0]), [256, 4]])),
               [hh[1].reg()], [lruout.reg(fo + 4, fo + 8)])
        for tb in range(4):
            b2 = tb % 2
            pb = PS[2 + b2]
            S.mm([(pb.ap, wgl[:, k, :], hT.ap[:, k, tb * 512:(tb + 1) * 512]) for k in range(8)], [slot_g.reg(), hT.reg()], [pb.reg()])
            S.act(lambda e, pb=pb, b2=b2: e.activation(out=sgq[b2].ap, in_=pb.ap, func=AF.Silu), [pb.reg()], [sgq[b2].reg()])
            S.pool(lambda e, tb=tb: e.tensor_tensor(out=hh[0].ap[:, tb * 512:(tb + 1) * 512], in0=hh[0].ap[:, tb * 512:(tb + 1) * 512], in1=hh[1].ap[:, tb * 512:(tb + 1) * 512], op=ALU.add),
                   [hh[0].reg(tb * 512, (tb + 1) * 512), hh[1].reg(tb * 512, (tb + 1) * 512)], [hh[0].reg(tb * 512, (tb + 1) * 512)])
            S.pool(lambda e, tb=tb, b2=b2: e.tensor_tensor(out=yT.ap[:, cb, tb * 512:(tb + 1) * 512], in0=hh[0].ap[:, tb * 512:(tb + 1) * 512], in1=sgq[b2].ap, op=ALU.mult),
                  [hh[0].reg(tb * 512, (tb + 1) * 512), sgq[b2].reg()], [yT.reg(cb * NT + tb * 512, cb * NT + (tb + 1) * 512)])

    wts_cur = lru_loads(0)
    lru_front_pe(0, wts_cur)
    lru_front_conv(0)
    lru_casts(0)
    for cb in range(10):
        nxt = cb + 1 < 10
        lru_step(cb, 0, wts_cur, 'A')
        wts_nxt = None
        if nxt:
            wts_nxt = lru_loads(cb + 1)
            lru_front_pe(cb + 1, wts_nxt)
            lru_front_conv(cb + 1, (0,))
        lru_step(cb, 0, wts_cur, 'B')
        if nxt:
            lru_front_conv(cb + 1, (1,))
        lru_step(cb, 1, wts_cur, 'A')
        if nxt:
            lru_front_conv(cb + 1, (2,))
        lru_step(cb, 1, wts_cur, 'B')
        if nxt:
            lru_front_conv(cb + 1, (3,))
        lru_back(cb, wts_cur)
        if cb + 1 < 10:
            lru_casts(cb + 1)
        if cb == 0:
            dump('hf0', hh[0], NT, F32)
            dump('hb0', hh[1], NT, F32)
        wts_cur = wts_nxt
    S.dma("sp", slru_d, lruout.ap, [lruout.reg()], [])

    dump('yT', yT, 10 * NT, BF16)
    checkpoint(7)
    wld_v = wld_d.rearrange("(k p) c -> p k c", p=128)
    P2d = Alloc(P1b.p, NW)
    sml = [P2d.f32(512) for _ in range(2)]
    tml = [P2d.f32(512) for _ in range(2)]
    def ld_loads(ct):
        return (load_w(win_v[:, :, OML + ct * 128:OML + (ct + 1) * 128], "p (k c) -> p k c", k=8),
                load_w(wld_v[:, :, ct * 128:(ct + 1) * 128], "p (k c) -> p k c", k=10))
    ld_next = ld_loads(0)
    WOA = Alloc(hh[0].lo // 4, PERS_END + 16 * NT // 2)
    wo = WOA.bf(8 * 1024, ("p (k c) -> p k c", dict(k=8)))
    wout_v = wout_d.rearrange("(k p) c -> p k c", p=128)
    S.dma("pool", wo.ap, wout_v, [], [wo.reg()])
    for ct in range(8):
        (slot_m, wm), (slot_r, wr) = ld_next
        if ct + 1 < 8:
            ld_next = ld_loads(ct + 1)
        for tb in range(4):
            b2 = tb % 2
            pm = PS[b2]
            S.mm([(pm.ap, wm[:, k, :], hT.ap[:, k, tb * 512:(tb + 1) * 512]) for k in range(8)], [slot_m.reg(), hT.reg()], [pm.reg()])
            S.act(lambda e, b2=b2, pm=pm: e.activation(out=sml[b2].ap, in_=pm.ap, func=AF.Sigmoid), [pm.reg()], [sml[b2].reg()])
            pr = PS[2 + b2]
            S.mm([(pr.ap, wr[:, k, :], yT.ap[:, k, tb * 512:(tb + 1) * 512]) for k in range(10)], [slot_r.reg(), yT.reg()], [pr.reg()])
            S.dve(lambda e, b2=b2, pr=pr: e.tensor_tensor(out=tml[b2].ap, in0=pr.ap, in1=sml[b2].ap, op=ALU.mult), [pr.reg(), sml[b2].reg()], [tml[b2].reg()])
            S.pool(lambda e, b2=b2, ct=ct, tb=tb: e.tensor_tensor(out=PT.ap[:, ct, tb * 512:(tb + 1) * 512], in0=PT.ap[:, ct, tb * 512:(tb + 1) * 512], in1=tml[b2].ap, op=ALU.add),
                   [PT.reg(ct * NT + tb * 512, ct * NT + (tb + 1) * 512), tml[b2].reg()], [PT.reg(ct * NT + tb * 512, ct * NT + (tb + 1) * 512)])

    dump('PT', PT, 8 * NT, BF16)
    checkpoint(8)
    P3 = Alloc(PERS_END, PERS_END + 16 * NT // 2)
    gate_bc = P3.f32(1024)
    fnw_bc = P3.f32(1024)
    x3 = [P3.f32(1024) for _ in range(4)]
    y3 = [P3.f32(1024) for _ in range(4)]
    junk3 = P2d.bf(1024)
    ss3 = [P2d.f32(1) for _ in range(2)]
    rs3 = [P2d.f32(1) for _ in range(2)]
    assert P3.p <= PERS_END + 10 * NT // 2, "output-phase tiles must stay inside the dead yT region"
    wout_v = wout_d.rearrange("(k p) c -> p k c", p=128)
    S.dma("sp", gate_bc.ap, bass.AP(gate_d.tensor, 0, [[0, 128], [1, 1024]]), [("scr2", 0, 4096)], [gate_bc.reg()])
    S.dma("sp", fnw_bc.ap, bass.AP(rows_d.tensor, 1024, [[0, 128], [1, 1024]]), [], [fnw_bc.reg()])
    def p3_load(n):
        S.dma("sp", x3[n % 4].ap, x_d[n * 128:(n + 1) * 128, :], [], [x3[n % 4].reg()])

    def p3_front(n):
        b = n % 2
        yb = y3[n % 4]
        if n + 3 < NCH:
            p3_load(n + 3)
        for half in range(2):
            pb = PS[2 * b + half]
            S.mm([(pb.ap, PT.ap[:, k, n * 128:(n + 1) * 128], wo.ap[:, k, half * 512:(half + 1) * 512]) for k in range(8)],
                 [PT.reg(), wo.reg()], [pb.reg()])
            S.dve(lambda e, half=half, pb=pb: e.tensor_tensor(out=yb.ap[:, half * 512:(half + 1) * 512], in0=pb.ap, in1=gate_bc.ap[:, half * 512:(half + 1) * 512], op=ALU.mult),
                  [pb.reg(), gate_bc.reg()], [yb.reg(half * 512, (half + 1) * 512)])
        S.pool(lambda e: e.tensor_tensor(out=yb.ap, in0=yb.ap, in1=x3[n % 4].ap, op=ALU.add), [yb.reg(), x3[n % 4].reg()], [yb.reg()])

    def p3_back(n):
        b = n % 2
        yb = y3[n % 4]
        S.act(lambda e: e.activation(out=junk3.ap, in_=yb.ap, func=AF.Square, accum_out=ss3[b].ap), [yb.reg()], [junk3.reg(), ss3[b].reg()])
        S.act(lambda e: e.activation(out=rs3[b].ap, in_=ss3[b].ap, func=AF.Ln, scale=1.0 / D, bias=eps_c.ap), [ss3[b].reg(), eps_c.reg()], [rs3[b].reg()])
        S.act(lambda e: e.activation(out=rs3[b].ap, in_=rs3[b].ap, func=AF.Exp, scale=-0.5), [rs3[b].reg()], [rs3[b].reg()])
        S.dve(lambda e: e.scalar_tensor_tensor(out=yb.ap, in0=yb.ap, scalar=rs3[b].ap, in1=fnw_bc.ap, op0=ALU.mult, op1=ALU.mult),
              [yb.reg(), rs3[b].reg(), fnw_bc.reg()], [yb.reg()])
        S.dma("sp", y_d[n * 128:(n + 1) * 128, :], yb.ap, [yb.reg()], [])

    for n_ in range(3):
        p3_load(n_)
    p3_front(0)
    for n in range(NCH):
        if n + 1 < NCH:
            p3_front(n + 1)
        p3_back(n)

    S.emit_all(sems)
    es.close()
    S.dumps = dumps
    return nc, S


_CACHE = {}


def _consts():
    c = np.zeros((128, 257), np.float32)
    c[:, 0:128] = np.eye(128, dtype=np.float32)
    c[:, 128:256] = np.arange(128, dtype=np.float32)[None, :]
    c[:, 256] = np.arange(128, dtype=np.float32)
    return c


def _colv(v, nt):
    return np.ascontiguousarray(np.asarray(v, np.float32).reshape(nt, 128).T)


def make_in_maps(x_prompt, x_sample, state_ret, state_lru, c, c_ctx, norm_w, w_ada, b_ada, w_in,
                 ret_decay_logit, ret_gn_w, w_ret_down, conv_w, conv_b, lru_wa, lru_ba, lru_wx, lru_bx,
                 lru_a_param, w_lru_down, w_out, final_norm_w):
    f = lambda a: np.ascontiguousarray(np.asarray(a, np.float32))
    consts = _consts()
    rows = np.stack([f(b_ada)[0, 2048:3072], f(final_norm_w)], 0)
    shared = dict(consts=consts, rows=rows, w_ada=f(w_ada)[0], w_in=f(w_in)[0], w_ret_down=f(w_ret_down)[0],
                  w_lru_down=f(w_lru_down)[0], w_out=f(w_out)[0], lru_wa=f(lru_wa)[0], lru_wx=f(lru_wx)[0])
    in_maps = []
    for core in range(8):
        cols = np.zeros((128, NCOLS), np.float32)
        cols[:, C_NW:C_NW + 8] = _colv(norm_w[0], 8)
        cols[:, C_BSH:C_BSH + 8] = _colv(b_ada[0, 0:1024], 8)
        cols[:, C_BSC:C_BSC + 8] = _colv(b_ada[0, 1024:2048], 8)
        cols[:, C_GN:C_GN + 16] = _colv(ret_gn_w[0], 16)
        cw = np.asarray(conv_w, np.float32)[0]
        for cb in range(10):
            for j in range(4):
                cols[:, C_CW + cb * 4 + j] = cw[j, cb * 128:(cb + 1) * 128]
        cols[:, C_CB:C_CB + 10] = _colv(conv_b[0], 10)
        for d in range(2):
            cols[:, C_BA + d * 10:C_BA + d * 10 + 10] = _colv(lru_ba[0, d], 10)
            cols[:, C_BX + d * 10:C_BX + d * 10 + 10] = _colv(lru_bx[0, d], 10)
            cols[:, C_AP + d * 10:C_AP + d * 10 + 10] = _colv(lru_a_param[0, d], 10)
        smalls = np.zeros((1, NSM), np.float32)
        smalls[0, 0:8] = np.asarray(ret_decay_logit, np.float32)[0].reshape(8)
        if core < 4:
            x = f(x_sample[core])
            cols[:, C_COND:C_COND + 8] = _colv(c[core], 8)
            init_ret = f(state_ret[core, 0])
            for d in range(2):
                cols[:, C_H0 + d * 10:C_H0 + d * 10 + 10] = _colv(state_lru[core, 0, d], 10)
            smalls[0, 8:24] = 1.0
            smalls[0, 24:40] = 1.0
            smalls[0, 40] = 0.0
            smalls[0, 41] = 1.0
        else:
            p0 = (core - 4) * 4
            x = np.zeros((NT, D), np.float32)
            x[:1024] = np.asarray(x_prompt[p0:p0 + 4], np.float32).reshape(1024, D)
            cols[:, C_COND:C_COND + 8] = _colv(c_ctx, 8)
            init_ret = np.zeros((2, 4, 256, 512), np.float32)
            kf = np.array([1.0 if (n % 2 == 1) else 0.0 for n in range(16)], np.float32)
            kf[0] = 1.0
            kb = np.array([1.0 if (n % 2 == 0) else 0.0 for n in range(16)], np.float32)
            kb[15] = 1.0
            smalls[0, 8:24] = kf
            smalls[0, 24:40] = kb
            smalls[0, 40] = 1.0
            smalls[0, 41] = 0.0
        m = dict(shared)
        m.update(x=x, cols=cols, smalls=smalls, init_ret=init_ret)
        in_maps.append(m)
    return in_maps


def kernel(**inputs):
    if "nc" not in _CACHE:
        _CACHE["nc"] = build_program(DEBUG)[0]
    nc = _CACHE["nc"]
    in_maps = make_in_maps(**inputs)
    res = run_bass_kernel_spmd(nc, in_maps, core_ids=list(range(8)))
    r = res.results
    y_sample = np.stack([r[i]["y"] for i in range(4)], 0).astype(np.float32)
    y_prompt = np.concatenate([r[i]["y"][:1024].reshape(4, 256, D) for i in range(4, 8)], 0).astype(np.float32)
    st_ret = np.concatenate([r[i]["st_ret"] for i in range(4, 8)], 0)[:, None].astype(np.float32)
    lr = []
    for i in range(4, 8):
        a = r[i]["st_lru"].reshape(128, 10, 2, 4)
        lr.append(a.transpose(3, 2, 1, 0).reshape(4, 2, 1280))
    st_lru = np.concatenate(lr, 0)[:, None].astype(np.float32)
    if DEBUG:
        _CACHE["dbg"] = r
    return (y_prompt, y_sample, st_ret, st_lru)
```
